# Optimizing a Trainium2 kernel written in Bass

```python
import jax, jax.numpy as jnp
from jax import lax
import numpy as np

D_MODEL = 1024
BATCH = 16
SEQ = 2048
DEPTH = 2
DEC_BATCH = 2
DEC_SEQ = 16384
PAST_LEN = 128

GRID_W = 64
HEAD_DIM = 64
EPS = 1e-6
A_Q_HEADS = 8
A_KV_HEADS = 2
ROPE_THETA = 10000.0
Q_BLOCK = 128
B_PATTERNS = ((128, 1), (512, 4), (2048, 16))
B_BRANCHES = len(B_PATTERNS)
B_HEADS = 4
C_HEADS = 16
C_WIN_ROWS = 8
C_WIN_COLS = 16
C_COL_BLOCK = 16
C_SLAB = 32
X_HEADS = 4
X_HEAD_DIM = D_MODEL // X_HEADS
MEM_TOKENS = 256
D_FF = ((8 * D_MODEL + 3 * 256 - 1) // (3 * 256)) * 256
A_Q_W = A_Q_HEADS * HEAD_DIM
A_KV_W = A_KV_HEADS * HEAD_DIM
B_QKV_W = 3 * B_BRANCHES * B_HEADS * HEAD_DIM
IN_AB = A_Q_W + 2 * A_KV_W + B_QKV_W
OUT_AB = A_Q_W + B_HEADS * HEAD_DIM
IN_C = 3 * C_HEADS * HEAD_DIM
OUT_C = C_HEADS * HEAD_DIM

kernel_name = "hybrid_bidir_encoder_gqa_dilated_natten"


def _rms_norm(x, g):
    xf = x.astype(jnp.float32)
    y = xf * lax.rsqrt(jnp.mean(xf * xf, axis=-1, keepdims=True) + EPS)
    return (y * g.astype(jnp.float32)).astype(x.dtype)


def _alibi_slopes(n):
    return jnp.exp2(-8.0 * jnp.arange(1, n + 1, dtype=jnp.float32) / n)


def _axial_rope_tables(n_tok):
    t = jnp.arange(n_tok, dtype=jnp.int32)
    row = (t // GRID_W).astype(jnp.float32)
    col = (t % GRID_W).astype(jnp.float32)
    axis_dim = HEAD_DIM // 2
    inv_freq = ROPE_THETA ** (-jnp.arange(0, axis_dim, 2, dtype=jnp.float32) / axis_dim)
    ang = jnp.concatenate([row[:, None] * inv_freq, col[:, None] * inv_freq], axis=-1)
    return jnp.cos(ang), jnp.sin(ang)


def _apply_rope(x, cos, sin):
    x2 = x.reshape(*x.shape[:-1], HEAD_DIM // 2, 2)
    xr, xi = x2[..., 0], x2[..., 1]
    c = cos[None, :, None, :].astype(x.dtype)
    s = sin[None, :, None, :].astype(x.dtype)
    return jnp.stack([xr * c - xi * s, xr * s + xi * c], axis=-1).reshape(x.shape)


def _gqa_block_attention(q, k, v):
    bn, s_len = q.shape[:2]
    grp = A_Q_HEADS // A_KV_HEADS
    nb = s_len // Q_BLOCK
    qb = q.reshape(bn, nb, Q_BLOCK, A_KV_HEADS, grp, HEAD_DIM).transpose(1, 0, 2, 3, 4, 5)
    scale = HEAD_DIM ** -0.5

    def one_block(qi):
        s = jnp.einsum('bqkgd,bskd->bkgqs', qi, k, preferred_element_type=jnp.float32) * scale
        p = jax.nn.softmax(s, axis=-1).astype(v.dtype)
        return jnp.einsum('bkgqs,bskd->bqkgd', p, v)

    o = lax.map(one_block, qb)
    return o.transpose(1, 0, 2, 3, 4, 5).reshape(bn, s_len, A_Q_HEADS * HEAD_DIM)


def _dilated_branch(q, k, v, dil, half, slopes):
    bn, s_len, nh, dh = q.shape
    L = s_len // dil
    nb = -(-L // half)
    Lp = nb * half

    def to_strided(x):
        return x.reshape(bn, L, dil, nh, dh).transpose(0, 2, 1, 3, 4).reshape(bn * dil, L, nh, dh)

    qs, ks, vs = to_strided(q), to_strided(k), to_strided(v)
    qb = jnp.pad(qs, ((0, 0), (0, Lp - L), (0, 0), (0, 0))).reshape(bn * dil, nb, half, nh, dh)

    def bands(x):
        xp = jnp.pad(x, ((0, 0), (half, Lp - L + half), (0, 0), (0, 0)))
        return jnp.concatenate(
            [xp[:, o:o + Lp].reshape(bn * dil, nb, half, nh, dh) for o in (0, half, 2 * half)], axis=2)

    kb, vb = bands(ks), bands(vs)
    qi = np.arange(nb)[:, None, None] * half + np.arange(half)[None, :, None]
    kj = np.arange(nb)[:, None, None] * half + np.arange(3 * half)[None, None, :] - half
    rel = kj - qi
    valid = (np.abs(rel) <= half) & (kj >= 0) & (kj < L)
    dist = (np.abs(rel) * dil).astype(np.float32)
    s = jnp.einsum('znqhd,znkhd->znhqk', qb, kb, preferred_element_type=jnp.float32) * (dh ** -0.5)
    s = s - slopes[None, :, None, None] * dist[:, None]
    s = jnp.where(valid[:, None], s, -jnp.inf)
    lse = jax.nn.logsumexp(s, axis=-1, keepdims=True)
    p = jnp.exp(s - lse).astype(vb.dtype)
    o = jnp.einsum('znhqk,znkhd->znqhd', p, vb)
    lse = jnp.moveaxis(lse[..., 0], 2, 3)

    def from_strided(y):
        rest = y.shape[3:]
        y = y.reshape(bn, dil, Lp, *rest)[:, :, :L]
        return jnp.moveaxis(y, 1, 2).reshape(bn, s_len, *rest)

    return from_strided(o), from_strided(lse)


def _dilated_mixture(zb):
    bn, s_len = zb.shape[:2]
    slopes = _alibi_slopes(B_BRANCHES * B_HEADS).reshape(B_BRANCHES, B_HEADS)
    outs, lses = [], []
    for g, (window, dil) in enumerate(B_PATTERNS):
        o, lse = _dilated_branch(zb[:, :, 0, g], zb[:, :, 1, g], zb[:, :, 2, g],
                                 dil, window // (2 * dil), slopes[g])
        outs.append(o)
        lses.append(lse)
    wts = jax.nn.softmax(jnp.stack(lses), axis=0)
    out = jnp.sum(wts[..., None].astype(outs[0].dtype) * jnp.stack(outs), axis=0)
    return out.reshape(bn, s_len, B_HEADS * HEAD_DIM)


def _neighbourhood_attention(q, k, v, rpb):
    bn, s_len, nh, dh = q.shape
    rows = s_len // GRID_W
    kh = min(C_WIN_ROWS, rows)
    qg = q.reshape(bn, rows, GRID_W, nh, dh)
    kg = k.reshape(bn, rows, GRID_W, nh, dh)
    vg = v.reshape(bn, rows, GRID_W, nh, dh)
    n_cb = GRID_W // C_COL_BLOCK
    slab_start = np.clip(np.arange(n_cb) * C_COL_BLOCK - C_WIN_COLS // 2, 0, GRID_W - C_SLAB)
    qcol = np.arange(n_cb)[:, None] * C_COL_BLOCK + np.arange(C_COL_BLOCK)[None, :]
    cstart = np.clip(qcol - C_WIN_COLS // 2, 0, GRID_W - C_WIN_COLS)
    kcol = slab_start[:, None] + np.arange(C_SLAB)[None, :]
    col_valid = (kcol[:, None, :] >= cstart[:, :, None]) & (kcol[:, None, :] < cstart[:, :, None] + C_WIN_COLS)
    col_idx = np.clip(kcol[:, None, :] - qcol[:, :, None] + C_WIN_COLS - 1, 0, 2 * C_WIN_COLS - 2)
    col_bias = rpb[:, :, col_idx]
    scale = dh ** -0.5

    def row_fn(r):
        rs = jnp.clip(r - kh // 2, 0, rows - kh)
        kr = lax.dynamic_slice_in_dim(kg, rs, kh, axis=1)
        vr = lax.dynamic_slice_in_dim(vg, rs, kh, axis=1)
        ks = jnp.stack([kr[:, :, s0:s0 + C_SLAB] for s0 in slab_start], axis=2)
        vs = jnp.stack([vr[:, :, s0:s0 + C_SLAB] for s0 in slab_start], axis=2)
        qr = lax.dynamic_index_in_dim(qg, r, axis=1, keepdims=False).reshape(bn, n_cb, C_COL_BLOCK, nh, dh)
        s = jnp.einsum('bnqhd,binchd->bhnqic', qr, ks, preferred_element_type=jnp.float32) * scale
        ridx = rs + jnp.arange(kh) - r + (C_WIN_ROWS - 1)
        bias = jnp.take(col_bias, ridx, axis=1).transpose(0, 2, 3, 1, 4)
        s = s + bias[None].astype(jnp.float32)
        s = jnp.where(col_valid[:, :, None, :], s, -jnp.inf)
        shp = s.shape
        p = jax.nn.softmax(s.reshape(*shp[:-2], kh * C_SLAB), axis=-1).reshape(shp).astype(vs.dtype)
        o = jnp.einsum('bhnqic,binchd->bnqhd', p, vs)
        return o.reshape(bn, GRID_W, nh, dh)

    out = lax.map(row_fn, jnp.arange(rows, dtype=jnp.int32))
    return out.transpose(1, 0, 2, 3, 4).reshape(bn, s_len, nh * dh)


def _mixer_ab(h, w_in, g_qn, g_kn, w_out):
    bn, s_len, _ = h.shape
    z = h @ w_in
    qa, ka, va, zb = jnp.split(z, [A_Q_W, A_Q_W + A_KV_W, A_Q_W + 2 * A_KV_W], axis=-1)
    qa = _rms_norm(qa.reshape(bn, s_len, A_Q_HEADS, HEAD_DIM), g_qn)
    ka = _rms_norm(ka.reshape(bn, s_len, A_KV_HEADS, HEAD_DIM), g_kn)
    cos, sin = _axial_rope_tables(s_len)
    qa = _apply_rope(qa, cos, sin)
    ka = _apply_rope(ka, cos, sin)
    va = va.reshape(bn, s_len, A_KV_HEADS, HEAD_DIM)
    oa = _gqa_block_attention(qa, ka, va)
    ob = _dilated_mixture(zb.reshape(bn, s_len, 3, B_BRANCHES, B_HEADS, HEAD_DIM))
    return jnp.concatenate([oa, ob], axis=-1) @ w_out


def _mixer_c(h, w_in, rpb, w_out):
    bn, s_len, _ = h.shape
    z = (h @ w_in).reshape(bn, s_len, 3, C_HEADS, HEAD_DIM)
    return _neighbourhood_attention(z[:, :, 0], z[:, :, 1], z[:, :, 2], rpb) @ w_out


def _memory_cross_attention(h, mem, g_mem, wq, wkv, wo):
    bn, s_len, _ = h.shape
    n_mem = mem.shape[1]
    m = _rms_norm(mem, g_mem)
    q = (h @ wq).reshape(bn, s_len, X_HEADS, X_HEAD_DIM)
    kv = (m @ wkv).reshape(bn, n_mem, 2, X_HEADS, X_HEAD_DIM)
    s = jnp.einsum('bshd,bmhd->bhsm', q, kv[:, :, 0], preferred_element_type=jnp.float32) * (X_HEAD_DIM ** -0.5)
    p = jax.nn.softmax(s, axis=-1).astype(kv.dtype)
    o = jnp.einsum('bhsm,bmhd->bshd', p, kv[:, :, 1]).reshape(bn, s_len, D_MODEL)
    return o @ wo


def _swiglu(h, w_gu, w_down):
    g, u = jnp.split(h @ w_gu, 2, axis=-1)
    return (jax.nn.silu(g) * u) @ w_down


def _encoder_trunk(x, mem, g_mix, w_in_ab, g_qn, g_kn, w_out_ab, w_in_c, rpb_c, w_out_c,
                   g_xattn, g_mem, wq_x, wkv_x, wo_x, g_ffn, w_gu, w_down, g_final):
    for l in range(DEPTH):
        h = _rms_norm(x, g_mix[l])
        if l % 2 == 0:
            e = l // 2
            x = x + _mixer_ab(h, w_in_ab[e], g_qn[e], g_kn[e], w_out_ab[e])
        else:
            o = l // 2
            x = x + _mixer_c(h, w_in_c[o], rpb_c[o], w_out_c[o])
        x = x + _memory_cross_attention(_rms_norm(x, g_xattn[l]), mem, g_mem[l], wq_x[l], wkv_x[l], wo_x[l])
        x = x + _swiglu(_rms_norm(x, g_ffn[l]), w_gu[l], w_down[l])
    return _rms_norm(x, g_final)


def setup_inputs(seed: int = 0) -> dict:
    key = jax.random.key(seed)
    ks = jax.random.split(key, 24)
    f32 = jnp.float32
    n_even = (DEPTH + 1) // 2
    n_odd = DEPTH // 2

    def nrm(k, shape, scale):
        return jax.random.normal(k, shape, f32) * scale

    def gain(k, shape):
        return 1.0 + 0.05 * jax.random.normal(k, shape, f32)

    return {
        "x_prompt": nrm(ks[0], (BATCH, SEQ, D_MODEL), 1.0),
        "x_sample": nrm(ks[1], (DEC_BATCH, DEC_SEQ, D_MODEL), 1.0),
        "mem_prompt": nrm(ks[2], (BATCH, MEM_TOKENS, D_MODEL), 1.0),
        "mem_sample": nrm(ks[3], (DEC_BATCH, MEM_TOKENS, D_MODEL), 1.0),
        "g_mix": gain(ks[4], (DEPTH, D_MODEL)),
        "w_in_ab": nrm(ks[5], (n_even, D_MODEL, IN_AB), D_MODEL ** -0.5),
        "g_qn": gain(ks[6], (n_even, HEAD_DIM)),
        "g_kn": gain(ks[7], (n_even, HEAD_DIM)),
        "w_out_ab": nrm(ks[8], (n_even, OUT_AB, D_MODEL), OUT_AB ** -0.5),
        "w_in_c": nrm(ks[9], (n_odd, D_MODEL, IN_C), D_MODEL ** -0.5),
        "rpb_c": nrm(ks[10], (n_odd, C_HEADS, 2 * C_WIN_ROWS - 1, 2 * C_WIN_COLS - 1), 0.1),
        "w_out_c": nrm(ks[11], (n_odd, OUT_C, D_MODEL), OUT_C ** -0.5),
        "g_xattn": gain(ks[12], (DEPTH, D_MODEL)),
        "g_mem": gain(ks[13], (DEPTH, D_MODEL)),
        "wq_x": nrm(ks[14], (DEPTH, D_MODEL, D_MODEL), D_MODEL ** -0.5),
        "wkv_x": nrm(ks[15], (DEPTH, D_MODEL, 2 * D_MODEL), D_MODEL ** -0.5),
        "wo_x": nrm(ks[16], (DEPTH, D_MODEL, D_MODEL), D_MODEL ** -0.5),
        "g_ffn": gain(ks[17], (DEPTH, D_MODEL)),
        "w_gu": nrm(ks[18], (DEPTH, D_MODEL, 2 * D_FF), D_MODEL ** -0.5),
        "w_down": nrm(ks[19], (DEPTH, D_FF, D_MODEL), D_FF ** -0.5),
        "g_final": gain(ks[20], (D_MODEL,)),
    }


def reference(x_prompt, x_sample, mem_prompt, mem_sample, g_mix, w_in_ab, g_qn, g_kn, w_out_ab,
              w_in_c, rpb_c, w_out_c, g_xattn, g_mem, wq_x, wkv_x, wo_x, g_ffn, w_gu, w_down, g_final):
    y_prompt = _encoder_trunk(x_prompt, mem_prompt, g_mix, w_in_ab, g_qn, g_kn, w_out_ab, w_in_c, rpb_c,
                              w_out_c, g_xattn, g_mem, wq_x, wkv_x, wo_x, g_ffn, w_gu, w_down, g_final)
    y_sample = _encoder_trunk(x_sample, mem_sample, g_mix, w_in_ab, g_qn, g_kn, w_out_ab, w_in_c, rpb_c,
                              w_out_c, g_xattn, g_mem, wq_x, wkv_x, wo_x, g_ffn, w_gu, w_down, g_final)
    return (y_prompt, y_sample)
```

```python
import numpy as np
from contextlib import ExitStack
import concourse.bass as bass
import concourse.mybir as mybir
from concourse.bass_utils import run_bass_kernel_spmd

F32 = mybir.dt.float32
BF16 = mybir.dt.bfloat16
AF = mybir.ActivationFunctionType
ALU = mybir.AluOpType

D = 1024
EPS = 1e-6
NEG = -30000.0
DFF = 2816
SEGS = [
    dict(X=2048, E0=0, E=2048, O0=0, OWN=2048, NA=2048, prompt=True),
    dict(X=2048, E0=0, E=2048, O0=0, OWN=2048, NA=2048, prompt=True),
    dict(X=6656, E0=1024, E=4608, O0=256, OWN=4096, NA=16384, prompt=False),
]
B_DIL = (1, 4, 16)


def alibi_c(g, h):
    return float(2.0 ** (-8.0 * (4 * g + h + 1) / 12.0)) * B_DIL[g]


def b_jobs(seg):
    jobs = []
    for g, d in enumerate(B_DIL):
        L = seg["E"] // d
        H = seg["E0"] // d
        H2 = (seg["X"] - seg["E0"] - seg["E"]) // d
        for r in range(d):
            i0 = 0
            while i0 < L:
                nq = min(128, L - i0)
                kts = []
                for a in range(2):
                    jn = i0 - 64 + 128 * a
                    ja = max(jn, -H)
                    je = min(jn + 128, i0 + nq + 64, L + H2)
                    if je > ja:
                        kts.append((ja, je - ja, ja - jn))
                jobs.append((g, r, i0, nq, kts))
                i0 += nq
    return jobs


def c_jobs(seg):
    rows = seg["OWN"] // 64
    off = seg["O0"] // 64
    jobs = []
    if seg["prompt"]:
        for r in range(rows):
            rs = min(max(r - 4, 0), rows - 8)
            jobs.append((r, rs, rs - r + 7, 0))
    else:
        for r in range(rows):
            jobs.append((r, r + off - 4, 3, 0))
            if r < 4:
                jobs.append((r, off, 0 - r + 7, 1))
            if r >= rows - 3:
                jobs.append((r, off + rows - 8, rows - 8 - r + 7, 2))
    return jobs


class Buf:
    __slots__ = ("name", "last_w", "readers")

    def __init__(self, name):
        self.name = name
        self.last_w = None
        self.readers = {}


class TB:
    def __init__(self, t, name):
        self.t = t
        self.buf = Buf(name)

    def __getitem__(self, k):
        return self.t[k]


class Sched:
    def __init__(self, nc, es):
        self.nc = nc
        self.eng = {"pe": nc.tensor, "act": nc.scalar, "dve": nc.vector, "pool": nc.gpsimd, "sp": nc.sync}
        self.sem = {e: es.enter_context(nc.semaphore("s_" + e)) for e in ("pe", "act", "dve", "pool")}
        self.cnt = {e: 0 for e in self.sem}
        self.dq = {"sp": list(range(0, 14)), "pool": list(range(14, 20)), "act": []}
        self.dsem = [es.enter_context(nc.semaphore("d%d" % i)) for i in range(20)]
        self.dcnt = [0] * 20
        self.dnext = {"sp": 0, "pool": 0, "act": 0}
        self.waited = {}
        self.nins = 0

    def _s(self, k):
        return self.sem[k[1]] if k[0] == "e" else self.dsem[k[1]]

    def _deps(self, e, reads, writes):
        deps = {}

        def add(tok, war=False):
            if tok is None:
                return
            k, v, src = tok
            if src == e and (e == "pe" or war):
                return
            if deps.get(k, 0) < v:
                deps[k] = v
        for b in reads:
            add(b.last_w)
        for b in writes:
            add(b.last_w)
            for k, (v, src) in b.readers.items():
                add((k, v, src), True)
        eng = self.eng[e]
        for k, v in deps.items():
            if self.waited.get((e, k), 0) >= v:
                continue
            self.waited[(e, k)] = v
            eng.wait_ge(self._s(k), v)

    def _fin(self, tok, reads, writes):
        k, v, src = tok
        for b in reads:
            b.readers[k] = (v, src)
        for b in writes:
            b.last_w = tok
            b.readers = {}

    def op(self, e, fn, reads=(), writes=()):
        self._deps(e, reads, writes)
        ins = fn(self.eng[e])
        self.cnt[e] += 1
        ins.then_inc(self.sem[e], 1)
        self._fin((("e", e), self.cnt[e], e), reads, writes)
        self.nins += 1

    def dma(self, q, out, in_, reads=(), writes=()):
        self._deps(q, reads, writes)
        lst = self.dq[q]
        i = lst[self.dnext[q] % len(lst)]
        self.dnext[q] += 1
        if self.dcnt[i] > self.waited.get((q, ("d", i)), 0):
            self.waited[(q, ("d", i))] = self.dcnt[i]
            self.eng[q].wait_ge(self.dsem[i], self.dcnt[i])
        ins = self.eng[q].dma_start(out=out, in_=in_)
        self.dcnt[i] += 16
        ins.then_inc(self.dsem[i], 16)
        self._fin((("d", i), self.dcnt[i], "dma"), reads, writes)
        self.nins += 1

    def barrier(self):
        for e in self.eng:
            eng = self.eng[e]
            for o in self.sem:
                if o != e and self.cnt[o] > self.waited.get((e, ("e", o)), 0):
                    self.waited[(e, ("e", o))] = self.cnt[o]
                    eng.wait_ge(self.sem[o], self.cnt[o])
            for i in range(len(self.dsem)):
                if self.dcnt[i] > self.waited.get((e, ("d", i)), 0):
                    self.waited[(e, ("d", i))] = self.dcnt[i]
                    eng.wait_ge(self.dsem[i], self.dcnt[i])


def build(debug=None):
    nc = bass.Bass("TRN2", target_bir_lowering=False)

    def din(name, shape, dt=F32):
        return nc.dram_tensor(name, list(shape), dt, kind="ExternalInput").ap()

    def dscr(name, shape, dt=BF16):
        kind = "ExternalOutput" if (debug and name in debug) else "Internal"
        return nc.dram_tensor(name, list(shape), dt, kind=kind).ap()

    xT = [din("xT0", [D, 2048]), din("xT1", [D, 2048]), din("xT2", [D, 6656])]
    xTa = [xT[0], xT[1], din("xTa", [D, 16384])]
    memT = din("memT", [3, D, 256])
    ropeq = [din("rope_p", [2, 128, 2048]), None, din("ropeq_s", [2, 128, 4608])]
    ropeq[1] = ropeq[0]
    ropek = [ropeq[0], ropeq[0], din("ropek_s", [2, 128, 16384])]
    gtab = din("gtab", [128, 74])
    consts = din("consts", [128, 897])
    vbt_in = din("vbt", [128, 2048])
    cmask_in = din("cmask", [128, 128], mybir.dt.uint32)
    ctab_in = din("ctab", [16, 128, 15 * 64])
    w_in_ab = din("w_in_ab", [D, 3072])
    w_out_ab = din("w_out_ab", [768, D])
    w_in_c = din("w_in_c", [D, 3072])
    w_out_c = din("w_out_c", [D, D])
    wq_x = din("wq_x", [2, D, D])
    wkv_x = din("wkv_x", [2, D, 2 * D])
    wo_x = din("wo_x", [2, D, D])
    w_gu = din("w_gu", [2, D, 2 * DFF])
    w_down = din("w_down", [2, DFF, D])
    yT = [nc.dram_tensor("yT%d" % s, [D, SEGS[s]["OWN"]], F32, kind="ExternalOutput").ap() for s in range(3)]
    QA = [dscr("QA%d" % s, [4, 128, SEGS[s]["E"]]) for s in range(3)]
    KAd = [dscr("KAd%d" % s, [2, 128, SEGS[s]["NA"]]) for s in range(3)]
    VAa = [dscr("VAa%d" % s, [SEGS[s]["NA"], 512]) for s in range(3)]
    QB = [dscr("QB%d" % s, [6, 128, SEGS[s]["E"]]) for s in range(3)]
    KB = [dscr("KB%d" % s, [6, 128, SEGS[s]["X"]]) for s in range(3)]
    VB = [dscr("VB%d" % s, [SEGS[s]["X"], 768]) for s in range(3)]
    OAB = [dscr("OAB%d" % s, [6, 128, SEGS[s]["E"]]) for s in range(3)]
    X2 = [dscr("X2_%d" % s, [8, 128, SEGS[s]["E"]], F32) for s in range(3)]
    X3 = [dscr("X3_%d" % s, [8, 128, SEGS[s]["E"]], F32) for s in range(3)]
    QC = [dscr("QC%d" % s, [8, 128, SEGS[s]["E"]]) for s in range(3)]
    KC = [dscr("KC%d" % s, [8, 128, SEGS[s]["E"]]) for s in range(3)]
    VCa = [dscr("VCa%d" % s, [SEGS[s]["E"], 16 * 128]) for s in range(3)]
    OC = [dscr("OC%d" % s, [8, 128, SEGS[s]["OWN"]]) for s in range(3)]
    KM = dscr("KM", [3, 2, 8, 128, 256])
    VM = dscr("VM", [3, 2, 2, 128, 1024])

    with ExitStack() as es:
        S = Sched(nc, es)
        op, dma = S.op, S.dma

        uniq = [0]

        def sb(stack, name, shape, dt):
            uniq[0] += 1
            name = "%s_%d" % (name, uniq[0])
            return TB(stack.enter_context(nc.sbuf_tensor(name, list(shape), dt)), name)

        psall = es.enter_context(nc.psum_tensor("psall", [128, 4096], F32))

        class PSV:
            def __init__(self, i):
                self.off = 512 * i
                self.buf = Buf("ps%d" % i)

            def __getitem__(self, k):
                if isinstance(k, tuple):
                    p, c = k
                else:
                    p, c = k, slice(None)
                a = 0 if c.start is None else c.start
                b = 512 if c.stop is None else c.stop
                assert c.step is None
                return psall[p, self.off + a:self.off + b]
        ps = [PSV(i) for i in range(8)]
        cst_f = sb(es, "cst_f", [128, 897], F32)
        cst_b = sb(es, "cst_b", [128, 384], BF16)
        g_sb = sb(es, "g_sb", [128, 74], F32)
        dma("sp", cst_f[:], consts, writes=[cst_f.buf])
        dma("sp", g_sb[:], gtab, writes=[g_sb.buf])
        op("dve", lambda e: e.tensor_copy(out=cst_b[:], in_=cst_f[:, 0:384]), [cst_f.buf], [cst_b.buf])
        ones_b = cst_b[:, 0:128]
        bd_b = cst_b[:, 128:256]
        pswap_f = cst_f[:, 256:384]
        btabA = cst_f[:, 384:640]
        btabM = cst_f[:, 640:896]
        eps_col = cst_f[:, 896:897]
        CB = [cst_b.buf]

        def gcol(i):
            return g_sb[:, 8 * i:8 * i + 8]

        def mm(pt, lhsT, rhs, start, stop, reads, out=None):
            o = pt[:] if out is None else out
            op("pe", lambda e: e.matmul(o, lhsT=lhsT, rhs=rhs, start=start, stop=stop), reads, [pt.buf])

        def rmsnorm(x_t, N, gi, h_t, sq_t, rs_t, pss):
            op("act", lambda e: e.activation(out=sq_t[:, :, 0:N], in_=x_t[:, :, 0:N], func=AF.Square),
               [x_t.buf], [sq_t.buf])
            for c in range(8):
                mm(pss, ones_b, sq_t[:, c, 0:N], c == 0, c == 7, [sq_t.buf] + CB, out=pss[:, 0:N])
            op("act", lambda e: e.activation(out=rs_t[:, 0:N], in_=pss[:, 0:N], func=AF.Ln, scale=1.0 / D,
                                             bias=eps_col), [pss.buf, cst_f.buf], [rs_t.buf])
            op("act", lambda e: e.activation(out=rs_t[:, 0:N], in_=rs_t[:, 0:N], func=AF.Exp, scale=-0.5),
               [rs_t.buf], [rs_t.buf])
            g = gcol(gi)
            for c in range(8):
                eng = "dve"
                op(eng, lambda e, c=c: e.scalar_tensor_tensor(out=h_t[:, c, 0:N], in0=x_t[:, c, 0:N],
                                                              scalar=g[:, c:c + 1], in1=rs_t[:, 0:N],
                                                              op0=ALU.mult, op1=ALU.mult),
                   [x_t.buf, rs_t.buf, g_sb.buf], [h_t.buf])

        def load_w(stack, name, src, kch, ncol, c0=0, c1=None, bw=None, order=None):
            c1 = ncol if c1 is None else c1
            w = sb(stack, name, [128, kch, c1 - c0], BF16)
            v = src.rearrange("(k p) n -> p k n", p=128)
            if bw is None:
                for k in range(kch):
                    dma("pool", w[:, k:k + 1, :], v[:, k:k + 1, c0:c1], writes=[w.buf])
            else:
                nb = (c1 - c0) // bw
                w.bw = bw
                w.bufs = [Buf("%s_b%d" % (name, i)) for i in range(nb)]
                for b_ in (order if order is not None else range(nb)):
                    dma("pool", w[:, :, b_ * bw:(b_ + 1) * bw], v[:, :, c0 + b_ * bw:c0 + (b_ + 1) * bw],
                        writes=[w.bufs[b_]])
            return w

        def wb(w, a, b_):
            return w.bufs[a // w.bw:(b_ - 1) // w.bw + 1]

        with ExitStack() as ph:
            xm = sb(ph, "xm", [128, 8, 256], F32)
            sq = sb(ph, "sq", [128, 8, 256], BF16)
            rs = sb(ph, "rs", [128, 256], F32)
            hm = sb(ph, "hm", [128, 8, 256], BF16)
            ko = [sb(ph, "ko%d" % i, [128, 256], BF16) for i in range(2)]
            vo = [sb(ph, "vo%d" % i, [128, 512], BF16) for i in range(2)]
            for l in range(2):
                with ExitStack() as ph2:
                    wkv = load_w(ph2, "wkv", wkv_x[l], 8, 2048)
                    for s in range(3):
                        dma("sp", xm[:], memT[s].rearrange("(c p) t -> p c t", p=128), writes=[xm.buf])
                        rmsnorm(xm, 256, 4 + l, hm, sq, rs, ps[0])
                        for oc in range(8):
                            pt = ps[1 + oc % 2]
                            for kc in range(8):
                                mm(pt, wkv[:, kc, oc * 128:(oc + 1) * 128], hm[:, kc, :], kc == 0, kc == 7,
                                   [wkv.buf, hm.buf], out=pt[:, 0:256])
                            o = ko[oc % 2]
                            op("act", lambda e, o=o, pt=pt: e.copy(out=o[:], in_=pt[:, 0:256]), [pt.buf], [o.buf])
                            dma("sp", KM[s, l, oc], o[:], reads=[o.buf])
                        for mt in range(2):
                            for hf in range(2):
                                pt = ps[3 + hf]
                                for kc in range(8):
                                    mm(pt, hm[:, kc, mt * 128:(mt + 1) * 128],
                                       wkv[:, kc, 1024 + hf * 512:1024 + (hf + 1) * 512], kc == 0, kc == 7,
                                       [wkv.buf, hm.buf])
                                o = vo[hf]
                                op("dve", lambda e, o=o, pt=pt: e.tensor_copy(out=o[:], in_=pt[:]), [pt.buf], [o.buf])
                                dma("sp", VM[s, l, mt][:, hf * 512:(hf + 1) * 512], o[:], reads=[o.buf])
                    S.barrier()
        S.barrier()

        def qknorm_rope(pz, gq_col, ctab, stab, tabbufs, out_t, tmp, pa, pb):
            zsq, rs_, qn, t1, t2 = tmp
            op("act", lambda e: e.activation(out=zsq[:], in_=pz[:], func=AF.Square), [pz.buf], [zsq.buf])
            mm(pa, bd_b, zsq[:], True, True, [zsq.buf] + CB)
            op("act", lambda e: e.activation(out=rs_[:], in_=pa[:], func=AF.Ln, scale=1.0 / 64, bias=eps_col),
               [pa.buf, cst_f.buf], [rs_.buf])
            op("act", lambda e: e.activation(out=rs_[:], in_=rs_[:], func=AF.Exp, scale=-0.5), [rs_.buf], [rs_.buf])
            op("dve", lambda e: e.scalar_tensor_tensor(out=qn[:], in0=pz[:], scalar=gq_col, in1=rs_[:],
                                                       op0=ALU.mult, op1=ALU.mult),
               [pz.buf, rs_.buf, g_sb.buf], [qn.buf])
            mm(pb, pswap_f, qn[:], True, True, [qn.buf, cst_f.buf])
            op("pool", lambda e: e.tensor_tensor(out=t1[:], in0=qn[:], in1=ctab, op=ALU.mult),
               [qn.buf] + tabbufs, [t1.buf])
            op("dve", lambda e: e.tensor_tensor(out=t2[:], in0=pb[:], in1=stab, op=ALU.mult),
               [pb.buf] + tabbufs, [t2.buf])
            op("dve", lambda e: e.tensor_tensor(out=out_t[:], in0=t1[:], in1=t2[:], op=ALU.add),
               [t1.buf, t2.buf], [out_t.buf])

        def qknorm_group(chunks, pas, pbs, tmpl):
            n = len(chunks)
            for i in range(n):
                pz = chunks[i][0]
                zsq = tmpl[i][0]
                op("act", lambda e, zsq=zsq, pz=pz: e.activation(out=zsq[:], in_=pz[:], func=AF.Square), [pz.buf], [zsq.buf])
            for i in range(n):
                mm(pas[i], bd_b, tmpl[i][0][:], True, True, [tmpl[i][0].buf] + CB)
            for i in range(n):
                rs_, pa = tmpl[i][1], pas[i]
                op("act", lambda e, rs_=rs_, pa=pa: e.activation(out=rs_[:], in_=pa[:], func=AF.Ln, scale=1.0 / 64,
                                                                 bias=eps_col), [pa.buf, cst_f.buf], [rs_.buf])
            for i in range(n):
                rs_ = tmpl[i][1]
                op("act", lambda e, rs_=rs_: e.activation(out=rs_[:], in_=rs_[:], func=AF.Exp, scale=-0.5),
                   [rs_.buf], [rs_.buf])
            for i in range(n):
                pz, gq = chunks[i][0], chunks[i][1]
                rs_, qn = tmpl[i][1], tmpl[i][2]
                op("dve", lambda e, pz=pz, gq=gq, rs_=rs_, qn=qn: e.scalar_tensor_tensor(
                    out=qn[:], in0=pz[:], scalar=gq, in1=rs_[:], op0=ALU.mult, op1=ALU.mult),
                   [pz.buf, rs_.buf, g_sb.buf], [qn.buf])
            for i in range(n):
                mm(pbs[i], pswap_f, tmpl[i][2][:], True, True, [tmpl[i][2].buf, cst_f.buf])
            for i in range(n):
                qn, t1 = tmpl[i][2], tmpl[i][3]
                ctab, tb_ = chunks[i][2], chunks[i][4]
                op("pool", lambda e, qn=qn, t1=t1, ctab=ctab: e.tensor_tensor(out=t1[:], in0=qn[:], in1=ctab, op=ALU.mult),
                   [qn.buf] + tb_, [t1.buf])
            for i in range(n):
                t2, pb = tmpl[i][4], pbs[i]
                stab, tb_ = chunks[i][3], chunks[i][4]
                op("dve", lambda e, t2=t2, pb=pb, stab=stab: e.tensor_tensor(out=t2[:], in0=pb[:], in1=stab, op=ALU.mult),
                   [pb.buf] + tb_, [t2.buf])
            for i in range(n):
                t1, t2, out_t = tmpl[i][3], tmpl[i][4], chunks[i][5]
                op("dve", lambda e, t1=t1, t2=t2, out_t=out_t: e.tensor_tensor(out=out_t[:], in0=t1[:], in1=t2[:], op=ALU.add),
                   [t1.buf, t2.buf], [out_t.buf])

        with ExitStack() as ph:
            wkd = sb(ph, "wkd", [128, 8, 256], BF16)
            wv_ = w_in_ab.rearrange("(k p) n -> p k n", p=128)
            for j in range(4):
                dma("pool", wkd[:, :, j * 64:(j + 1) * 64], wv_[:, :, 512 + (j // 2) * 64:512 + (j // 2) * 64 + 64],
                    writes=[wkd.buf])
            w_in = load_w(ph, "w_in", w_in_ab, 8, 3072, bw=128, order=[5, 0, 1, 2, 3, 4] + list(range(6, 24)))
            xt = [sb(ph, "xt%d" % i, [128, 8, 512], F32) for i in range(2)]
            sq = sb(ph, "sq", [128, 8, 512], BF16)
            rs = sb(ph, "rs", [128, 512], F32)
            ht = [sb(ph, "ht%d" % i, [128, 8, 512], BF16) for i in range(2)]
            tmp = [sb(ph, "zsq", [128, 512], BF16), sb(ph, "rs2", [128, 512], F32), sb(ph, "qn", [128, 512], F32),
                   sb(ph, "t1", [128, 512], F32), sb(ph, "t2", [128, 512], F32)]
            rtab = [sb(ph, "rtab%d" % i, [128, 2, 512], F32) for i in range(2)]
            ob = [sb(ph, "ob%d" % i, [128, 512], BF16) for i in range(4)]
            vaug = [sb(ph, "vaug%d" % i, [128, 4, 4, 128], BF16) for i in range(2)]
            vbt = [sb(ph, "vbo%d" % i, [128, 4, 768], BF16) for i in range(2)]
            for v in vaug:
                op("pool", lambda e, v=v: e.memset(v[:], 1.0), [], [v.buf])
            nob = [0]

            def next_ob():
                nob[0] += 1
                return ob[nob[0] % 4]
            tmp2 = [sb(ph, "zsq", [128, 512], BF16), sb(ph, "rs2", [128, 512], F32), sb(ph, "qn", [128, 512], F32),
                    sb(ph, "t1", [128, 512], F32), sb(ph, "t2", [128, 512], F32)]
            tmps = [tmp, tmp2]
            nqk = [0]

            def next_tmp():
                nqk[0] += 1
                return tmps[nqk[0] % 2]
            tiles = []
            for s in range(3):
                for ti in range(SEGS[s]["NA"] // 512):
                    tiles.append(("a", s, ti))
            for s in range(3):
                for ti in range(SEGS[s]["X"] // 512):
                    tiles.append(("b", s, ti))

            def prep_load(n):
                if n >= len(tiles):
                    return
                kind, s, ti = tiles[n]
                sg = SEGS[s]
                t0 = ti * 512
                x_t, rt = xt[n % 2], rtab[n % 2]
                src = (xTa if kind == "a" else xT)[s].rearrange("(c p) t -> p c t", p=128)
                dma("sp", x_t[:], src[:, :, t0:t0 + 512], writes=[x_t.buf])
                if kind == "a":
                    dma("sp", rt[:], ropek[s][:, :, t0:t0 + 512].rearrange("a p t -> p a t"), writes=[rt.buf])
                else:
                    e0 = t0 - sg["E0"]
                    if 0 <= e0 < sg["E"]:
                        dma("sp", rt[:], ropeq[s][:, :, e0:e0 + 512].rearrange("a p t -> p a t"), writes=[rt.buf])

            def prep_norm(n):
                if n >= len(tiles):
                    return
                rmsnorm(xt[n % 2], 512, 0, ht[n % 2], sq, rs, ps[0])
            prep_load(0)
            prep_norm(0)
            for n, (kind, s, ti) in enumerate(tiles):
                prep_load(n + 1)
                sg = SEGS[s]
                t0 = ti * 512
                h_t, rt = ht[n % 2], rtab[n % 2]
                if kind == "a":
                    va = vaug[n % 2]
                    chunks = []
                    for j in range(2):
                        pz = ps[1 + j]
                        for kc in range(8):
                            mm(pz, wkd[:, kc, j * 128:(j + 1) * 128], h_t[:, kc, :], kc == 0, kc == 7,
                               [wkd.buf, h_t.buf])
                        chunks.append((pz, g_sb[:, 73:74], rt[:, 0, :], rt[:, 1, :], [rt.buf], next_ob()))
                    prep_norm(n + 1)
                    qknorm_group(chunks, [ps[3], ps[4]], [ps[5], ps[6]], tmps)
                    for j in range(2):
                        o = chunks[j][5]
                        dma("sp", KAd[s][j][:, t0:t0 + 512], o[:], reads=[o.buf])
                    for sub in range(4):
                        pv = ps[7] if sub % 2 == 0 else ps[0]
                        for kc in range(8):
                            mm(pv, h_t[:, kc, sub * 128:(sub + 1) * 128], w_in[:, kc, 640:768], kc == 0, kc == 7,
                               wb(w_in, 640, 768) + [h_t.buf], out=pv[:, 0:128])
                        for kv in range(2):
                            op("act", lambda e, pv=pv, kv=kv, sub=sub, va=va: e.copy(
                                out=va[:, sub, 2 * kv, 0:64], in_=pv[:, kv * 64:kv * 64 + 64]), [pv.buf], [va.buf])
                            op("dve", lambda e, pv=pv, kv=kv, sub=sub, va=va: e.tensor_copy(
                                out=va[:, sub, 2 * kv + 1, 64:128], in_=pv[:, kv * 64:kv * 64 + 64]), [pv.buf], [va.buf])
                    dma("sp", VAa[s][t0:t0 + 512, :].rearrange("(s p) c -> p s c", p=128),
                        va[:].rearrange("p s a c -> p s (a c)"), reads=[va.buf])
                else:
                    e0 = t0 - sg["E0"]
                    inE = 0 <= e0 < sg["E"]
                    vb = vbt[n % 2]
                    if inE:
                        for jg in range(2):
                            chunks = []
                            for jj in range(2):
                                j = 2 * jg + jj
                                pz = ps[1 + jj]
                                for kc in range(8):
                                    mm(pz, w_in[:, kc, j * 128:(j + 1) * 128], h_t[:, kc, :], kc == 0, kc == 7,
                                       wb(w_in, j * 128, (j + 1) * 128) + [h_t.buf])
                                chunks.append((pz, g_sb[:, 72:73], rt[:, 0, :], rt[:, 1, :], [rt.buf], next_ob()))
                            qknorm_group(chunks, [ps[3], ps[4]], [ps[5], ps[6]], tmps)
                            for jj in range(2):
                                o = chunks[jj][5]
                                dma("sp", QA[s][2 * jg + jj][:, e0:e0 + 512], o[:], reads=[o.buf])
                    for j in range(12):
                        isq = j < 6
                        if isq and not inE:
                            continue
                        pz = ps[1 + j % 2]
                        c0 = 768 + j * 128
                        for kc in range(8):
                            mm(pz, w_in[:, kc, c0:c0 + 128], h_t[:, kc, :], kc == 0, kc == 7, wb(w_in, c0, c0 + 128) + [h_t.buf])
                        o = next_ob()
                        if j % 2 == 0:
                            op("act", lambda e, o=o, pz=pz: e.copy(out=o[:], in_=pz[:]), [pz.buf], [o.buf])
                        else:
                            op("dve", lambda e, o=o, pz=pz: e.tensor_copy(out=o[:], in_=pz[:]), [pz.buf], [o.buf])
                        if isq:
                            dma("sp", QB[s][j][:, e0:e0 + 512], o[:], reads=[o.buf])
                        else:
                            dma("sp", KB[s][j - 6][:, t0:t0 + 512], o[:], reads=[o.buf])
                    prep_norm(n + 1)
                    for sub in range(4):
                        for hf in range(2):
                            pv = ps[5 + hf]
                            c0 = 768 + 1536 + hf * 384
                            for kc in range(8):
                                mm(pv, h_t[:, kc, sub * 128:(sub + 1) * 128], w_in[:, kc, c0:c0 + 384], kc == 0, kc == 7,
                                   wb(w_in, c0, c0 + 384) + [h_t.buf], out=pv[:, 0:384])
                            if hf == 0:
                                op("act", lambda e, pv=pv, sub=sub, vb=vb: e.copy(
                                    out=vb[:, sub, 0:384], in_=pv[:, 0:384]), [pv.buf], [vb.buf])
                            else:
                                op("dve", lambda e, pv=pv, sub=sub, vb=vb: e.tensor_copy(
                                    out=vb[:, sub, 384:768], in_=pv[:, 0:384]), [pv.buf], [vb.buf])
                    dma("sp", VB[s][t0:t0 + 512, :].rearrange("(s p) c -> p s c", p=128), vb[:], reads=[vb.buf])
        S.barrier()

        for s in range(3):
            sg = SEGS[s]
            NA, E = sg["NA"], sg["E"]
            nkt = NA // 128
            for kv in range(2):
                with ExitStack() as ph:
                    kt = sb(ph, "kt", [128, NA], BF16)
                    vt = sb(ph, "vt", [128, nkt, 256], BF16)
                    qz = [[sb(ph, "qz%d_%d" % (i, j), [128, 512], BF16) for j in range(2)] for i in range(3)]
                    pt_ = [sb(ph, "pt%d" % i, [128, 1024], BF16) for i in range(3)]
                    rd = sb(ph, "rd", [128, 512], F32)
                    oo = [sb(ph, "oo%d" % i, [128, 512], BF16) for i in range(2)]
                    ktb = [Buf("ktb%d" % i) for i in range(NA // 2048)]
                    vtb = [Buf("vtb%d" % i) for i in range(nkt // 16)]
                    for ci in range(NA // 2048):
                        c = ci * 2048
                        dma("sp", kt[:, c:c + 2048], KAd[s][kv][:, c:c + 2048], writes=[ktb[ci]])
                        c = ci * 16
                        dma("sp", vt[:, c:c + 16, :],
                            VAa[s][c * 128:(c + 16) * 128, kv * 256:(kv + 1) * 256].rearrange("(t p) c -> p t c", p=128),
                            writes=[vtb[ci]])
                    nqt = E // 512
                    qorder = [(hp, qi) for hp in range(2) for qi in range(nqt)]
                    nk2 = nkt // 2
                    items = [(n, par, kk) for n in range(len(qorder)) for par in range(2) for kk in range(nk2)]
                    for qq in qz:
                        for q1 in qq:
                            op("pool", lambda e, q1=q1: e.memset(q1[:], 0.0), [], [q1.buf])

                    def load_q(n):
                        if n < len(qorder):
                            hp, qi = qorder[n]
                            for par in range(2):
                                q1 = qz[n % 3][par]
                                dma("sp", q1[par * 64:par * 64 + 64, :],
                                    QA[s][kv * 2 + hp][par * 64:par * 64 + 64, qi * 512:(qi + 1) * 512], writes=[q1.buf])

                    def emit_S(i):
                        n, par, kk = items[i]
                        if par == 0 and kk == 0:
                            load_q(n + 1)
                        q1 = qz[n % 3][par]
                        for a_ in range(2):
                            k = 2 * kk + a_
                            pS = ps[2 * (i % 3) + a_]
                            mm(pS, kt[:, k * 128:(k + 1) * 128], q1[:], True, True, [ktb[k // 16], q1.buf])
                    LA = 2
                    load_q(0)
                    for i in range(min(LA, len(items))):
                        emit_S(i)
                    for i, (n, par, kk) in enumerate(items):
                        if i + LA < len(items):
                            emit_S(i + LA)
                        hp, qi = qorder[n]
                        ch = kv * 2 + hp
                        b0 = par * 64
                        d0 = 64 - b0
                        j3 = i % 3
                        p_t = pt_[j3]
                        po = ps[6 + (2 * n + par) % 2]
                        o_t = oo[n % 2]
                        op("act", lambda e, j3=j3, p_t=p_t: e.activation(
                            out=p_t[:], in_=psall[:, 1024 * j3:1024 * j3 + 1024], func=AF.Exp, scale=0.125),
                           [ps[2 * j3].buf, ps[2 * j3 + 1].buf], [p_t.buf])
                        for a_ in range(2):
                            k = 2 * kk + a_
                            mm(po, vt[:, k, par * 128:(par + 1) * 128], p_t[:, a_ * 512:(a_ + 1) * 512], k == 0, k == nkt - 1,
                               [vtb[k // 16], p_t.buf])
                        if kk == nk2 - 1:
                            op("dve", lambda e, po=po, d0=d0: e.reciprocal(out=rd[d0:d0 + 64, :], in_=po[d0:d0 + 64, :]),
                               [po.buf], [rd.buf])
                            op("dve", lambda e, po=po, d0=d0, b0=b0, o_t=o_t: e.tensor_tensor(
                                out=o_t[b0:b0 + 64, :], in0=po[b0:b0 + 64, :], in1=rd[d0:d0 + 64, :], op=ALU.mult),
                               [po.buf, rd.buf], [o_t.buf])
                            if par == 1:
                                dma("sp", OAB[s][ch][:, qi * 512:(qi + 1) * 512], o_t[:], reads=[o_t.buf])
                S.barrier()

        vb_sb = sb(es, "vb_sb", [128, 2048], F32)
        dma("sp", vb_sb[:], vbt_in, writes=[vb_sb.buf])
        ktile_no = 0
        for s in range(3):
            sg = SEGS[s]
            E, X, E0 = sg["E"], sg["X"], sg["E0"]
            jobs = b_jobs(sg)
            with ExitStack() as ph:
                accO = [sb(ph, "accO%d" % i, [128, E], F32) for i in range(2)]
                accD = [sb(ph, "accD%d" % i, [128, E], F32) for i in range(2)]
                qb = [sb(ph, "qb%d" % i, [128, E], BF16) for i in range(2)]
                kb = [sb(ph, "kb%d" % i, [128, X], BF16) for i in range(2)]
                tab = [sb(ph, "tab%d" % i, [128, 256], F32) for i in range(4)]
                vk = [sb(ph, "vk%d" % i, [128, 2, 256], BF16) for i in range(3)]
                sc = [sb(ph, "sc%d" % i, [128, 256], F32) for i in range(2)]
                pp = [sb(ph, "pp%d" % i, [128, 256], BF16) for i in range(3)]
                ofin = sb(ph, "ofin", [128, E], BF16)
                nj0 = [0]
                kt_state = [ktile_no]
                for g, d in enumerate(B_DIL):
                    for pr in range(2):
                        dma("sp", qb[pr][:], QB[s][g * 2 + pr], writes=[qb[pr].buf])
                        dma("sp", kb[pr][:], KB[s][g * 2 + pr], writes=[kb[pr].buf])
                    for h in range(4):
                        c = alibi_c(g, h)
                        op("dve", lambda e, h=h, c=c: e.scalar_tensor_tensor(
                            out=tab[h][:], in0=btabA, scalar=c, in1=btabM, op0=ALU.mult, op1=ALU.add),
                           [cst_f.buf], [tab[h].buf])
                    gjobs = [j for j in jobs if j[0] == g]
                    items = [(ji, h) for ji in range(len(gjobs)) for h in range(4)]
                    jinfo = {}

                    def load_v(ji):
                        nonlocal_k = kt_state
                        jg, r, i0, nq, kts = gjobs[ji]
                        v_t = vk[(nj0[0] + ji) % 3]
                        ids = []
                        for a_, (ja, kp, p0) in enumerate(kts):
                            tok0 = E0 + ja * d + r
                            src = VB[s][tok0:tok0 + (kp - 1) * d + 1:d, g * 256:(g + 1) * 256]
                            dma("sp", v_t[0:kp, a_, :], src, writes=[v_t.buf])
                            ids.append(nonlocal_k[0])
                            nonlocal_k[0] += 1
                        jinfo[ji] = (v_t, ids)

                    def emit_S(i):
                        ji, h = items[i]
                        if h == 0:
                            load_v(ji)
                        jg, r, i0, nq, kts = gjobs[ji]
                        pr, b0 = h // 2, (h % 2) * 64
                        qs = slice(i0 * d + r, (i0 + nq - 1) * d + r + 1, d)
                        pS = ps[i % 3]
                        for a_, (ja, kp, p0) in enumerate(kts):
                            t0k = E0 + ja * d + r
                            ks = slice(t0k, t0k + (kp - 1) * d + 1, d)
                            mm(pS, kb[pr][b0:b0 + 64, ks], qb[pr][b0:b0 + 64, qs], True, True,
                               [kb[pr].buf, qb[pr].buf], out=pS[0:kp, a_ * 128:a_ * 128 + nq])

                    def emit_bias(i):
                        ji, h = items[i]
                        jg, r, i0, nq, kts = gjobs[ji]
                        pS, s_t = ps[i % 3], sc[i % 2]
                        for a_, (ja, kp, p0) in enumerate(kts):
                            op("dve", lambda e, a_=a_, kp=kp, p0=p0: e.scalar_tensor_tensor(
                                out=s_t[0:kp, a_ * 128:a_ * 128 + nq], in0=pS[0:kp, a_ * 128:a_ * 128 + nq], scalar=0.125,
                                in1=tab[h][p0:p0 + kp, a_ * 128:a_ * 128 + nq], op0=ALU.mult, op1=ALU.add),
                               [pS.buf, tab[h].buf], [s_t.buf])
                    emit_S(0)
                    if len(items) > 1:
                        emit_S(1)
                    emit_bias(0)
                    for i, (ji, h) in enumerate(items):
                        if i + 2 < len(items):
                            emit_S(i + 2)
                        if i + 1 < len(items):
                            emit_bias(i + 1)
                        jg, r, i0, nq, kts = gjobs[ji]
                        v_t, kt_ids = jinfo[ji]
                        pr, par = h // 2, h % 2
                        b0 = par * 64
                        qs = slice(i0 * d + r, (i0 + nq - 1) * d + r + 1, d)
                        pO, pD = ps[3 + 2 * (i % 2)], ps[4 + 2 * (i % 2)]
                        s_t, p_t = sc[i % 2], pp[i % 3]
                        na = len(kts)
                        for a_, (ja, kp, p0) in enumerate(kts):
                            kid = kt_ids[a_]
                            op("act", lambda e, a_=a_, kp=kp, kid=kid: e.activation(
                                out=p_t[0:kp, a_ * 128:a_ * 128 + nq], in_=s_t[0:kp, a_ * 128:a_ * 128 + nq], func=AF.Exp,
                                bias=vb_sb[0:kp, kid:kid + 1]), [s_t.buf, vb_sb.buf], [p_t.buf])
                        for a_, (ja, kp, p0) in enumerate(kts):
                            mm(pO, v_t[0:kp, a_, pr * 128:(pr + 1) * 128], p_t[0:kp, a_ * 128:a_ * 128 + nq],
                               a_ == 0, a_ == na - 1, [v_t.buf, p_t.buf], out=pO[:, 0:nq])
                        for a_, (ja, kp, p0) in enumerate(kts):
                            mm(pD, cst_b[0:kp, 0:128], p_t[0:kp, a_ * 128:a_ * 128 + nq],
                               a_ == 0, a_ == na - 1, [p_t.buf] + CB, out=pD[:, 0:nq])
                        for (pX, acc) in ((pO, accO[pr]), (pD, accD[pr])):
                            if g == 0:
                                op("dve", lambda e, pX=pX, acc=acc: e.tensor_copy(
                                    out=acc[b0:b0 + 64, qs], in_=pX[b0:b0 + 64, 0:nq]), [pX.buf], [acc.buf])
                            else:
                                op("dve", lambda e, pX=pX, acc=acc: e.tensor_tensor(
                                    out=acc[b0:b0 + 64, qs], in0=pX[b0:b0 + 64, 0:nq], in1=acc[b0:b0 + 64, qs],
                                    op=ALU.add), [pX.buf, acc.buf], [acc.buf])
                    nj0[0] += len(gjobs)
                ktile_no = kt_state[0]
                assert ktile_no <= 2048
                for pr in range(2):
                    op("dve", lambda e, pr=pr: e.tensor_scalar_max(out=accD[pr][:], in0=accD[pr][:], scalar1=1e-30),
                       [accD[pr].buf], [accD[pr].buf])
                    op("dve", lambda e, pr=pr: e.reciprocal(out=accD[pr][:], in_=accD[pr][:]), [accD[pr].buf], [accD[pr].buf])
                    op("dve", lambda e, pr=pr: e.tensor_tensor(out=ofin[:], in0=accO[pr][:], in1=accD[pr][:], op=ALU.mult),
                       [accO[pr].buf, accD[pr].buf], [ofin.buf])
                    dma("sp", OAB[s][4 + pr], ofin[:], reads=[ofin.buf])
            S.barrier()

        def post_mixer(l, src_x, src_x_off, src_o, nchunk_o, w_out_dram, dst_x2):
            with ExitStack() as ph:
                wo_ = load_w(ph, "w_o", w_out_dram, nchunk_o, D)
                wq = load_w(ph, "wq", wq_x[l], 8, D)
                wo = load_w(ph, "wo", wo_x[l], 8, D)
                km = [sb(ph, "km", [128, 8, 256], BF16) for i in range(2)]
                vm = [sb(ph, "vm", [128, 2, 1024], BF16) for i in range(2)]
                xt = [sb(ph, "xt%d" % i, [128, 8, 512], F32) for i in range(2)]
                ot = [sb(ph, "ot%d" % i, [128, 8, 512], BF16) for i in range(2)]
                sq = sb(ph, "sq", [128, 8, 512], BF16)
                rs = sb(ph, "rs", [128, 512], F32)
                hx = [sb(ph, "hx%d" % i, [128, 8, 512], BF16) for i in range(2)]
                qx = sb(ph, "qx", [128, 8, 512], BF16)
                px = [sb(ph, "px%d" % i, [128, 2, 512], BF16) for i in range(2)]
                rdx = sb(ph, "rdx", [128, 512], F32)
                ox = [sb(ph, "ox%d" % i, [128, 8, 512], BF16) for i in range(2)]
                tiles = [(s, ti) for s in range(3) for ti in range(dst_x2[s].shape[2] // 512)]

                def S0(n):
                    if n >= len(tiles):
                        return
                    s, ti = tiles[n]
                    t0 = ti * 512
                    x_t, o_t = xt[n % 2], ot[n % 2]
                    if ti == 0:
                        dma("sp", km[s % 2][:], KM[s, l].rearrange("c p m -> p c m"), writes=[km[s % 2].buf])
                        dma("sp", vm[s % 2][:], VM[s, l].rearrange("t p c -> p t c"), writes=[vm[s % 2].buf])
                    xo = src_x_off[s]
                    dma("sp", x_t[:], src_x[s][:, :, xo + t0:xo + t0 + 512].rearrange("c p t -> p c t")
                        if len(src_x[s].shape) == 3 else
                        src_x[s].rearrange("(c p) t -> p c t", p=128)[:, :, xo + t0:xo + t0 + 512], writes=[x_t.buf])
                    dma("sp", o_t[:, 0:nchunk_o, :], src_o[s][:, :, t0:t0 + 512].rearrange("c p t -> p c t"),
                        writes=[o_t.buf])

                def S1(n):
                    if n >= len(tiles):
                        return
                    x_t, o_t = xt[n % 2], ot[n % 2]
                    for oc in range(8):
                        pz = ps[oc % 2]
                        for kc in range(nchunk_o):
                            mm(pz, wo_[:, kc, oc * 128:(oc + 1) * 128], o_t[:, kc, :], kc == 0, kc == nchunk_o - 1,
                               [wo_.buf, o_t.buf])
                        op("dve", lambda e, pz=pz, oc=oc: e.tensor_tensor(
                            out=x_t[:, oc, :], in0=pz[:], in1=x_t[:, oc, :], op=ALU.add), [pz.buf, x_t.buf], [x_t.buf])
                    rmsnorm(x_t, 512, 2 + l, hx[n % 2], sq, rs, ps[2])

                def S2a(n):
                    h_x = hx[n % 2]
                    for oc in range(8):
                        pz = ps[oc % 2]
                        for kc in range(8):
                            mm(pz, wq[:, kc, oc * 128:(oc + 1) * 128], h_x[:, kc, :], kc == 0, kc == 7, [wq.buf, h_x.buf])
                        if oc % 2 == 0:
                            op("act", lambda e, pz=pz, oc=oc: e.copy(out=qx[:, oc, :], in_=pz[:]), [pz.buf], [qx.buf])
                        else:
                            op("dve", lambda e, pz=pz, oc=oc: e.tensor_copy(out=qx[:, oc, :], in_=pz[:]), [pz.buf], [qx.buf])

                def S2(n):
                    s, ti = tiles[n]
                    o_x = ox[n % 2]
                    k_m, v_m = km[s % 2], vm[s % 2]
                    for hd in range(4):
                        p_x = px[hd % 2]
                        for mt in range(2):
                            pS = ps[3 + mt]
                            for dc in range(2):
                                mm(pS, k_m[:, 2 * hd + dc, mt * 128:(mt + 1) * 128], qx[:, 2 * hd + dc, :], dc == 0, dc == 1,
                                   [k_m.buf, qx.buf])
                            op("act", lambda e, pS=pS, mt=mt, p_x=p_x: e.activation(
                                out=p_x[:, mt, :], in_=pS[:], func=AF.Exp, scale=1.0 / 16), [pS.buf], [p_x.buf])
                        pD = ps[5]
                        for mt in range(2):
                            mm(pD, ones_b, p_x[:, mt, :], mt == 0, mt == 1, [p_x.buf] + CB)
                        op("act", lambda e, pD=pD: e.activation(out=rdx[:], in_=pD[:], func=AF.Ln), [pD.buf], [rdx.buf])
                        op("act", lambda e: e.activation(out=rdx[:], in_=rdx[:], func=AF.Exp, scale=-1.0), [rdx.buf], [rdx.buf])
                        for dc in range(2):
                            pO = ps[6 + dc]
                            for mt in range(2):
                                mm(pO, v_m[:, mt, (2 * hd + dc) * 128:(2 * hd + dc + 1) * 128], p_x[:, mt, :], mt == 0, mt == 1,
                                   [v_m.buf, p_x.buf])
                            op("dve", lambda e, pO=pO, hd=hd, dc=dc: e.tensor_tensor(
                                out=o_x[:, 2 * hd + dc, :], in0=pO[:], in1=rdx[:], op=ALU.mult), [pO.buf, rdx.buf], [o_x.buf])

                def S3(n):
                    s, ti = tiles[n]
                    t0 = ti * 512
                    x_t, o_x = xt[n % 2], ox[n % 2]
                    for oc in range(8):
                        pz = ps[oc % 2]
                        for kc in range(8):
                            mm(pz, wo[:, kc, oc * 128:(oc + 1) * 128], o_x[:, kc, :], kc == 0, kc == 7, [wo.buf, o_x.buf])
                        op("dve", lambda e, pz=pz, oc=oc: e.tensor_tensor(
                            out=x_t[:, oc, :], in0=pz[:], in1=x_t[:, oc, :], op=ALU.add), [pz.buf, x_t.buf], [x_t.buf])
                    dma("sp", dst_x2[s][:, :, t0:t0 + 512].rearrange("c p t -> p c t"), x_t[:], reads=[x_t.buf])
                S0(0)
                S1(0)
                for n in range(len(tiles)):
                    S0(n + 1)
                    S2a(n)
                    S1(n + 1)
                    S2(n)
                    S3(n)
            S.barrier()

        def ffn(l, src, dst, final):
            N = 256
            with ExitStack() as ph:
                wgu = load_w(ph, "wgu", w_gu[l], 8, 2 * DFF, bw=256, order=[x for i in range(11) for x in (i, 11 + i)])
                wdn = load_w(ph, "wdn", w_down[l], 22, D)
                xt = [sb(ph, "xt%d" % i, [128, 8, N], F32) for i in range(2)]
                sq = sb(ph, "sq", [128, 8, N], BF16)
                rs = sb(ph, "rs", [128, N], F32)
                hfs = [sb(ph, "hf%d" % i, [128, 8, N], BF16) for i in range(2)]
                at = sb(ph, "at", [128, 22, N], BF16)
                sg_ = [sb(ph, "sg%d" % i, [128, N], F32) for i in range(2)]
                yo = sb(ph, "yo", [128, 8, N], F32)
                sq2 = sb(ph, "sq2", [128, 8, N], BF16) if final else None
                rs2 = sb(ph, "rs2", [128, N], F32) if final else None
                tiles = [(s, ti) for s in range(3) for ti in range(src[s].shape[2] // N)]

                def fload(n):
                    if n >= len(tiles):
                        return
                    s, ti = tiles[n]
                    t0 = ti * N
                    x_t = xt[n % 2]
                    dma("sp", x_t[:], src[s][:, :, t0:t0 + N].rearrange("c p t -> p c t"), writes=[x_t.buf])

                def prep(n):
                    if n >= len(tiles):
                        return
                    rmsnorm(xt[n % 2], N, 6 + l, hfs[n % 2], sq, rs, ps[0])
                fload(0)
                prep(0)
                for n, (s, ti) in enumerate(tiles):
                    fload(n + 1)
                    t0 = ti * N
                    x_t, hf = xt[n % 2], hfs[n % 2]
                    for j in range(22):
                        pg, pu = ps[1 + 2 * (j % 2)], ps[2 + 2 * (j % 2)]
                        for kc in range(8):
                            mm(pg, wgu[:, kc, j * 128:(j + 1) * 128], hf[:, kc, :], kc == 0, kc == 7, wb(wgu, j * 128, (j + 1) * 128) + [hf.buf],
                               out=pg[:, 0:N])
                        for kc in range(8):
                            mm(pu, wgu[:, kc, DFF + j * 128:DFF + (j + 1) * 128], hf[:, kc, :], kc == 0, kc == 7,
                               wb(wgu, DFF + j * 128, DFF + (j + 1) * 128) + [hf.buf], out=pu[:, 0:N])
                        st = sg_[j % 2]
                        op("act", lambda e, pg=pg, st=st: e.activation(out=st[:], in_=pg[:, 0:N], func=AF.Silu),
                           [pg.buf], [st.buf])
                        op("dve", lambda e, pu=pu, st=st, j=j: e.tensor_tensor(
                            out=at[:, j, :], in0=pu[:, 0:N], in1=st[:], op=ALU.mult), [pu.buf, st.buf], [at.buf])
                    prep(n + 1)
                    for oc in range(8):
                        pz = ps[5 + oc % 2]
                        for kc in range(22):
                            mm(pz, wdn[:, kc, oc * 128:(oc + 1) * 128], at[:, kc, :], kc == 0, kc == 21, [wdn.buf, at.buf],
                               out=pz[:, 0:N])
                        op("dve", lambda e, pz=pz, oc=oc, x_t=x_t: e.tensor_tensor(
                            out=x_t[:, oc, :], in0=pz[:, 0:N], in1=x_t[:, oc, :], op=ALU.add), [pz.buf, x_t.buf], [x_t.buf])
                    if not final:
                        dma("sp", dst[s][:, :, t0:t0 + N].rearrange("c p t -> p c t"), x_t[:], reads=[x_t.buf])
                    else:
                        op("act", lambda e, x_t=x_t: e.activation(out=sq2[:], in_=x_t[:], func=AF.Square), [x_t.buf], [sq2.buf])
                        for c in range(8):
                            mm(ps[7], ones_b, sq2[:, c, :], c == 0, c == 7, [sq2.buf] + CB, out=ps[7][:, 0:N])
                        op("act", lambda e: e.activation(out=rs2[:], in_=ps[7][:, 0:N], func=AF.Ln, scale=1.0 / D,
                                                         bias=eps_col), [ps[7].buf, cst_f.buf], [rs2.buf])
                        op("act", lambda e: e.activation(out=rs2[:], in_=rs2[:], func=AF.Exp, scale=-0.5), [rs2.buf], [rs2.buf])
                        g = gcol(8)
                        for c in range(8):
                            op("dve", lambda e, c=c, x_t=x_t: e.scalar_tensor_tensor(
                                out=yo[:, c, :], in0=x_t[:, c, :], scalar=g[:, c:c + 1], in1=rs2[:],
                                op0=ALU.mult, op1=ALU.mult), [x_t.buf, rs2.buf, g_sb.buf], [yo.buf])
                        dma("sp", dst[s].rearrange("(c p) t -> p c t", p=128)[:, :, t0:t0 + N], yo[:], reads=[yo.buf])
            S.barrier()

        post_mixer(0, xT, [SEGS[s]["E0"] for s in range(3)], OAB, 6, w_out_ab[:, :], X2)
        ffn(0, X2, X3, False)

        X2o = [dscr("X2o_%d" % s, [8, 128, SEGS[s]["OWN"]], F32) for s in range(3)]
        with ExitStack() as ph:
            w_c = load_w(ph, "w_c", w_in_c, 8, 3072, bw=128)
            xt = [sb(ph, "xt%d" % i, [128, 8, 512], F32) for i in range(2)]
            sq = sb(ph, "sq", [128, 8, 512], BF16)
            rs = sb(ph, "rs", [128, 512], F32)
            ht = [sb(ph, "ht%d" % i, [128, 8, 512], BF16) for i in range(2)]
            ob = [sb(ph, "ob%d" % i, [128, 512], BF16) for i in range(3)]
            vca = [sb(ph, "vca%d" % i, [128, 4, 16, 128], BF16) for i in range(2)]
            for v in vca:
                op("pool", lambda e, v=v: e.memset(v[:], 1.0), [], [v.buf])
            nob = 0
            tiles5 = [(s, ti) for s in range(3) for ti in range(SEGS[s]["E"] // 512)]

            def load5(n):
                if n >= len(tiles5):
                    return
                s, ti = tiles5[n]
                x_t = xt[n % 2]
                dma("sp", x_t[:], X3[s][:, :, ti * 512:ti * 512 + 512].rearrange("c p t -> p c t"), writes=[x_t.buf])

            def norm5(n):
                if n >= len(tiles5):
                    return
                rmsnorm(xt[n % 2], 512, 1, ht[n % 2], sq, rs, ps[0])
            load5(0)
            norm5(0)
            for tno, (s, ti) in enumerate(tiles5):
                if True:
                    load5(tno + 1)
                    t0 = ti * 512
                    h_t, vc = ht[tno % 2], vca[tno % 2]
                    for j in range(16):
                        pz = ps[1 + j % 2]
                        for kc in range(8):
                            mm(pz, w_c[:, kc, j * 128:(j + 1) * 128], h_t[:, kc, :], kc == 0, kc == 7, wb(w_c, j * 128, (j + 1) * 128) + [h_t.buf])
                        o = ob[nob % 3]
                        nob += 1
                        if j % 2 == 0:
                            op("act", lambda e, o=o, pz=pz: e.copy(out=o[:], in_=pz[:]), [pz.buf], [o.buf])
                        else:
                            op("dve", lambda e, o=o, pz=pz: e.tensor_copy(out=o[:], in_=pz[:]), [pz.buf], [o.buf])
                        dst = QC[s][j] if j < 8 else KC[s][j - 8]
                        dma("sp", dst[:, t0:t0 + 512], o[:], reads=[o.buf])
                    norm5(tno + 1)
                    for sub in range(4):
                        for hf in range(2):
                            pv = ps[5 + hf]
                            c0 = 2048 + hf * 512
                            for kc in range(8):
                                mm(pv, h_t[:, kc, sub * 128:(sub + 1) * 128], w_c[:, kc, c0:c0 + 512], kc == 0, kc == 7,
                                   wb(w_c, c0, c0 + 512) + [h_t.buf])
                            pv4 = pv[:].rearrange("p (h t d) -> p h t d", t=2, d=64)
                            vc5 = vc[:, sub, hf * 8:(hf + 1) * 8, :].rearrange("p (h t) c -> p h t c", t=2)
                            op("act", lambda e, pv4=pv4, vc5=vc5: e.copy(out=vc5[:, :, 0, 0:64], in_=pv4[:, :, 0, :]),
                               [pv.buf], [vc.buf])
                            op("dve", lambda e, pv4=pv4, vc5=vc5: e.tensor_copy(out=vc5[:, :, 1, 64:128], in_=pv4[:, :, 1, :]),
                               [pv.buf], [vc.buf])
                    dma("sp", VCa[s][t0:t0 + 512, :].rearrange("(s p) c -> p s c", p=128),
                        vc[:].rearrange("p s h c -> p s (h c)"), reads=[vc.buf])
        S.barrier()

        cm_sb = sb(es, "cm_sb", [128, 128], mybir.dt.uint32)
        dma("sp", cm_sb[:], cmask_in, writes=[cm_sb.buf])
        for s in range(3):
            sg = SEGS[s]
            E, OWN, O0 = sg["E"], sg["OWN"], sg["O0"]
            jobs = c_jobs(sg)
            nrE = E // 64
            with ExitStack() as ph:
                npA = nrE // 2
                sets = []
                for i in range(2):
                    sets.append(dict(
                        qc=sb(ph, "qc", [128, OWN], BF16), kc=sb(ph, "kc", [128, E], BF16),
                        vA=sb(ph, "vA", [128, npA, 256], BF16), vB=sb(ph, "vB", [128, npA - 1, 256], BF16),
                        ctb=[sb(ph, "ctb%d" % j, [128, 15, 64], F32) for j in range(2)],
                        oc=sb(ph, "oc_t", [128, OWN], BF16)))
                sc = [sb(ph, "sc%d" % i, [128, 256], F32) for i in range(2)]
                pp = [sb(ph, "pp%d" % i, [128, 256], BF16) for i in range(3)]
                rdb = [sb(ph, "rd%d" % i, [128, 512], F32) for i in range(2)]
                oe = sb(ph, "oe", [128, 512], BF16)

                def load_set(ch):
                    if ch >= 8:
                        return
                    st = sets[ch % 2]
                    dma("sp", st["qc"][:], QC[s][ch][:, O0:O0 + OWN], writes=[st["qc"].buf])
                    dma("sp", st["kc"][:], KC[s][ch], writes=[st["kc"].buf])
                    dma("sp", st["vA"][:], VCa[s][:, ch * 256:(ch + 1) * 256].rearrange("(m p) c -> p m c", p=128),
                        writes=[st["vA"].buf])
                    dma("sp", st["vB"][:], VCa[s][64:E - 64, ch * 256:(ch + 1) * 256].rearrange("(m p) c -> p m c", p=128),
                        writes=[st["vB"].buf])
                    for par in range(2):
                        dma("sp", st["ctb"][par][:], ctab_in[2 * ch + par].rearrange("p (a b) -> p a b", b=64),
                            writes=[st["ctb"][par].buf])
                load_set(0)
                for ch in range(8):
                    load_set(ch + 1)
                    st = sets[ch % 2]
                    qc, kc_, vA, vB, ctb, oc_t = st["qc"], st["kc"], st["vA"], st["vB"], st["ctb"], st["oc"]
                    items = []
                    for par in range(2):
                        nrm = [j for j in jobs if j[3] == 0]
                        edg = [j for j in jobs if j[3] != 0]
                        for bi in range(0, len(nrm), 8):
                            grp = nrm[bi:bi + 8]
                            for gi, j in enumerate(grp):
                                items.append((par,) + j + (gi, len(grp), bi // 8))
                        for gi, j in enumerate(edg):
                            items.append((par,) + j + (gi, len(edg), -1))
                    nbatch = [0]

                    def emit_S(i):
                        par, r, kr0, dr0, kind = items[i][:5]
                        b0 = par * 64
                        pS = ps[i % 3]
                        for i2 in range(4):
                            k0 = (kr0 + 2 * i2) * 64
                            mm(pS, kc_[b0:b0 + 64, k0:k0 + 128], qc[b0:b0 + 64, r * 64:(r + 1) * 64],
                               True, True, [kc_.buf, qc.buf], out=pS[:, i2 * 64:(i2 + 1) * 64])

                    def emit_bias(i):
                        par, r, kr0, dr0, kind = items[i][:5]
                        pS, s_t = ps[i % 3], sc[i % 2]
                        op("dve", lambda e: e.scalar_tensor_tensor(
                            out=s_t[:].rearrange("p (a b) -> p a b", b=64),
                            in0=pS[:, 0:256].rearrange("p (a b) -> p a b", b=64), scalar=0.125,
                            in1=ctb[par][:, dr0:dr0 + 8:2, :], op0=ALU.mult, op1=ALU.add),
                           [pS.buf, ctb[par].buf], [s_t.buf])

                    def finalize(fin):
                        par, pO, rd_, grp = fin
                        b0 = par * 64
                        d0 = 64 - b0
                        nb = len(grp)
                        W = nb * 64
                        op("act", lambda e: e.activation(out=rd_[d0:d0 + 64, 0:W], in_=pO[d0:d0 + 64, 0:W], func=AF.Ln),
                           [pO.buf], [rd_.buf])
                        op("act", lambda e: e.activation(out=rd_[d0:d0 + 64, 0:W], in_=rd_[d0:d0 + 64, 0:W], func=AF.Exp,
                                                         scale=-1.0), [rd_.buf], [rd_.buf])
                        if grp[0][3] == 0:
                            r0 = grp[0][0]
                            op("dve", lambda e: e.tensor_tensor(
                                out=oc_t[b0:b0 + 64, r0 * 64:r0 * 64 + W], in0=pO[b0:b0 + 64, 0:W], in1=rd_[d0:d0 + 64, 0:W],
                                op=ALU.mult), [pO.buf, rd_.buf], [oc_t.buf])
                        else:
                            op("dve", lambda e: e.tensor_tensor(
                                out=oe[b0:b0 + 64, 0:W], in0=pO[b0:b0 + 64, 0:W], in1=rd_[d0:d0 + 64, 0:W],
                                op=ALU.mult), [pO.buf, rd_.buf], [oe.buf])
                            for gi, (r, kr0, dr0, kind) in enumerate(grp):
                                mk = (kind - 1) * 64
                                op("dve", lambda e, gi=gi, r=r, mk=mk: e.copy_predicated(
                                    out=oc_t[b0:b0 + 64, r * 64:(r + 1) * 64], mask=cm_sb[b0:b0 + 64, mk:mk + 64],
                                    data=oe[b0:b0 + 64, gi * 64:(gi + 1) * 64]), [oe.buf, cm_sb.buf, oc_t.buf], [oc_t.buf])
                    emit_S(0)
                    emit_S(1)
                    emit_bias(0)
                    pending = None
                    cur = []
                    for i, it in enumerate(items):
                        par, r, kr0, dr0, kind, gi, gn, bid = it
                        if i + 2 < len(items):
                            emit_S(i + 2)
                        if i + 1 < len(items):
                            emit_bias(i + 1)
                        if gi == 0:
                            nbatch[0] += 1
                            cur = []
                        cur.append((r, kr0, dr0, kind))
                        pO = ps[3 + nbatch[0] % 2]
                        rd_ = rdb[nbatch[0] % 2]
                        s_t, p_t = sc[i % 2], pp[i % 3]
                        op("act", lambda e, s_t=s_t, p_t=p_t: e.activation(out=p_t[:], in_=s_t[:], func=AF.Exp),
                           [s_t.buf], [p_t.buf])
                        if pending is not None:
                            finalize(pending)
                            pending = None
                        V2, m0 = (vA, kr0 // 2) if kr0 % 2 == 0 else (vB, (kr0 - 1) // 2)
                        for i2 in range(4):
                            mm(pO, V2[:, m0 + i2, par * 128:(par + 1) * 128], p_t[:, i2 * 64:(i2 + 1) * 64], i2 == 0, i2 == 3,
                               [V2.buf, p_t.buf], out=pO[:, gi * 64:(gi + 1) * 64])
                        if gi == gn - 1:
                            pending = (par, pO, rd_, list(cur))
                    if pending is not None:
                        finalize(pending)
                    dma("sp", OC[s][ch], oc_t[:], reads=[oc_t.buf])
            S.barrier()

        post_mixer(1, X3, [SEGS[s]["O0"] for s in range(3)], OC, 8, w_out_c[:, :], X2o)
        ffn(1, X2o, yT, True)
        S.barrier()
        print("bass instructions:", S.nins)
    return nc


def _rope_tab(pos):
    pos = np.asarray(pos, dtype=np.int64)
    valid = (pos >= 0)
    p = np.where(valid, pos, 0)
    row = (p // 64).astype(np.float32)
    col = (p % 64).astype(np.float32)
    inv = (np.float32(10000.0) ** (-np.arange(0, 32, 2, dtype=np.float32) / np.float32(32))).astype(np.float32)
    ang = np.concatenate([row[:, None] * inv, col[:, None] * inv], axis=-1).astype(np.float32)
    cos, sin = np.cos(ang).astype(np.float32), np.sin(ang).astype(np.float32)
    C = np.repeat(cos, 2, axis=1).T
    Sg = np.repeat(sin, 2, axis=1).T.copy()
    Sg[0::2, :] *= -1.0
    C = np.concatenate([C, C], axis=0)
    Sg = np.concatenate([Sg, Sg], axis=0)
    return np.ascontiguousarray(np.stack([C, Sg]).astype(np.float32))


def _consts():
    c = np.zeros((128, 897), np.float32)
    c[:, 896] = EPS
    c[:, 0:128] = 1.0
    c[0:64, 128:192] = 1.0
    c[64:128, 192:256] = 1.0
    for p in range(128):
        c[p, 256 + (p ^ 1)] = 1.0
    pidx = np.arange(128)[:, None]
    for a in range(2):
        cidx = np.arange(128)[None, :]
        delta = (-64 + 128 * a + pidx) - cidx
        c[:, 384 + a * 128:384 + (a + 1) * 128] = -np.abs(delta)
        c[:, 640 + a * 128:640 + (a + 1) * 128] = np.where(np.abs(delta) <= 64, 0.0, NEG)
    return c


def _ctab(rpb):
    qc = np.arange(64)[None, :]
    kc = np.arange(64)[:, None]
    cstart = np.clip(qc - 8, 0, 48)
    valid = (kc >= cstart) & (kc < cstart + 16)
    idx = np.clip(kc - qc + 15, 0, 30)
    t = rpb[:, :, idx]
    t = np.where(valid[None, None], t, np.float32(NEG)).astype(np.float32)
    t = t.transpose(0, 2, 1, 3)
    up = np.full_like(t, np.float32(NEG))
    up[:, :, 0:14] = t[:, :, 1:15]
    return np.ascontiguousarray(np.concatenate([t, up], axis=1).reshape(16, 128, 15 * 64))


_NC_CACHE = {}


def _prep(inputs):
    f = lambda a: np.ascontiguousarray(np.asarray(a, dtype=np.float32))
    xp, xs = f(inputs["x_prompt"]), f(inputs["x_sample"])
    mp, ms = f(inputs["mem_prompt"]), f(inputs["mem_sample"])
    gt = np.zeros((128, 74), np.float32)
    glist = [inputs["g_mix"][0], inputs["g_mix"][1], inputs["g_xattn"][0], inputs["g_xattn"][1],
             inputs["g_mem"][0], inputs["g_mem"][1], inputs["g_ffn"][0], inputs["g_ffn"][1], inputs["g_final"]]
    for i, g in enumerate(glist):
        gt[:, 8 * i:8 * i + 8] = f(g).reshape(8, 128).T
    gt[:, 72] = np.tile(f(inputs["g_qn"])[0], 2)
    gt[:, 73] = np.tile(f(inputs["g_kn"])[0], 2)
    shared = dict(
        gtab=gt, consts=_consts(), ctab=_ctab(f(inputs["rpb_c"])[0]),
        w_in_ab=f(inputs["w_in_ab"])[0], w_out_ab=f(inputs["w_out_ab"])[0], w_in_c=f(inputs["w_in_c"])[0],
        w_out_c=f(inputs["w_out_c"])[0], wq_x=f(inputs["wq_x"]), wkv_x=f(inputs["wkv_x"]), wo_x=f(inputs["wo_x"]),
        w_gu=f(inputs["w_gu"]), w_down=f(inputs["w_down"]),
        rope_p=_rope_tab(np.arange(2048)), ropek_s=_rope_tab(np.arange(16384)),
    )
    xsT = [np.ascontiguousarray(xs[b].T) for b in range(2)]
    in_maps = []
    sg = SEGS[2]
    for c in range(8):
        sbi, qd = c // 4, c % 4
        own0 = qd * 4096
        x0 = own0 - 1280
        pos = np.arange(x0, x0 + sg["X"])
        ok = (pos >= 0) & (pos < 16384)
        xT2 = np.zeros((D, sg["X"]), np.float32)
        xT2[:, ok] = xsT[sbi][:, pos[ok]]
        epos = pos[sg["E0"]:sg["E0"] + sg["E"]]
        epos = np.where((epos >= 0) & (epos < 16384), epos, -1)
        vb = np.zeros((128, 2048), np.float32)
        kid = 0
        for s in range(3):
            seg = SEGS[s]
            for (g, r, i0, nq, kts) in b_jobs(seg):
                d = B_DIL[g]
                for (ja, kp, p0) in kts:
                    if s == 2:
                        tok = seg["E0"] + (ja + np.arange(kp)) * d + r
                        vb[0:kp, kid] = np.where(ok[tok], 0.0, NEG)
                    kid += 1
        cm = np.zeros((128, 128), np.uint32)
        cm[:, 0:64] = 1 if qd == 0 else 0
        cm[:, 64:128] = 1 if qd == 3 else 0
        m = dict(shared)
        m.update(
            xT0=np.ascontiguousarray(xp[2 * c].T), xT1=np.ascontiguousarray(xp[2 * c + 1].T), xT2=xT2, xTa=xsT[sbi],
            memT=np.ascontiguousarray(np.stack([mp[2 * c].T, mp[2 * c + 1].T, ms[sbi].T])),
            ropeq_s=_rope_tab(epos), vbt=vb, cmask=cm,
        )
        in_maps.append(m)
    return in_maps


def kernel(**inputs):
    in_maps = _prep(inputs)
    if "nc" not in _NC_CACHE:
        _NC_CACHE["nc"] = build()
    nc = _NC_CACHE["nc"]
    res = run_bass_kernel_spmd(nc, in_maps, core_ids=list(range(8)))
    yp = np.zeros((16, 2048, D), np.float32)
    ys = np.zeros((2, 16384, D), np.float32)
    for c in range(8):
        r = res.results[c]
        yp[2 * c] = r["yT0"].T
        yp[2 * c + 1] = r["yT1"].T
        ys[c // 4, (c % 4) * 4096:(c % 4 + 1) * 4096] = r["yT2"].T
    return yp, ys
```

```python
import numpy as np
from contextlib import ExitStack
import concourse.bass as bass
import concourse.mybir as mybir
from concourse.bass_utils import run_bass_kernel_spmd

F32 = mybir.dt.float32
BF16 = mybir.dt.bfloat16
AF = mybir.ActivationFunctionType
ALU = mybir.AluOpType

D = 1024
EPS = 1e-6
NEG = -30000.0
DFF = 2816
SEGS = [
    dict(X=2048, E0=0, E=2048, O0=0, OWN=2048, NA=2048, prompt=True),
    dict(X=2048, E0=0, E=2048, O0=0, OWN=2048, NA=2048, prompt=True),
    dict(X=6656, E0=1024, E=4608, O0=256, OWN=4096, NA=16384, prompt=False),
]
B_DIL = (1, 4, 16)


def alibi_c(g, h):
    return float(2.0 ** (-8.0 * (4 * g + h + 1) / 12.0)) * B_DIL[g]


def b_jobs(seg):
    jobs = []
    for g, d in enumerate(B_DIL):
        L = seg["E"] // d
        H = seg["E0"] // d
        H2 = (seg["X"] - seg["E0"] - seg["E"]) // d
        for r in range(d):
            i0 = 0
            while i0 < L:
                nq = min(128, L - i0)
                kts = []
                for a in range(2):
                    jn = i0 - 64 + 128 * a
                    ja = max(jn, -H)
                    je = min(jn + 128, i0 + nq + 64, L + H2)
                    if je > ja:
                        kts.append((ja, je - ja, ja - jn))
                jobs.append((g, r, i0, nq, kts))
                i0 += nq
    return jobs


def c_jobs(seg):
    rows = seg["OWN"] // 64
    off = seg["O0"] // 64
    jobs = []
    if seg["prompt"]:
        for r in range(rows):
            rs = min(max(r - 4, 0), rows - 8)
            jobs.append((r, rs, rs - r + 7, 0))
    else:
        for r in range(rows):
            jobs.append((r, r + off - 4, 3, 0))
            if r < 4:
                jobs.append((r, off, 0 - r + 7, 1))
            if r >= rows - 3:
                jobs.append((r, off + rows - 8, rows - 8 - r + 7, 2))
    return jobs


class Buf:
    __slots__ = ("name", "last_w", "readers")

    def __init__(self, name):
        self.name = name
        self.last_w = None
        self.readers = {}


class TB:
    def __init__(self, t, name):
        self.t = t
        self.buf = Buf(name)

    def __getitem__(self, k):
        return self.t[k]


class Sched:
    def __init__(self, nc, es):
        self.nc = nc
        self.eng = {"pe": nc.tensor, "act": nc.scalar, "dve": nc.vector, "pool": nc.gpsimd, "sp": nc.sync}
        self.sem = {e: es.enter_context(nc.semaphore("s_" + e)) for e in ("pe", "act", "dve", "pool")}
        self.cnt = {e: 0 for e in self.sem}
        self.dq = {"sp": list(range(0, 14)), "pool": list(range(14, 20)), "act": []}
        self.dsem = [es.enter_context(nc.semaphore("d%d" % i)) for i in range(20)]
        self.dcnt = [0] * 20
        self.dnext = {"sp": 0, "pool": 0, "act": 0}
        self.waited = {}
        self.nins = 0

    def _s(self, k):
        return self.sem[k[1]] if k[0] == "e" else self.dsem[k[1]]

    def _deps(self, e, reads, writes):
        deps = {}

        def add(tok, war=False):
            if tok is None:
                return
            k, v, src = tok
            if src == e and (e == "pe" or war):
                return
            if deps.get(k, 0) < v:
                deps[k] = v
        for b in reads:
            add(b.last_w)
        for b in writes:
            add(b.last_w)
            for k, (v, src) in b.readers.items():
                add((k, v, src), True)
        eng = self.eng[e]
        for k, v in deps.items():
            if self.waited.get((e, k), 0) >= v:
                continue
            self.waited[(e, k)] = v
            eng.wait_ge(self._s(k), v)

    def _fin(self, tok, reads, writes):
        k, v, src = tok
        for b in reads:
            b.readers[k] = (v, src)
        for b in writes:
            b.last_w = tok
            b.readers = {}

    def op(self, e, fn, reads=(), writes=()):
        self._deps(e, reads, writes)
        ins = fn(self.eng[e])
        self.cnt[e] += 1
        ins.then_inc(self.sem[e], 1)
        self._fin((("e", e), self.cnt[e], e), reads, writes)
        self.nins += 1

    def dma(self, q, out, in_, reads=(), writes=()):
        self._deps(q, reads, writes)
        lst = self.dq[q]
        i = lst[self.dnext[q] % len(lst)]
        self.dnext[q] += 1
        if self.dcnt[i] > self.waited.get((q, ("d", i)), 0):
            self.waited[(q, ("d", i))] = self.dcnt[i]
            self.eng[q].wait_ge(self.dsem[i], self.dcnt[i])
        ins = self.eng[q].dma_start(out=out, in_=in_)
        self.dcnt[i] += 16
        ins.then_inc(self.dsem[i], 16)
        self._fin((("d", i), self.dcnt[i], "dma"), reads, writes)
        self.nins += 1

    def barrier(self):
        for e in self.eng:
            eng = self.eng[e]
            for o in self.sem:
                if o != e and self.cnt[o] > self.waited.get((e, ("e", o)), 0):
                    self.waited[(e, ("e", o))] = self.cnt[o]
                    eng.wait_ge(self.sem[o], self.cnt[o])
            for i in range(len(self.dsem)):
                if self.dcnt[i] > self.waited.get((e, ("d", i)), 0):
                    self.waited[(e, ("d", i))] = self.dcnt[i]
                    eng.wait_ge(self.dsem[i], self.dcnt[i])


def build(debug=None):
    nc = bass.Bass("TRN2", target_bir_lowering=False)

    def din(name, shape, dt=F32):
        return nc.dram_tensor(name, list(shape), dt, kind="ExternalInput").ap()

    def dscr(name, shape, dt=BF16):
        kind = "ExternalOutput" if (debug and name in debug) else "Internal"
        return nc.dram_tensor(name, list(shape), dt, kind=kind).ap()

    xT = [din("xT0", [D, 2048]), din("xT1", [D, 2048]), din("xT2", [D, 6656])]
    xTa = [xT[0], xT[1], din("xTa", [D, 16384])]
    memT = din("memT", [3, D, 256])
    ropeq = [din("rope_p", [2, 128, 2048]), None, din("ropeq_s", [2, 128, 4608])]
    ropeq[1] = ropeq[0]
    ropek = [ropeq[0], ropeq[0], din("ropek_s", [2, 128, 16384])]
    gtab = din("gtab", [128, 74])
    consts = din("consts", [128, 897])
    vbt_in = din("vbt", [128, 2048])
    cmask_in = din("cmask", [128, 128], mybir.dt.uint32)
    ctab_in = din("ctab", [16, 128, 15 * 64])
    w_in_ab = din("w_in_ab", [D, 3072])
    w_out_ab = din("w_out_ab", [768, D])
    w_in_c = din("w_in_c", [D, 3072])
    w_out_c = din("w_out_c", [D, D])
    wq_x = din("wq_x", [2, D, D])
    wkv_x = din("wkv_x", [2, D, 2 * D])
    wo_x = din("wo_x", [2, D, D])
    w_gu = din("w_gu", [2, D, 2 * DFF])
    w_down = din("w_down", [2, DFF, D])
    yT = [nc.dram_tensor("yT%d" % s, [D, SEGS[s]["OWN"]], F32, kind="ExternalOutput").ap() for s in range(3)]
    QA = [dscr("QA%d" % s, [4, 128, SEGS[s]["E"]]) for s in range(3)]
    KAd = [dscr("KAd%d" % s, [2, 128, SEGS[s]["NA"]]) for s in range(3)]
    VAa = [dscr("VAa%d" % s, [SEGS[s]["NA"], 512]) for s in range(3)]
    QB = [dscr("QB%d" % s, [6, 128, SEGS[s]["E"]]) for s in range(3)]
    KB = [dscr("KB%d" % s, [6, 128, SEGS[s]["X"]]) for s in range(3)]
    VB = [dscr("VB%d" % s, [SEGS[s]["X"], 768]) for s in range(3)]
    OAB = [dscr("OAB%d" % s, [6, 128, SEGS[s]["E"]]) for s in range(3)]
    X2 = [dscr("X2_%d" % s, [8, 128, SEGS[s]["E"]], F32) for s in range(3)]
    X3 = [dscr("X3_%d" % s, [8, 128, SEGS[s]["E"]], F32) for s in range(3)]
    QC = [dscr("QC%d" % s, [8, 128, SEGS[s]["E"]]) for s in range(3)]
    KC = [dscr("KC%d" % s, [8, 128, SEGS[s]["E"]]) for s in range(3)]
    VCa = [dscr("VCa%d" % s, [SEGS[s]["E"], 16 * 128]) for s in range(3)]
    OC = [dscr("OC%d" % s, [8, 128, SEGS[s]["OWN"]]) for s in range(3)]
    KM = dscr("KM", [3, 2, 8, 128, 256])
    VM = dscr("VM", [3, 2, 2, 128, 1024])

    with ExitStack() as es:
        S = Sched(nc, es)
        op, dma = S.op, S.dma

        uniq = [0]

        def sb(stack, name, shape, dt):
            uniq[0] += 1
            name = "%s_%d" % (name, uniq[0])
            return TB(stack.enter_context(nc.sbuf_tensor(name, list(shape), dt)), name)

        psall = es.enter_context(nc.psum_tensor("psall", [128, 4096], F32))

        class PSV:
            def __init__(self, i):
                self.off = 512 * i
                self.buf = Buf("ps%d" % i)

            def __getitem__(self, k):
                if isinstance(k, tuple):
                    p, c = k
                else:
                    p, c = k, slice(None)
                a = 0 if c.start is None else c.start
                b = 512 if c.stop is None else c.stop
                assert c.step is None
                return psall[p, self.off + a:self.off + b]
        ps = [PSV(i) for i in range(8)]
        cst_f = sb(es, "cst_f", [128, 897], F32)
        cst_b = sb(es, "cst_b", [128, 384], BF16)
        g_sb = sb(es, "g_sb", [128, 74], F32)
        dma("sp", cst_f[:], consts, writes=[cst_f.buf])
        dma("sp", g_sb[:], gtab, writes=[g_sb.buf])
        op("dve", lambda e: e.tensor_copy(out=cst_b[:], in_=cst_f[:, 0:384]), [cst_f.buf], [cst_b.buf])
        ones_b = cst_b[:, 0:128]
        bd_b = cst_b[:, 128:256]
        pswap_f = cst_f[:, 256:384]
        btabA = cst_f[:, 384:640]
        btabM = cst_f[:, 640:896]
        eps_col = cst_f[:, 896:897]
        CB = [cst_b.buf]

        def gcol(i):
            return g_sb[:, 8 * i:8 * i + 8]

        def mm(pt, lhsT, rhs, start, stop, reads, out=None):
            o = pt[:] if out is None else out
            op("pe", lambda e: e.matmul(o, lhsT=lhsT, rhs=rhs, start=start, stop=stop), reads, [pt.buf])

        def rmsnorm(x_t, N, gi, h_t, sq_t, rs_t, pss):
            op("act", lambda e: e.activation(out=sq_t[:, :, 0:N], in_=x_t[:, :, 0:N], func=AF.Square),
               [x_t.buf], [sq_t.buf])
            for c in range(8):
                mm(pss, ones_b, sq_t[:, c, 0:N], c == 0, c == 7, [sq_t.buf] + CB, out=pss[:, 0:N])
            op("act", lambda e: e.activation(out=rs_t[:, 0:N], in_=pss[:, 0:N], func=AF.Ln, scale=1.0 / D,
                                             bias=eps_col), [pss.buf, cst_f.buf], [rs_t.buf])
            op("act", lambda e: e.activation(out=rs_t[:, 0:N], in_=rs_t[:, 0:N], func=AF.Exp, scale=-0.5),
               [rs_t.buf], [rs_t.buf])
            g = gcol(gi)
            for c in range(8):
                eng = "dve"
                op(eng, lambda e, c=c: e.scalar_tensor_tensor(out=h_t[:, c, 0:N], in0=x_t[:, c, 0:N],
                                                              scalar=g[:, c:c + 1], in1=rs_t[:, 0:N],
                                                              op0=ALU.mult, op1=ALU.mult),
                   [x_t.buf, rs_t.buf, g_sb.buf], [h_t.buf])

        def load_w(stack, name, src, kch, ncol, c0=0, c1=None, bw=None, order=None):
            c1 = ncol if c1 is None else c1
            w = sb(stack, name, [128, kch, c1 - c0], BF16)
            v = src.rearrange("(k p) n -> p k n", p=128)
            if bw is None:
                for k in range(kch):
                    dma("pool", w[:, k:k + 1, :], v[:, k:k + 1, c0:c1], writes=[w.buf])
            else:
                nb = (c1 - c0) // bw
                w.bw = bw
                w.bufs = [Buf("%s_b%d" % (name, i)) for i in range(nb)]
                for b_ in (order if order is not None else range(nb)):
                    dma("pool", w[:, :, b_ * bw:(b_ + 1) * bw], v[:, :, c0 + b_ * bw:c0 + (b_ + 1) * bw],
                        writes=[w.bufs[b_]])
            return w

        def wb(w, a, b_):
            return w.bufs[a // w.bw:(b_ - 1) // w.bw + 1]

        with ExitStack() as ph:
            xm = sb(ph, "xm", [128, 8, 256], F32)
            sq = sb(ph, "sq", [128, 8, 256], BF16)
            rs = sb(ph, "rs", [128, 256], F32)
            hm = sb(ph, "hm", [128, 8, 256], BF16)
            ko = [sb(ph, "ko%d" % i, [128, 256], BF16) for i in range(2)]
            vo = [sb(ph, "vo%d" % i, [128, 512], BF16) for i in range(2)]
            for l in range(2):
                with ExitStack() as ph2:
                    wkv = load_w(ph2, "wkv", wkv_x[l], 8, 2048)
                    for s in range(3):
                        dma("sp", xm[:], memT[s].rearrange("(c p) t -> p c t", p=128), writes=[xm.buf])
                        rmsnorm(xm, 256, 4 + l, hm, sq, rs, ps[0])
                        for oc in range(8):
                            pt = ps[1 + oc % 2]
                            for kc in range(8):
                                mm(pt, wkv[:, kc, oc * 128:(oc + 1) * 128], hm[:, kc, :], kc == 0, kc == 7,
                                   [wkv.buf, hm.buf], out=pt[:, 0:256])
                            o = ko[oc % 2]
                            op("act", lambda e, o=o, pt=pt: e.copy(out=o[:], in_=pt[:, 0:256]), [pt.buf], [o.buf])
                            dma("sp", KM[s, l, oc], o[:], reads=[o.buf])
                        for mt in range(2):
                            for hf in range(2):
                                pt = ps[3 + hf]
                                for kc in range(8):
                                    mm(pt, hm[:, kc, mt * 128:(mt + 1) * 128],
                                       wkv[:, kc, 1024 + hf * 512:1024 + (hf + 1) * 512], kc == 0, kc == 7,
                                       [wkv.buf, hm.buf])
                                o = vo[hf]
                                op("dve", lambda e, o=o, pt=pt: e.tensor_copy(out=o[:], in_=pt[:]), [pt.buf], [o.buf])
                                dma("sp", VM[s, l, mt][:, hf * 512:(hf + 1) * 512], o[:], reads=[o.buf])
                    S.barrier()
        S.barrier()

        def qknorm_rope(pz, gq_col, ctab, stab, tabbufs, out_t, tmp, pa, pb):
            zsq, rs_, qn, t1, t2 = tmp
            op("act", lambda e: e.activation(out=zsq[:], in_=pz[:], func=AF.Square), [pz.buf], [zsq.buf])
            mm(pa, bd_b, zsq[:], True, True, [zsq.buf] + CB)
            op("act", lambda e: e.activation(out=rs_[:], in_=pa[:], func=AF.Ln, scale=1.0 / 64, bias=eps_col),
               [pa.buf, cst_f.buf], [rs_.buf])
            op("act", lambda e: e.activation(out=rs_[:], in_=rs_[:], func=AF.Exp, scale=-0.5), [rs_.buf], [rs_.buf])
            op("dve", lambda e: e.scalar_tensor_tensor(out=qn[:], in0=pz[:], scalar=gq_col, in1=rs_[:],
                                                       op0=ALU.mult, op1=ALU.mult),
               [pz.buf, rs_.buf, g_sb.buf], [qn.buf])
            mm(pb, pswap_f, qn[:], True, True, [qn.buf, cst_f.buf])
            op("pool", lambda e: e.tensor_tensor(out=t1[:], in0=qn[:], in1=ctab, op=ALU.mult),
               [qn.buf] + tabbufs, [t1.buf])
            op("dve", lambda e: e.tensor_tensor(out=t2[:], in0=pb[:], in1=stab, op=ALU.mult),
               [pb.buf] + tabbufs, [t2.buf])
            op("dve", lambda e: e.tensor_tensor(out=out_t[:], in0=t1[:], in1=t2[:], op=ALU.add),
               [t1.buf, t2.buf], [out_t.buf])

        def qknorm_group(chunks, pas, pbs, tmpl):
            n = len(chunks)
            for i in range(n):
                pz = chunks[i][0]
                zsq = tmpl[i][0]
                op("act", lambda e, zsq=zsq, pz=pz: e.activation(out=zsq[:], in_=pz[:], func=AF.Square), [pz.buf], [zsq.buf])
            for i in range(n):
                mm(pas[i], bd_b, tmpl[i][0][:], True, True, [tmpl[i][0].buf] + CB)
            for i in range(n):
                rs_, pa = tmpl[i][1], pas[i]
                op("act", lambda e, rs_=rs_, pa=pa: e.activation(out=rs_[:], in_=pa[:], func=AF.Ln, scale=1.0 / 64,
                                                                 bias=eps_col), [pa.buf, cst_f.buf], [rs_.buf])
            for i in range(n):
                rs_ = tmpl[i][1]
                op("act", lambda e, rs_=rs_: e.activation(out=rs_[:], in_=rs_[:], func=AF.Exp, scale=-0.5),
                   [rs_.buf], [rs_.buf])
            for i in range(n):
                pz, gq = chunks[i][0], chunks[i][1]
                rs_, qn = tmpl[i][1], tmpl[i][2]
                op("dve", lambda e, pz=pz, gq=gq, rs_=rs_, qn=qn: e.scalar_tensor_tensor(
                    out=qn[:], in0=pz[:], scalar=gq, in1=rs_[:], op0=ALU.mult, op1=ALU.mult),
                   [pz.buf, rs_.buf, g_sb.buf], [qn.buf])
            for i in range(n):
                mm(pbs[i], pswap_f, tmpl[i][2][:], True, True, [tmpl[i][2].buf, cst_f.buf])
            for i in range(n):
                qn, t1 = tmpl[i][2], tmpl[i][3]
                ctab, tb_ = chunks[i][2], chunks[i][4]
                op("pool", lambda e, qn=qn, t1=t1, ctab=ctab: e.tensor_tensor(out=t1[:], in0=qn[:], in1=ctab, op=ALU.mult),
                   [qn.buf] + tb_, [t1.buf])
            for i in range(n):
                t2, pb = tmpl[i][4], pbs[i]
                stab, tb_ = chunks[i][3], chunks[i][4]
                op("dve", lambda e, t2=t2, pb=pb, stab=stab: e.tensor_tensor(out=t2[:], in0=pb[:], in1=stab, op=ALU.mult),
                   [pb.buf] + tb_, [t2.buf])
            for i in range(n):
                t1, t2, out_t = tmpl[i][3], tmpl[i][4], chunks[i][5]
                op("dve", lambda e, t1=t1, t2=t2, out_t=out_t: e.tensor_tensor(out=out_t[:], in0=t1[:], in1=t2[:], op=ALU.add),
                   [t1.buf, t2.buf], [out_t.buf])

        with ExitStack() as ph:
            wkd = sb(ph, "wkd", [128, 8, 256], BF16)
            wv_ = w_in_ab.rearrange("(k p) n -> p k n", p=128)
            for j in range(4):
                dma("pool", wkd[:, :, j * 64:(j + 1) * 64], wv_[:, :, 512 + (j // 2) * 64:512 + (j // 2) * 64 + 64],
                    writes=[wkd.buf])
            w_in = load_w(ph, "w_in", w_in_ab, 8, 3072, bw=128, order=[5, 0, 1, 2, 3, 4] + list(range(6, 24)))
            xt = [sb(ph, "xt%d" % i, [128, 8, 512], F32) for i in range(2)]
            sq = sb(ph, "sq", [128, 8, 512], BF16)
            rs = sb(ph, "rs", [128, 512], F32)
            ht = [sb(ph, "ht%d" % i, [128, 8, 512], BF16) for i in range(2)]
            tmp = [sb(ph, "zsq", [128, 512], BF16), sb(ph, "rs2", [128, 512], F32), sb(ph, "qn", [128, 512], F32),
                   sb(ph, "t1", [128, 512], F32), sb(ph, "t2", [128, 512], F32)]
            rtab = [sb(ph, "rtab%d" % i, [128, 2, 512], F32) for i in range(2)]
            ob = [sb(ph, "ob%d" % i, [128, 512], BF16) for i in range(4)]
            vaug = [sb(ph, "vaug%d" % i, [128, 4, 4, 128], BF16) for i in range(2)]
            vbt = [sb(ph, "vbo%d" % i, [128, 4, 768], BF16) for i in range(2)]
            for v in vaug:
                op("pool", lambda e, v=v: e.memset(v[:], 1.0), [], [v.buf])
            nob = [0]

            def next_ob():
                nob[0] += 1
                return ob[nob[0] % 4]
            tmp2 = [sb(ph, "zsq", [128, 512], BF16), sb(ph, "rs2", [128, 512], F32), sb(ph, "qn", [128, 512], F32),
                    sb(ph, "t1", [128, 512], F32), sb(ph, "t2", [128, 512], F32)]
            tmps = [tmp, tmp2]
            nqk = [0]

            def next_tmp():
                nqk[0] += 1
                return tmps[nqk[0] % 2]
            tiles = []
            for s in range(3):
                for ti in range(SEGS[s]["NA"] // 512):
                    tiles.append(("a", s, ti))
            for s in range(3):
                for ti in range(SEGS[s]["X"] // 512):
                    tiles.append(("b", s, ti))

            def prep_load(n):
                if n >= len(tiles):
                    return
                kind, s, ti = tiles[n]
                sg = SEGS[s]
                t0 = ti * 512
                x_t, rt = xt[n % 2], rtab[n % 2]
                src = (xTa if kind == "a" else xT)[s].rearrange("(c p) t -> p c t", p=128)
                dma("sp", x_t[:], src[:, :, t0:t0 + 512], writes=[x_t.buf])
                if kind == "a":
                    dma("sp", rt[:], ropek[s][:, :, t0:t0 + 512].rearrange("a p t -> p a t"), writes=[rt.buf])
                else:
                    e0 = t0 - sg["E0"]
                    if 0 <= e0 < sg["E"]:
                        dma("sp", rt[:], ropeq[s][:, :, e0:e0 + 512].rearrange("a p t -> p a t"), writes=[rt.buf])

            def prep_norm(n):
                if n >= len(tiles):
                    return
                rmsnorm(xt[n % 2], 512, 0, ht[n % 2], sq, rs, ps[0])
            prep_load(0)
            prep_norm(0)
            for n, (kind, s, ti) in enumerate(tiles):
                prep_load(n + 1)
                sg = SEGS[s]
                t0 = ti * 512
                h_t, rt = ht[n % 2], rtab[n % 2]
                if kind == "a":
                    va = vaug[n % 2]
                    chunks = []
                    for j in range(2):
                        pz = ps[1 + j]
                        for kc in range(8):
                            mm(pz, wkd[:, kc, j * 128:(j + 1) * 128], h_t[:, kc, :], kc == 0, kc == 7,
                               [wkd.buf, h_t.buf])
                        chunks.append((pz, g_sb[:, 73:74], rt[:, 0, :], rt[:, 1, :], [rt.buf], next_ob()))
                    prep_norm(n + 1)
                    qknorm_group(chunks, [ps[3], ps[4]], [ps[5], ps[6]], tmps)
                    for j in range(2):
                        o = chunks[j][5]
                        dma("sp", KAd[s][j][:, t0:t0 + 512], o[:], reads=[o.buf])
                    for sub in range(4):
                        pv = ps[7] if sub % 2 == 0 else ps[0]
                        for kc in range(8):
                            mm(pv, h_t[:, kc, sub * 128:(sub + 1) * 128], w_in[:, kc, 640:768], kc == 0, kc == 7,
                               wb(w_in, 640, 768) + [h_t.buf], out=pv[:, 0:128])
                        for kv in range(2):
                            op("act", lambda e, pv=pv, kv=kv, sub=sub, va=va: e.copy(
                                out=va[:, sub, 2 * kv, 0:64], in_=pv[:, kv * 64:kv * 64 + 64]), [pv.buf], [va.buf])
                            op("dve", lambda e, pv=pv, kv=kv, sub=sub, va=va: e.tensor_copy(
                                out=va[:, sub, 2 * kv + 1, 64:128], in_=pv[:, kv * 64:kv * 64 + 64]), [pv.buf], [va.buf])
                    dma("sp", VAa[s][t0:t0 + 512, :].rearrange("(s p) c -> p s c", p=128),
                        va[:].rearrange("p s a c -> p s (a c)"), reads=[va.buf])
                else:
                    e0 = t0 - sg["E0"]
                    inE = 0 <= e0 < sg["E"]
                    vb = vbt[n % 2]
                    if inE:
                        for jg in range(2):
                            chunks = []
                            for jj in range(2):
                                j = 2 * jg + jj
                                pz = ps[1 + jj]
                                for kc in range(8):
                                    mm(pz, w_in[:, kc, j * 128:(j + 1) * 128], h_t[:, kc, :], kc == 0, kc == 7,
                                       wb(w_in, j * 128, (j + 1) * 128) + [h_t.buf])
                                chunks.append((pz, g_sb[:, 72:73], rt[:, 0, :], rt[:, 1, :], [rt.buf], next_ob()))
                            qknorm_group(chunks, [ps[3], ps[4]], [ps[5], ps[6]], tmps)
                            for jj in range(2):
                                o = chunks[jj][5]
                                dma("sp", QA[s][2 * jg + jj][:, e0:e0 + 512], o[:], reads=[o.buf])
                    for j in range(12):
                        isq = j < 6
                        if isq and not inE:
                            continue
                        pz = ps[1 + j % 2]
                        c0 = 768 + j * 128
                        for kc in range(8):
                            mm(pz, w_in[:, kc, c0:c0 + 128], h_t[:, kc, :], kc == 0, kc == 7, wb(w_in, c0, c0 + 128) + [h_t.buf])
                        o = next_ob()
                        if j % 2 == 0:
                            op("act", lambda e, o=o, pz=pz: e.copy(out=o[:], in_=pz[:]), [pz.buf], [o.buf])
                        else:
                            op("dve", lambda e, o=o, pz=pz: e.tensor_copy(out=o[:], in_=pz[:]), [pz.buf], [o.buf])
                        if isq:
                            dma("sp", QB[s][j][:, e0:e0 + 512], o[:], reads=[o.buf])
                        else:
                            dma("sp", KB[s][j - 6][:, t0:t0 + 512], o[:], reads=[o.buf])
                    prep_norm(n + 1)
                    for sub in range(4):
                        for hf in range(2):
                            pv = ps[5 + hf]
                            c0 = 768 + 1536 + hf * 384
                            for kc in range(8):
                                mm(pv, h_t[:, kc, sub * 128:(sub + 1) * 128], w_in[:, kc, c0:c0 + 384], kc == 0, kc == 7,
                                   wb(w_in, c0, c0 + 384) + [h_t.buf], out=pv[:, 0:384])
                            if hf == 0:
                                op("act", lambda e, pv=pv, sub=sub, vb=vb: e.copy(
                                    out=vb[:, sub, 0:384], in_=pv[:, 0:384]), [pv.buf], [vb.buf])
                            else:
                                op("dve", lambda e, pv=pv, sub=sub, vb=vb: e.tensor_copy(
                                    out=vb[:, sub, 384:768], in_=pv[:, 0:384]), [pv.buf], [vb.buf])
                    dma("sp", VB[s][t0:t0 + 512, :].rearrange("(s p) c -> p s c", p=128), vb[:], reads=[vb.buf])
        S.barrier()

        for s in range(3):
            sg = SEGS[s]
            NA, E = sg["NA"], sg["E"]
            nkt = NA // 128
            for kv in range(2):
                with ExitStack() as ph:
                    kt = sb(ph, "kt", [128, NA], BF16)
                    vt = sb(ph, "vt", [128, nkt, 256], BF16)
                    qz = [[sb(ph, "qz%d_%d" % (i, j), [128, 512], BF16) for j in range(2)] for i in range(3)]
                    pt_ = [sb(ph, "pt%d" % i, [128, 1024], BF16) for i in range(3)]
                    rd = sb(ph, "rd", [128, 512], F32)
                    oo = [sb(ph, "oo%d" % i, [128, 512], BF16) for i in range(2)]
                    ktb = [Buf("ktb%d" % i) for i in range(NA // 2048)]
                    vtb = [Buf("vtb%d" % i) for i in range(nkt // 16)]
                    for ci in range(NA // 2048):
                        c = ci * 2048
                        dma("sp", kt[:, c:c + 2048], KAd[s][kv][:, c:c + 2048], writes=[ktb[ci]])
                        c = ci * 16
                        dma("sp", vt[:, c:c + 16, :],
                            VAa[s][c * 128:(c + 16) * 128, kv * 256:(kv + 1) * 256].rearrange("(t p) c -> p t c", p=128),
                            writes=[vtb[ci]])
                    nqt = E // 512
                    qorder = [(hp, qi) for hp in range(2) for qi in range(nqt)]
                    nk2 = nkt // 2
                    items = [(n, par, kk) for n in range(len(qorder)) for par in range(2) for kk in range(nk2)]
                    for qq in qz:
                        for q1 in qq:
                            op("pool", lambda e, q1=q1: e.memset(q1[:], 0.0), [], [q1.buf])

                    def load_q(n):
                        if n < len(qorder):
                            hp, qi = qorder[n]
                            for par in range(2):
                                q1 = qz[n % 3][par]
                                dma("sp", q1[par * 64:par * 64 + 64, :],
                                    QA[s][kv * 2 + hp][par * 64:par * 64 + 64, qi * 512:(qi + 1) * 512], writes=[q1.buf])

                    def emit_S(i):
                        n, par, kk = items[i]
                        if par == 0 and kk == 0:
                            load_q(n + 1)
                        q1 = qz[n % 3][par]
                        for a_ in range(2):
                            k = 2 * kk + a_
                            pS = ps[2 * (i % 3) + a_]
                            mm(pS, kt[:, k * 128:(k + 1) * 128], q1[:], True, True, [ktb[k // 16], q1.buf])
                    LA = 2
                    load_q(0)
                    for i in range(min(LA, len(items))):
                        emit_S(i)
                    for i, (n, par, kk) in enumerate(items):
                        if i + LA < len(items):
                            emit_S(i + LA)
                        hp, qi = qorder[n]
                        ch = kv * 2 + hp
                        b0 = par * 64
                        d0 = 64 - b0
                        j3 = i % 3
                        p_t = pt_[j3]
                        po = ps[6 + (2 * n + par) % 2]
                        o_t = oo[n % 2]
                        op("act", lambda e, j3=j3, p_t=p_t: e.activation(
                            out=p_t[:], in_=psall[:, 1024 * j3:1024 * j3 + 1024], func=AF.Exp, scale=0.125),
                           [ps[2 * j3].buf, ps[2 * j3 + 1].buf], [p_t.buf])
                        for a_ in range(2):
                            k = 2 * kk + a_
                            mm(po, vt[:, k, par * 128:(par + 1) * 128], p_t[:, a_ * 512:(a_ + 1) * 512], k == 0, k == nkt - 1,
                               [vtb[k // 16], p_t.buf])
                        if kk == nk2 - 1:
                            op("dve", lambda e, po=po, d0=d0: e.reciprocal(out=rd[d0:d0 + 64, :], in_=po[d0:d0 + 64, :]),
                               [po.buf], [rd.buf])
                            op("dve", lambda e, po=po, d0=d0, b0=b0, o_t=o_t: e.tensor_tensor(
                                out=o_t[b0:b0 + 64, :], in0=po[b0:b0 + 64, :], in1=rd[d0:d0 + 64, :], op=ALU.mult),
                               [po.buf, rd.buf], [o_t.buf])
                            if par == 1:
                                dma("sp", OAB[s][ch][:, qi * 512:(qi + 1) * 512], o_t[:], reads=[o_t.buf])
                S.barrier()

        vb_sb = sb(es, "vb_sb", [128, 2048], F32)
        dma("sp", vb_sb[:], vbt_in, writes=[vb_sb.buf])
        ktile_no = 0
        for s in range(3):
            sg = SEGS[s]
            E, X, E0 = sg["E"], sg["X"], sg["E0"]
            jobs = b_jobs(sg)
            with ExitStack() as ph:
                accO = [sb(ph, "accO%d" % i, [128, E], F32) for i in range(2)]
                accD = [sb(ph, "accD%d" % i, [128, E], F32) for i in range(2)]
                qb = [sb(ph, "qb%d" % i, [128, E], BF16) for i in range(2)]
                kb = [sb(ph, "kb%d" % i, [128, X], BF16) for i in range(2)]
                tab = [sb(ph, "tab%d" % i, [128, 256], F32) for i in range(4)]
                vk = [sb(ph, "vk%d" % i, [128, 2, 256], BF16) for i in range(3)]
                sc = [sb(ph, "sc%d" % i, [128, 256], F32) for i in range(2)]
                pp = [sb(ph, "pp%d" % i, [128, 256], BF16) for i in range(3)]
                ofin = sb(ph, "ofin", [128, E], BF16)
                nj0 = [0]
                kt_state = [ktile_no]
                for g, d in enumerate(B_DIL):
                    for pr in range(2):
                        dma("sp", qb[pr][:], QB[s][g * 2 + pr], writes=[qb[pr].buf])
                        dma("sp", kb[pr][:], KB[s][g * 2 + pr], writes=[kb[pr].buf])
                    for h in range(4):
                        c = alibi_c(g, h)
                        op("dve", lambda e, h=h, c=c: e.scalar_tensor_tensor(
                            out=tab[h][:], in0=btabA, scalar=c, in1=btabM, op0=ALU.mult, op1=ALU.add),
                           [cst_f.buf], [tab[h].buf])
                    gjobs = [j for j in jobs if j[0] == g]
                    items = [(ji, h) for ji in range(len(gjobs)) for h in range(4)]
                    jinfo = {}

                    def load_v(ji):
                        nonlocal_k = kt_state
                        jg, r, i0, nq, kts = gjobs[ji]
                        v_t = vk[(nj0[0] + ji) % 3]
                        ids = []
                        for a_, (ja, kp, p0) in enumerate(kts):
                            tok0 = E0 + ja * d + r
                            src = VB[s][tok0:tok0 + (kp - 1) * d + 1:d, g * 256:(g + 1) * 256]
                            dma("sp", v_t[0:kp, a_, :], src, writes=[v_t.buf])
                            ids.append(nonlocal_k[0])
                            nonlocal_k[0] += 1
                        jinfo[ji] = (v_t, ids)

                    def emit_S(i):
                        ji, h = items[i]
                        if h == 0:
                            load_v(ji)
                        jg, r, i0, nq, kts = gjobs[ji]
                        pr, b0 = h // 2, (h % 2) * 64
                        qs = slice(i0 * d + r, (i0 + nq - 1) * d + r + 1, d)
                        pS = ps[i % 3]
                        for a_, (ja, kp, p0) in enumerate(kts):
                            t0k = E0 + ja * d + r
                            ks = slice(t0k, t0k + (kp - 1) * d + 1, d)
                            mm(pS, kb[pr][b0:b0 + 64, ks], qb[pr][b0:b0 + 64, qs], True, True,
                               [kb[pr].buf, qb[pr].buf], out=pS[0:kp, a_ * 128:a_ * 128 + nq])

                    def emit_bias(i):
                        ji, h = items[i]
                        jg, r, i0, nq, kts = gjobs[ji]
                        pS, s_t = ps[i % 3], sc[i % 2]
                        for a_, (ja, kp, p0) in enumerate(kts):
                            op("dve", lambda e, a_=a_, kp=kp, p0=p0: e.scalar_tensor_tensor(
                                out=s_t[0:kp, a_ * 128:a_ * 128 + nq], in0=pS[0:kp, a_ * 128:a_ * 128 + nq], scalar=0.125,
                                in1=tab[h][p0:p0 + kp, a_ * 128:a_ * 128 + nq], op0=ALU.mult, op1=ALU.add),
                               [pS.buf, tab[h].buf], [s_t.buf])
                    emit_S(0)
                    if len(items) > 1:
                        emit_S(1)
                    emit_bias(0)
                    for i, (ji, h) in enumerate(items):
                        if i + 2 < len(items):
                            emit_S(i + 2)
                        if i + 1 < len(items):
                            emit_bias(i + 1)
                        jg, r, i0, nq, kts = gjobs[ji]
                        v_t, kt_ids = jinfo[ji]
                        pr, par = h // 2, h % 2
                        b0 = par * 64
                        qs = slice(i0 * d + r, (i0 + nq - 1) * d + r + 1, d)
                        pO, pD = ps[3 + 2 * (i % 2)], ps[4 + 2 * (i % 2)]
                        s_t, p_t = sc[i % 2], pp[i % 3]
                        na = len(kts)
                        for a_, (ja, kp, p0) in enumerate(kts):
                            kid = kt_ids[a_]
                            op("act", lambda e, a_=a_, kp=kp, kid=kid: e.activation(
                                out=p_t[0:kp, a_ * 128:a_ * 128 + nq], in_=s_t[0:kp, a_ * 128:a_ * 128 + nq], func=AF.Exp,
                                bias=vb_sb[0:kp, kid:kid + 1]), [s_t.buf, vb_sb.buf], [p_t.buf])
                        for a_, (ja, kp, p0) in enumerate(kts):
                            mm(pO, v_t[0:kp, a_, pr * 128:(pr + 1) * 128], p_t[0:kp, a_ * 128:a_ * 128 + nq],
                               a_ == 0, a_ == na - 1, [v_t.buf, p_t.buf], out=pO[:, 0:nq])
                        for a_, (ja, kp, p0) in enumerate(kts):
                            mm(pD, cst_b[0:kp, 0:128], p_t[0:kp, a_ * 128:a_ * 128 + nq],
                               a_ == 0, a_ == na - 1, [p_t.buf] + CB, out=pD[:, 0:nq])
                        for (pX, acc) in ((pO, accO[pr]), (pD, accD[pr])):
                            if g == 0:
                                op("dve", lambda e, pX=pX, acc=acc: e.tensor_copy(
                                    out=acc[b0:b0 + 64, qs], in_=pX[b0:b0 + 64, 0:nq]), [pX.buf], [acc.buf])
                            else:
                                op("dve", lambda e, pX=pX, acc=acc: e.tensor_tensor(
                                    out=acc[b0:b0 + 64, qs], in0=pX[b0:b0 + 64, 0:nq], in1=acc[b0:b0 + 64, qs],
                                    op=ALU.add), [pX.buf, acc.buf], [acc.buf])
                    nj0[0] += len(gjobs)
                ktile_no = kt_state[0]
                assert ktile_no <= 2048
                for pr in range(2):
                    op("dve", lambda e, pr=pr: e.tensor_scalar_max(out=accD[pr][:], in0=accD[pr][:], scalar1=1e-30),
                       [accD[pr].buf], [accD[pr].buf])
                    op("dve", lambda e, pr=pr: e.reciprocal(out=accD[pr][:], in_=accD[pr][:]), [accD[pr].buf], [accD[pr].buf])
                    op("dve", lambda e, pr=pr: e.tensor_tensor(out=ofin[:], in0=accO[pr][:], in1=accD[pr][:], op=ALU.mult),
                       [accO[pr].buf, accD[pr].buf], [ofin.buf])
                    dma("sp", OAB[s][4 + pr], ofin[:], reads=[ofin.buf])
            S.barrier()

        def post_mixer(l, src_x, src_x_off, src_o, nchunk_o, w_out_dram, dst_x2):
            with ExitStack() as ph:
                wo_ = load_w(ph, "w_o", w_out_dram, nchunk_o, D)
                wq = load_w(ph, "wq", wq_x[l], 8, D)
                wo = load_w(ph, "wo", wo_x[l], 8, D)
                km = [sb(ph, "km", [128, 8, 256], BF16) for i in range(2)]
                vm = [sb(ph, "vm", [128, 2, 1024], BF16) for i in range(2)]
                xt = [sb(ph, "xt%d" % i, [128, 8, 512], F32) for i in range(2)]
                ot = [sb(ph, "ot%d" % i, [128, 8, 512], BF16) for i in range(2)]
                sq = sb(ph, "sq", [128, 8, 512], BF16)
                rs = sb(ph, "rs", [128, 512], F32)
                hx = [sb(ph, "hx%d" % i, [128, 8, 512], BF16) for i in range(2)]
                qx = sb(ph, "qx", [128, 8, 512], BF16)
                px = [sb(ph, "px%d" % i, [128, 2, 512], BF16) for i in range(2)]
                rdx = sb(ph, "rdx", [128, 512], F32)
                ox = [sb(ph, "ox%d" % i, [128, 8, 512], BF16) for i in range(2)]
                tiles = [(s, ti) for s in range(3) for ti in range(dst_x2[s].shape[2] // 512)]

                def S0(n):
                    if n >= len(tiles):
                        return
                    s, ti = tiles[n]
                    t0 = ti * 512
                    x_t, o_t = xt[n % 2], ot[n % 2]
                    if ti == 0:
                        dma("sp", km[s % 2][:], KM[s, l].rearrange("c p m -> p c m"), writes=[km[s % 2].buf])
                        dma("sp", vm[s % 2][:], VM[s, l].rearrange("t p c -> p t c"), writes=[vm[s % 2].buf])
                    xo = src_x_off[s]
                    dma("sp", x_t[:], src_x[s][:, :, xo + t0:xo + t0 + 512].rearrange("c p t -> p c t")
                        if len(src_x[s].shape) == 3 else
                        src_x[s].rearrange("(c p) t -> p c t", p=128)[:, :, xo + t0:xo + t0 + 512], writes=[x_t.buf])
                    dma("sp", o_t[:, 0:nchunk_o, :], src_o[s][:, :, t0:t0 + 512].rearrange("c p t -> p c t"),
                        writes=[o_t.buf])

                def S1(n):
                    if n >= len(tiles):
                        return
                    x_t, o_t = xt[n % 2], ot[n % 2]
                    for oc in range(8):
                        pz = ps[oc % 2]
                        for kc in range(nchunk_o):
                            mm(pz, wo_[:, kc, oc * 128:(oc + 1) * 128], o_t[:, kc, :], kc == 0, kc == nchunk_o - 1,
                               [wo_.buf, o_t.buf])
                        op("dve", lambda e, pz=pz, oc=oc: e.tensor_tensor(
                            out=x_t[:, oc, :], in0=pz[:], in1=x_t[:, oc, :], op=ALU.add), [pz.buf, x_t.buf], [x_t.buf])
                    rmsnorm(x_t, 512, 2 + l, hx[n % 2], sq, rs, ps[2])

                def S2a(n):
                    h_x = hx[n % 2]
                    for oc in range(8):
                        pz = ps[oc % 2]
                        for kc in range(8):
                            mm(pz, wq[:, kc, oc * 128:(oc + 1) * 128], h_x[:, kc, :], kc == 0, kc == 7, [wq.buf, h_x.buf])
                        if oc % 2 == 0:
                            op("act", lambda e, pz=pz, oc=oc: e.copy(out=qx[:, oc, :], in_=pz[:]), [pz.buf], [qx.buf])
                        else:
                            op("dve", lambda e, pz=pz, oc=oc: e.tensor_copy(out=qx[:, oc, :], in_=pz[:]), [pz.buf], [qx.buf])

                def S2(n):
                    s, ti = tiles[n]
                    o_x = ox[n % 2]
                    k_m, v_m = km[s % 2], vm[s % 2]
                    for hd in range(4):
                        p_x = px[hd % 2]
                        for mt in range(2):
                            pS = ps[3 + mt]
                            for dc in range(2):
                                mm(pS, k_m[:, 2 * hd + dc, mt * 128:(mt + 1) * 128], qx[:, 2 * hd + dc, :], dc == 0, dc == 1,
                                   [k_m.buf, qx.buf])
                            op("act", lambda e, pS=pS, mt=mt, p_x=p_x: e.activation(
                                out=p_x[:, mt, :], in_=pS[:], func=AF.Exp, scale=1.0 / 16), [pS.buf], [p_x.buf])
                        pD = ps[5]
                        for mt in range(2):
                            mm(pD, ones_b, p_x[:, mt, :], mt == 0, mt == 1, [p_x.buf] + CB)
                        op("act", lambda e, pD=pD: e.activation(out=rdx[:], in_=pD[:], func=AF.Ln), [pD.buf], [rdx.buf])
                        op("act", lambda e: e.activation(out=rdx[:], in_=rdx[:], func=AF.Exp, scale=-1.0), [rdx.buf], [rdx.buf])
                        for dc in range(2):
                            pO = ps[6 + dc]
                            for mt in range(2):
                                mm(pO, v_m[:, mt, (2 * hd + dc) * 128:(2 * hd + dc + 1) * 128], p_x[:, mt, :], mt == 0, mt == 1,
                                   [v_m.buf, p_x.buf])
                            op("dve", lambda e, pO=pO, hd=hd, dc=dc: e.tensor_tensor(
                                out=o_x[:, 2 * hd + dc, :], in0=pO[:], in1=rdx[:], op=ALU.mult), [pO.buf, rdx.buf], [o_x.buf])

                def S3(n):
                    s, ti = tiles[n]
                    t0 = ti * 512
                    x_t, o_x = xt[n % 2], ox[n % 2]
                    for oc in range(8):
                        pz = ps[oc % 2]
                        for kc in range(8):
                            mm(pz, wo[:, kc, oc * 128:(oc + 1) * 128], o_x[:, kc, :], kc == 0, kc == 7, [wo.buf, o_x.buf])
                        op("dve", lambda e, pz=pz, oc=oc: e.tensor_tensor(
                            out=x_t[:, oc, :], in0=pz[:], in1=x_t[:, oc, :], op=ALU.add), [pz.buf, x_t.buf], [x_t.buf])
                    dma("sp", dst_x2[s][:, :, t0:t0 + 512].rearrange("c p t -> p c t"), x_t[:], reads=[x_t.buf])
                S0(0)
                S1(0)
                for n in range(len(tiles)):
                    S0(n + 1)
                    S2a(n)
                    S1(n + 1)
                    S2(n)
                    S3(n)
            S.barrier()

        def ffn(l, src, dst, final):
            N = 256
            with ExitStack() as ph:
                wgu = load_w(ph, "wgu", w_gu[l], 8, 2 * DFF, bw=256, order=[x for i in range(11) for x in (i, 11 + i)])
                wdn = load_w(ph, "wdn", w_down[l], 22, D)
                xt = [sb(ph, "xt%d" % i, [128, 8, N], F32) for i in range(2)]
                sq = sb(ph, "sq", [128, 8, N], BF16)
                rs = sb(ph, "rs", [128, N], F32)
                hfs = [sb(ph, "hf%d" % i, [128, 8, N], BF16) for i in range(2)]
                at = sb(ph, "at", [128, 22, N], BF16)
                sg_ = [sb(ph, "sg%d" % i, [128, N], F32) for i in range(2)]
                yo = sb(ph, "yo", [128, 8, N], F32)
                sq2 = sb(ph, "sq2", [128, 8, N], BF16) if final else None
                rs2 = sb(ph, "rs2", [128, N], F32) if final else None
                tiles = [(s, ti) for s in range(3) for ti in range(src[s].shape[2] // N)]

                def fload(n):
                    if n >= len(tiles):
                        return
                    s, ti = tiles[n]
                    t0 = ti * N
                    x_t = xt[n % 2]
                    dma("sp", x_t[:], src[s][:, :, t0:t0 + N].rearrange("c p t -> p c t"), writes=[x_t.buf])

                def prep(n):
                    if n >= len(tiles):
                        return
                    rmsnorm(xt[n % 2], N, 6 + l, hfs[n % 2], sq, rs, ps[0])
                fload(0)
                prep(0)
                for n, (s, ti) in enumerate(tiles):
                    fload(n + 1)
                    t0 = ti * N
                    x_t, hf = xt[n % 2], hfs[n % 2]
                    for j in range(22):
                        pg, pu = ps[1 + 2 * (j % 2)], ps[2 + 2 * (j % 2)]
                        for kc in range(8):
                            mm(pg, wgu[:, kc, j * 128:(j + 1) * 128], hf[:, kc, :], kc == 0, kc == 7, wb(wgu, j * 128, (j + 1) * 128) + [hf.buf],
                               out=pg[:, 0:N])
                        for kc in range(8):
                            mm(pu, wgu[:, kc, DFF + j * 128:DFF + (j + 1) * 128], hf[:, kc, :], kc == 0, kc == 7,
                               wb(wgu, DFF + j * 128, DFF + (j + 1) * 128) + [hf.buf], out=pu[:, 0:N])
                        st = sg_[j % 2]
                        op("act", lambda e, pg=pg, st=st: e.activation(out=st[:], in_=pg[:, 0:N], func=AF.Silu),
                           [pg.buf], [st.buf])
                        op("dve", lambda e, pu=pu, st=st, j=j: e.tensor_tensor(
                            out=at[:, j, :], in0=pu[:, 0:N], in1=st[:], op=ALU.mult), [pu.buf, st.buf], [at.buf])
                    prep(n + 1)
                    for oc in range(8):
                        pz = ps[5 + oc % 2]
                        for kc in range(22):
                            mm(pz, wdn[:, kc, oc * 128:(oc + 1) * 128], at[:, kc, :], kc == 0, kc == 21, [wdn.buf, at.buf],
                               out=pz[:, 0:N])
                        op("dve", lambda e, pz=pz, oc=oc, x_t=x_t: e.tensor_tensor(
                            out=x_t[:, oc, :], in0=pz[:, 0:N], in1=x_t[:, oc, :], op=ALU.add), [pz.buf, x_t.buf], [x_t.buf])
                    if not final:
                        dma("sp", dst[s][:, :, t0:t0 + N].rearrange("c p t -> p c t"), x_t[:], reads=[x_t.buf])
                    else:
                        op("act", lambda e, x_t=x_t: e.activation(out=sq2[:], in_=x_t[:], func=AF.Square), [x_t.buf], [sq2.buf])
                        for c in range(8):
                            mm(ps[7], ones_b, sq2[:, c, :], c == 0, c == 7, [sq2.buf] + CB, out=ps[7][:, 0:N])
                        op("act", lambda e: e.activation(out=rs2[:], in_=ps[7][:, 0:N], func=AF.Ln, scale=1.0 / D,
                                                         bias=eps_col), [ps[7].buf, cst_f.buf], [rs2.buf])
                        op("act", lambda e: e.activation(out=rs2[:], in_=rs2[:], func=AF.Exp, scale=-0.5), [rs2.buf], [rs2.buf])
                        g = gcol(8)
                        for c in range(8):
                            op("dve", lambda e, c=c, x_t=x_t: e.scalar_tensor_tensor(
                                out=yo[:, c, :], in0=x_t[:, c, :], scalar=g[:, c:c + 1], in1=rs2[:],
                                op0=ALU.mult, op1=ALU.mult), [x_t.buf, rs2.buf, g_sb.buf], [yo.buf])
                        dma("sp", dst[s].rearrange("(c p) t -> p c t", p=128)[:, :, t0:t0 + N], yo[:], reads=[yo.buf])
            S.barrier()

        post_mixer(0, xT, [SEGS[s]["E0"] for s in range(3)], OAB, 6, w_out_ab[:, :], X2)
        ffn(0, X2, X3, False)

        X2o = [dscr("X2o_%d" % s, [8, 128, SEGS[s]["OWN"]], F32) for s in range(3)]
        with ExitStack() as ph:
            w_c = load_w(ph, "w_c", w_in_c, 8, 3072, bw=128)
            xt = [sb(ph, "xt%d" % i, [128, 8, 512], F32) for i in range(2)]
            sq = sb(ph, "sq", [128, 8, 512], BF16)
            rs = sb(ph, "rs", [128, 512], F32)
            ht = [sb(ph, "ht%d" % i, [128, 8, 512], BF16) for i in range(2)]
            ob = [sb(ph, "ob%d" % i, [128, 512], BF16) for i in range(3)]
            vca = [sb(ph, "vca%d" % i, [128, 4, 16, 128], BF16) for i in range(2)]
            for v in vca:
                op("pool", lambda e, v=v: e.memset(v[:], 1.0), [], [v.buf])
            nob = 0
            tiles5 = [(s, ti) for s in range(3) for ti in range(SEGS[s]["E"] // 512)]

            def load5(n):
                if n >= len(tiles5):
                    return
                s, ti = tiles5[n]
                x_t = xt[n % 2]
                dma("sp", x_t[:], X3[s][:, :, ti * 512:ti * 512 + 512].rearrange("c p t -> p c t"), writes=[x_t.buf])

            def norm5(n):
                if n >= len(tiles5):
                    return
                rmsnorm(xt[n % 2], 512, 1, ht[n % 2], sq, rs, ps[0])
            load5(0)
            norm5(0)
            for tno, (s, ti) in enumerate(tiles5):
                if True:
                    load5(tno + 1)
                    t0 = ti * 512
                    h_t, vc = ht[tno % 2], vca[tno % 2]
                    for j in range(16):
                        pz = ps[1 + j % 2]
                        for kc in range(8):
                            mm(pz, w_c[:, kc, j * 128:(j + 1) * 128], h_t[:, kc, :], kc == 0, kc == 7, wb(w_c, j * 128, (j + 1) * 128) + [h_t.buf])
                        o = ob[nob % 3]
                        nob += 1
                        if j % 2 == 0:
                            op("act", lambda e, o=o, pz=pz: e.copy(out=o[:], in_=pz[:]), [pz.buf], [o.buf])
                        else:
                            op("dve", lambda e, o=o, pz=pz: e.tensor_copy(out=o[:], in_=pz[:]), [pz.buf], [o.buf])
                        dst = QC[s][j] if j < 8 else KC[s][j - 8]
                        dma("sp", dst[:, t0:t0 + 512], o[:], reads=[o.buf])
                    norm5(tno + 1)
                    for sub in range(4):
                        for hf in range(2):
                            pv = ps[5 + hf]
                            c0 = 2048 + hf * 512
                            for kc in range(8):
                                mm(pv, h_t[:, kc, sub * 128:(sub + 1) * 128], w_c[:, kc, c0:c0 + 512], kc == 0, kc == 7,
                                   wb(w_c, c0, c0 + 512) + [h_t.buf])
                            pv4 = pv[:].rearrange("p (h t d) -> p h t d", t=2, d=64)
                            vc5 = vc[:, sub, hf * 8:(hf + 1) * 8, :].rearrange("p (h t) c -> p h t c", t=2)
                            op("act", lambda e, pv4=pv4, vc5=vc5: e.copy(out=vc5[:, :, 0, 0:64], in_=pv4[:, :, 0, :]),
                               [pv.buf], [vc.buf])
                            op("dve", lambda e, pv4=pv4, vc5=vc5: e.tensor_copy(out=vc5[:, :, 1, 64:128], in_=pv4[:, :, 1, :]),
                               [pv.buf], [vc.buf])
                    dma("sp", VCa[s][t0:t0 + 512, :].rearrange("(s p) c -> p s c", p=128),
                        vc[:].rearrange("p s h c -> p s (h c)"), reads=[vc.buf])
        S.barrier()

        cm_sb = sb(es, "cm_sb", [128, 128], mybir.dt.uint32)
        dma("sp", cm_sb[:], cmask_in, writes=[cm_sb.buf])
        for s in range(3):
            sg = SEGS[s]
            E, OWN, O0 = sg["E"], sg["OWN"], sg["O0"]
            jobs = c_jobs(sg)
            nrE = E // 64
            nrO = OWN // 64
            with ExitStack() as ph:
                npA = nrE // 2
                sets = []
                for i in range(2):
                    sets.append(dict(
                        qbd=sb(ph, "qbd", [128, nrO, 128], BF16), kc=sb(ph, "kc", [128, E], BF16),
                        vA=sb(ph, "vA", [128, npA, 256], BF16), vB=sb(ph, "vB", [128, npA - 1, 256], BF16),
                        ctb=[sb(ph, "ctb%d" % j, [128, 15, 64], F32) for j in range(2)],
                        oc=sb(ph, "oc_t", [128, OWN], BF16)))
                    op("pool", lambda e, t=sets[i]["qbd"]: e.memset(t[:], 0.0), [], [sets[i]["qbd"].buf])
                sc = [sb(ph, "sc%d" % i, [128, 512], F32) for i in range(2)]
                pp = [sb(ph, "pp%d" % i, [128, 512], BF16) for i in range(3)]
                rdb = [[sb(ph, "rd%d_%d" % (i, j), [128, 512], F32) for j in range(2)] for i in range(2)]
                oe = sb(ph, "oe", [128, 512], BF16)

                def load_set(ch):
                    if ch >= 8:
                        return
                    st = sets[ch % 2]
                    for par in range(2):
                        b0 = par * 64
                        dma("sp", st["qbd"][b0:b0 + 64, :, b0:b0 + 64],
                            QC[s][ch][b0:b0 + 64, O0:O0 + OWN].rearrange("p (r c) -> p r c", c=64), writes=[st["qbd"].buf])
                    dma("sp", st["kc"][:], KC[s][ch], writes=[st["kc"].buf])
                    dma("sp", st["vA"][:], VCa[s][:, ch * 256:(ch + 1) * 256].rearrange("(m p) c -> p m c", p=128),
                        writes=[st["vA"].buf])
                    dma("sp", st["vB"][:], VCa[s][64:E - 64, ch * 256:(ch + 1) * 256].rearrange("(m p) c -> p m c", p=128),
                        writes=[st["vB"].buf])
                    for par in range(2):
                        dma("sp", st["ctb"][par][:], ctab_in[2 * ch + par].rearrange("p (a b) -> p a b", b=64),
                            writes=[st["ctb"][par].buf])
                load_set(0)
                for ch in range(8):
                    load_set(ch + 1)
                    st = sets[ch % 2]
                    qbd, kc_, vA, vB, ctb, oc_t = st["qbd"], st["kc"], st["vA"], st["vB"], st["ctb"], st["oc"]
                    items = []
                    nrm = [j for j in jobs if j[3] == 0]
                    edg = [j for j in jobs if j[3] != 0]
                    for bi in range(0, len(nrm), 8):
                        grp = nrm[bi:bi + 8]
                        for gi, j in enumerate(grp):
                            items.append(j + (gi, len(grp)))
                    for gi, j in enumerate(edg):
                        items.append(j + (gi, len(edg)))
                    nbatch = [0]

                    def emit_S(i):
                        r, kr0, dr0, kind = items[i][:4]
                        pS = ps[i % 3]
                        for i2 in range(4):
                            k0 = (kr0 + 2 * i2) * 64
                            mm(pS, kc_[:, k0:k0 + 128], qbd[:, r, :], True, True, [kc_.buf, qbd.buf],
                               out=pS[:, i2 * 128:(i2 + 1) * 128])

                    def emit_bias(i):
                        r, kr0, dr0, kind = items[i][:4]
                        pS, s_t = ps[i % 3], sc[i % 2]
                        for par in range(2):
                            op("dve", lambda e, par=par: e.scalar_tensor_tensor(
                                out=s_t[:].rearrange("p (a h b) -> p a h b", h=2, b=64)[:, :, par, :],
                                in0=pS[:].rearrange("p (a h b) -> p a h b", h=2, b=64)[:, :, par, :], scalar=0.125,
                                in1=ctb[par][:, dr0:dr0 + 8:2, :], op0=ALU.mult, op1=ALU.add),
                               [pS.buf, ctb[par].buf], [s_t.buf])

                    def finalize(fin):
                        par, pO, rd_, grp = fin
                        b0 = par * 64
                        d0 = 64 - b0
                        nb = len(grp)
                        W = nb * 64
                        op("act", lambda e: e.activation(out=rd_[d0:d0 + 64, 0:W], in_=pO[d0:d0 + 64, 0:W], func=AF.Ln),
                           [pO.buf], [rd_.buf])
                        op("act", lambda e: e.activation(out=rd_[d0:d0 + 64, 0:W], in_=rd_[d0:d0 + 64, 0:W], func=AF.Exp,
                                                         scale=-1.0), [rd_.buf], [rd_.buf])
                        if grp[0][3] == 0:
                            r0 = grp[0][0]
                            op("dve", lambda e: e.tensor_tensor(
                                out=oc_t[b0:b0 + 64, r0 * 64:r0 * 64 + W], in0=pO[b0:b0 + 64, 0:W], in1=rd_[d0:d0 + 64, 0:W],
                                op=ALU.mult), [pO.buf, rd_.buf], [oc_t.buf])
                        else:
                            op("dve", lambda e: e.tensor_tensor(
                                out=oe[b0:b0 + 64, 0:W], in0=pO[b0:b0 + 64, 0:W], in1=rd_[d0:d0 + 64, 0:W],
                                op=ALU.mult), [pO.buf, rd_.buf], [oe.buf])
                            for gi, (r, kr0, dr0, kind) in enumerate(grp):
                                mk = (kind - 1) * 64
                                op("dve", lambda e, gi=gi, r=r, mk=mk: e.copy_predicated(
                                    out=oc_t[b0:b0 + 64, r * 64:(r + 1) * 64], mask=cm_sb[b0:b0 + 64, mk:mk + 64],
                                    data=oe[b0:b0 + 64, gi * 64:(gi + 1) * 64]), [oe.buf, cm_sb.buf, oc_t.buf], [oc_t.buf])
                    emit_S(0)
                    emit_S(1)
                    emit_bias(0)
                    pending = []
                    cur = []
                    for i, it in enumerate(items):
                        r, kr0, dr0, kind, gi, gn = it
                        if i + 2 < len(items):
                            emit_S(i + 2)
                        if i + 1 < len(items):
                            emit_bias(i + 1)
                        if gi == 0:
                            nbatch[0] += 1
                            cur = []
                        cur.append((r, kr0, dr0, kind))
                        nbm = nbatch[0] % 2
                        s_t, p_t = sc[i % 2], pp[i % 3]
                        op("act", lambda e, s_t=s_t, p_t=p_t: e.activation(out=p_t[:], in_=s_t[:], func=AF.Exp),
                           [s_t.buf], [p_t.buf])
                        for fin in pending:
                            finalize(fin)
                        pending = []
                        V2, m0 = (vA, kr0 // 2) if kr0 % 2 == 0 else (vB, (kr0 - 1) // 2)
                        for par in range(2):
                            pO = ps[3 + 2 * nbm + par]
                            for i2 in range(4):
                                mm(pO, V2[:, m0 + i2, par * 128:(par + 1) * 128],
                                   p_t[:, i2 * 128 + par * 64:i2 * 128 + par * 64 + 64], i2 == 0, i2 == 3,
                                   [V2.buf, p_t.buf], out=pO[:, gi * 64:(gi + 1) * 64])
                        if gi == gn - 1:
                            pending = [(par, ps[3 + 2 * nbm + par], rdb[nbm][par], list(cur)) for par in range(2)]
                    for fin in pending:
                        finalize(fin)
                    dma("sp", OC[s][ch], oc_t[:], reads=[oc_t.buf])
            S.barrier()

        post_mixer(1, X3, [SEGS[s]["O0"] for s in range(3)], OC, 8, w_out_c[:, :], X2o)
        ffn(1, X2o, yT, True)
        S.barrier()
        print("bass instructions:", S.nins)
    return nc


def _rope_tab(pos):
    pos = np.asarray(pos, dtype=np.int64)
    valid = (pos >= 0)
    p = np.where(valid, pos, 0)
    row = (p // 64).astype(np.float32)
    col = (p % 64).astype(np.float32)
    inv = (np.float32(10000.0) ** (-np.arange(0, 32, 2, dtype=np.float32) / np.float32(32))).astype(np.float32)
    ang = np.concatenate([row[:, None] * inv, col[:, None] * inv], axis=-1).astype(np.float32)
    cos, sin = np.cos(ang).astype(np.float32), np.sin(ang).astype(np.float32)
    C = np.repeat(cos, 2, axis=1).T
    Sg = np.repeat(sin, 2, axis=1).T.copy()
    Sg[0::2, :] *= -1.0
    C = np.concatenate([C, C], axis=0)
    Sg = np.concatenate([Sg, Sg], axis=0)
    return np.ascontiguousarray(np.stack([C, Sg]).astype(np.float32))


def _consts():
    c = np.zeros((128, 897), np.float32)
    c[:, 896] = EPS
    c[:, 0:128] = 1.0
    c[0:64, 128:192] = 1.0
    c[64:128, 192:256] = 1.0
    for p in range(128):
        c[p, 256 + (p ^ 1)] = 1.0
    pidx = np.arange(128)[:, None]
    for a in range(2):
        cidx = np.arange(128)[None, :]
        delta = (-64 + 128 * a + pidx) - cidx
        c[:, 384 + a * 128:384 + (a + 1) * 128] = -np.abs(delta)
        c[:, 640 + a * 128:640 + (a + 1) * 128] = np.where(np.abs(delta) <= 64, 0.0, NEG)
    return c


def _ctab(rpb):
    qc = np.arange(64)[None, :]
    kc = np.arange(64)[:, None]
    cstart = np.clip(qc - 8, 0, 48)
    valid = (kc >= cstart) & (kc < cstart + 16)
    idx = np.clip(kc - qc + 15, 0, 30)
    t = rpb[:, :, idx]
    t = np.where(valid[None, None], t, np.float32(NEG)).astype(np.float32)
    t = t.transpose(0, 2, 1, 3)
    up = np.full_like(t, np.float32(NEG))
    up[:, :, 0:14] = t[:, :, 1:15]
    return np.ascontiguousarray(np.concatenate([t, up], axis=1).reshape(16, 128, 15 * 64))


_NC_CACHE = {}


def _prep(inputs):
    f = lambda a: np.ascontiguousarray(np.asarray(a, dtype=np.float32))
    xp, xs = f(inputs["x_prompt"]), f(inputs["x_sample"])
    mp, ms = f(inputs["mem_prompt"]), f(inputs["mem_sample"])
    gt = np.zeros((128, 74), np.float32)
    glist = [inputs["g_mix"][0], inputs["g_mix"][1], inputs["g_xattn"][0], inputs["g_xattn"][1],
             inputs["g_mem"][0], inputs["g_mem"][1], inputs["g_ffn"][0], inputs["g_ffn"][1], inputs["g_final"]]
    for i, g in enumerate(glist):
        gt[:, 8 * i:8 * i + 8] = f(g).reshape(8, 128).T
    gt[:, 72] = np.tile(f(inputs["g_qn"])[0], 2)
    gt[:, 73] = np.tile(f(inputs["g_kn"])[0], 2)
    shared = dict(
        gtab=gt, consts=_consts(), ctab=_ctab(f(inputs["rpb_c"])[0]),
        w_in_ab=f(inputs["w_in_ab"])[0], w_out_ab=f(inputs["w_out_ab"])[0], w_in_c=f(inputs["w_in_c"])[0],
        w_out_c=f(inputs["w_out_c"])[0], wq_x=f(inputs["wq_x"]), wkv_x=f(inputs["wkv_x"]), wo_x=f(inputs["wo_x"]),
        w_gu=f(inputs["w_gu"]), w_down=f(inputs["w_down"]),
        rope_p=_rope_tab(np.arange(2048)), ropek_s=_rope_tab(np.arange(16384)),
    )
    xsT = [np.ascontiguousarray(xs[b].T) for b in range(2)]
    in_maps = []
    sg = SEGS[2]
    for c in range(8):
        sbi, qd = c // 4, c % 4
        own0 = qd * 4096
        x0 = own0 - 1280
        pos = np.arange(x0, x0 + sg["X"])
        ok = (pos >= 0) & (pos < 16384)
        xT2 = np.zeros((D, sg["X"]), np.float32)
        xT2[:, ok] = xsT[sbi][:, pos[ok]]
        epos = pos[sg["E0"]:sg["E0"] + sg["E"]]
        epos = np.where((epos >= 0) & (epos < 16384), epos, -1)
        vb = np.zeros((128, 2048), np.float32)
        kid = 0
        for s in range(3):
            seg = SEGS[s]
            for (g, r, i0, nq, kts) in b_jobs(seg):
                d = B_DIL[g]
                for (ja, kp, p0) in kts:
                    if s == 2:
                        tok = seg["E0"] + (ja + np.arange(kp)) * d + r
                        vb[0:kp, kid] = np.where(ok[tok], 0.0, NEG)
                    kid += 1
        cm = np.zeros((128, 128), np.uint32)
        cm[:, 0:64] = 1 if qd == 0 else 0
        cm[:, 64:128] = 1 if qd == 3 else 0
        m = dict(shared)
        m.update(
            xT0=np.ascontiguousarray(xp[2 * c].T), xT1=np.ascontiguousarray(xp[2 * c + 1].T), xT2=xT2, xTa=xsT[sbi],
            memT=np.ascontiguousarray(np.stack([mp[2 * c].T, mp[2 * c + 1].T, ms[sbi].T])),
            ropeq_s=_rope_tab(epos), vbt=vb, cmask=cm,
        )
        in_maps.append(m)
    return in_maps


def kernel(**inputs):
    in_maps = _prep(inputs)
    if "nc" not in _NC_CACHE:
        _NC_CACHE["nc"] = build()
    nc = _NC_CACHE["nc"]
    res = run_bass_kernel_spmd(nc, in_maps, core_ids=list(range(8)))
    yp = np.zeros((16, 2048, D), np.float32)
    ys = np.zeros((2, 16384, D), np.float32)
    for c in range(8):
        r = res.results[c]
        yp[2 * c] = r["yT0"].T
        yp[2 * c + 1] = r["yT1"].T
        ys[c // 4, (c % 4) * 4096:(c % 4 + 1) * 4096] = r["yT2"].T
    return yp, ys
```

```python
import numpy as np
from contextlib import ExitStack
import concourse.bass as bass
import concourse.mybir as mybir
from concourse.bass_utils import run_bass_kernel_spmd

F32 = mybir.dt.float32
BF16 = mybir.dt.bfloat16
AF = mybir.ActivationFunctionType
ALU = mybir.AluOpType

D = 1024
EPS = 1e-6
NEG = -30000.0
DFF = 2816
SEGS = [
    dict(X=2048, E0=0, E=2048, O0=0, OWN=2048, NA=2048, prompt=True),
    dict(X=2048, E0=0, E=2048, O0=0, OWN=2048, NA=2048, prompt=True),
    dict(X=6656, E0=1024, E=4608, O0=256, OWN=4096, NA=16384, prompt=False),
]
B_DIL = (1, 4, 16)


def alibi_c(g, h):
    return float(2.0 ** (-8.0 * (4 * g + h + 1) / 12.0)) * B_DIL[g]


def b_jobs(seg):
    jobs = []
    for g, d in enumerate(B_DIL):
        L = seg["E"] // d
        H = seg["E0"] // d
        H2 = (seg["X"] - seg["E0"] - seg["E"]) // d
        for r in range(d):
            i0 = 0
            while i0 < L:
                nq = min(128, L - i0)
                kts = []
                for a in range(2):
                    jn = i0 - 64 + 128 * a
                    ja = max(jn, -H)
                    je = min(jn + 128, i0 + nq + 64, L + H2)
                    if je > ja:
                        kts.append((ja, je - ja, ja - jn))
                jobs.append((g, r, i0, nq, kts))
                i0 += nq
    return jobs


def c_jobs(seg):
    rows = seg["OWN"] // 64
    off = seg["O0"] // 64
    jobs = []
    if seg["prompt"]:
        for r in range(rows):
            rs = min(max(r - 4, 0), rows - 8)
            jobs.append((r, rs, rs - r + 7, 0))
    else:
        for r in range(rows):
            jobs.append((r, r + off - 4, 3, 0))
            if r < 4:
                jobs.append((r, off, 0 - r + 7, 1))
            if r >= rows - 3:
                jobs.append((r, off + rows - 8, rows - 8 - r + 7, 2))
    return jobs


class Buf:
    __slots__ = ("name", "last_w", "readers")

    def __init__(self, name):
        self.name = name
        self.last_w = None
        self.readers = {}


class TB:
    def __init__(self, t, name):
        self.t = t
        self.buf = Buf(name)

    def __getitem__(self, k):
        return self.t[k]


class Sched:
    def __init__(self, nc, es):
        self.nc = nc
        self.eng = {"pe": nc.tensor, "act": nc.scalar, "dve": nc.vector, "pool": nc.gpsimd, "sp": nc.sync}
        self.sem = {e: es.enter_context(nc.semaphore("s_" + e)) for e in ("pe", "act", "dve", "pool")}
        self.cnt = {e: 0 for e in self.sem}
        self.dq = {"sp": list(range(0, 14)), "pool": list(range(14, 20)), "act": []}
        self.dsem = [es.enter_context(nc.semaphore("d%d" % i)) for i in range(20)]
        self.dcnt = [0] * 20
        self.dnext = {"sp": 0, "pool": 0, "act": 0}
        self.waited = {}
        self.nins = 0

    def _s(self, k):
        return self.sem[k[1]] if k[0] == "e" else self.dsem[k[1]]

    def _deps(self, e, reads, writes):
        deps = {}

        def add(tok, war=False):
            if tok is None:
                return
            k, v, src = tok
            if src == e and (e == "pe" or war):
                return
            if deps.get(k, 0) < v:
                deps[k] = v
        for b in reads:
            add(b.last_w)
        for b in writes:
            add(b.last_w)
            for k, (v, src) in b.readers.items():
                add((k, v, src), True)
        eng = self.eng[e]
        for k, v in deps.items():
            if self.waited.get((e, k), 0) >= v:
                continue
            self.waited[(e, k)] = v
            eng.wait_ge(self._s(k), v)

    def _fin(self, tok, reads, writes):
        k, v, src = tok
        for b in reads:
            b.readers[k] = (v, src)
        for b in writes:
            b.last_w = tok
            b.readers = {}

    def op(self, e, fn, reads=(), writes=()):
        self._deps(e, reads, writes)
        ins = fn(self.eng[e])
        self.cnt[e] += 1
        ins.then_inc(self.sem[e], 1)
        self._fin((("e", e), self.cnt[e], e), reads, writes)
        self.nins += 1

    def dma(self, q, out, in_, reads=(), writes=()):
        self._deps(q, reads, writes)
        lst = self.dq[q]
        i = lst[self.dnext[q] % len(lst)]
        self.dnext[q] += 1
        if self.dcnt[i] > self.waited.get((q, ("d", i)), 0):
            self.waited[(q, ("d", i))] = self.dcnt[i]
            self.eng[q].wait_ge(self.dsem[i], self.dcnt[i])
        ins = self.eng[q].dma_start(out=out, in_=in_)
        self.dcnt[i] += 16
        ins.then_inc(self.dsem[i], 16)
        self._fin((("d", i), self.dcnt[i], "dma"), reads, writes)
        self.nins += 1

    def barrier(self):
        for e in self.eng:
            eng = self.eng[e]
            for o in self.sem:
                if o != e and self.cnt[o] > self.waited.get((e, ("e", o)), 0):
                    self.waited[(e, ("e", o))] = self.cnt[o]
                    eng.wait_ge(self.sem[o], self.cnt[o])
            for i in range(len(self.dsem)):
                if self.dcnt[i] > self.waited.get((e, ("d", i)), 0):
                    self.waited[(e, ("d", i))] = self.dcnt[i]
                    eng.wait_ge(self.dsem[i], self.dcnt[i])


def build(debug=None):
    nc = bass.Bass("TRN2", target_bir_lowering=False)

    def din(name, shape, dt=F32):
        return nc.dram_tensor(name, list(shape), dt, kind="ExternalInput").ap()

    def dscr(name, shape, dt=BF16):
        kind = "ExternalOutput" if (debug and name in debug) else "Internal"
        return nc.dram_tensor(name, list(shape), dt, kind=kind).ap()

    xT = [din("xT0", [D, 2048]), din("xT1", [D, 2048]), din("xT2", [D, 6656])]
    xTa = [xT[0], xT[1], din("xTa", [D, 16384])]
    memT = din("memT", [3, D, 256])
    ropeq = [din("rope_p", [2, 128, 2048]), None, din("ropeq_s", [2, 128, 4608])]
    ropeq[1] = ropeq[0]
    ropek = [ropeq[0], ropeq[0], din("ropek_s", [2, 128, 16384])]
    gtab = din("gtab", [128, 74])
    consts = din("consts", [128, 897])
    vbt_in = din("vbt", [128, 2048])
    cmask_in = din("cmask", [128, 128], mybir.dt.uint32)
    ctab_in = din("ctab", [16, 128, 15 * 64])
    w_in_ab = din("w_in_ab", [D, 3072])
    w_out_ab = din("w_out_ab", [768, D])
    w_in_c = din("w_in_c", [D, 3072])
    w_out_c = din("w_out_c", [D, D])
    wq_x = din("wq_x", [2, D, D])
    wkv_x = din("wkv_x", [2, D, 2 * D])
    wo_x = din("wo_x", [2, D, D])
    w_gu = din("w_gu", [2, D, 2 * DFF])
    w_down = din("w_down", [2, DFF, D])
    yT = [nc.dram_tensor("yT%d" % s, [D, SEGS[s]["OWN"]], F32, kind="ExternalOutput").ap() for s in range(3)]
    QA = [dscr("QA%d" % s, [4, 128, SEGS[s]["E"]]) for s in range(3)]
    KAd = [dscr("KAd%d" % s, [2, 128, SEGS[s]["NA"]]) for s in range(3)]
    VAa = [dscr("VAa%d" % s, [SEGS[s]["NA"], 512]) for s in range(3)]
    QB = [dscr("QB%d" % s, [6, 128, SEGS[s]["E"]]) for s in range(3)]
    KB = [dscr("KB%d" % s, [6, 128, SEGS[s]["X"]]) for s in range(3)]
    VB = [dscr("VB%d" % s, [SEGS[s]["X"], 768]) for s in range(3)]
    OAB = [dscr("OAB%d" % s, [6, 128, SEGS[s]["E"]]) for s in range(3)]
    X2 = [dscr("X2_%d" % s, [8, 128, SEGS[s]["E"]], F32) for s in range(3)]
    X3 = [dscr("X3_%d" % s, [8, 128, SEGS[s]["E"]], F32) for s in range(3)]
    QC = [dscr("QC%d" % s, [8, 128, SEGS[s]["E"]]) for s in range(3)]
    KC = [dscr("KC%d" % s, [8, 128, SEGS[s]["E"]]) for s in range(3)]
    VCa = [dscr("VCa%d" % s, [SEGS[s]["E"], 16 * 128]) for s in range(3)]
    OC = [dscr("OC%d" % s, [8, 128, SEGS[s]["OWN"]]) for s in range(3)]
    KM = dscr("KM", [3, 2, 8, 128, 256])
    VM = dscr("VM", [3, 2, 2, 128, 1024])

    with ExitStack() as es:
        S = Sched(nc, es)
        op, dma = S.op, S.dma

        uniq = [0]

        def sb(stack, name, shape, dt):
            uniq[0] += 1
            name = "%s_%d" % (name, uniq[0])
            return TB(stack.enter_context(nc.sbuf_tensor(name, list(shape), dt)), name)

        psall = es.enter_context(nc.psum_tensor("psall", [128, 4096], F32))

        class PSV:
            def __init__(self, i):
                self.off = 512 * i
                self.buf = Buf("ps%d" % i)

            def __getitem__(self, k):
                if isinstance(k, tuple):
                    p, c = k
                else:
                    p, c = k, slice(None)
                a = 0 if c.start is None else c.start
                b = 512 if c.stop is None else c.stop
                assert c.step is None
                return psall[p, self.off + a:self.off + b]
        ps = [PSV(i) for i in range(8)]
        cst_f = sb(es, "cst_f", [128, 897], F32)
        cst_b = sb(es, "cst_b", [128, 384], BF16)
        g_sb = sb(es, "g_sb", [128, 74], F32)
        dma("sp", cst_f[:], consts, writes=[cst_f.buf])
        dma("sp", g_sb[:], gtab, writes=[g_sb.buf])
        op("dve", lambda e: e.tensor_copy(out=cst_b[:], in_=cst_f[:, 0:384]), [cst_f.buf], [cst_b.buf])
        ones_b = cst_b[:, 0:128]
        bd_b = cst_b[:, 128:256]
        pswap_f = cst_f[:, 256:384]
        btabA = cst_f[:, 384:640]
        btabM = cst_f[:, 640:896]
        eps_col = cst_f[:, 896:897]
        CB = [cst_b.buf]

        def gcol(i):
            return g_sb[:, 8 * i:8 * i + 8]

        def mm(pt, lhsT, rhs, start, stop, reads, out=None):
            o = pt[:] if out is None else out
            op("pe", lambda e: e.matmul(o, lhsT=lhsT, rhs=rhs, start=start, stop=stop), reads, [pt.buf])

        def rmsnorm(x_t, N, gi, h_t, sq_t, rs_t, pss):
            op("act", lambda e: e.activation(out=sq_t[:, :, 0:N], in_=x_t[:, :, 0:N], func=AF.Square),
               [x_t.buf], [sq_t.buf])
            for c in range(8):
                mm(pss, ones_b, sq_t[:, c, 0:N], c == 0, c == 7, [sq_t.buf] + CB, out=pss[:, 0:N])
            op("act", lambda e: e.activation(out=rs_t[:, 0:N], in_=pss[:, 0:N], func=AF.Ln, scale=1.0 / D,
                                             bias=eps_col), [pss.buf, cst_f.buf], [rs_t.buf])
            op("act", lambda e: e.activation(out=rs_t[:, 0:N], in_=rs_t[:, 0:N], func=AF.Exp, scale=-0.5),
               [rs_t.buf], [rs_t.buf])
            g = gcol(gi)
            for c in range(8):
                eng = "dve"
                op(eng, lambda e, c=c: e.scalar_tensor_tensor(out=h_t[:, c, 0:N], in0=x_t[:, c, 0:N],
                                                              scalar=g[:, c:c + 1], in1=rs_t[:, 0:N],
                                                              op0=ALU.mult, op1=ALU.mult),
                   [x_t.buf, rs_t.buf, g_sb.buf], [h_t.buf])

        def load_w(stack, name, src, kch, ncol, c0=0, c1=None, bw=None, order=None):
            c1 = ncol if c1 is None else c1
            w = sb(stack, name, [128, kch, c1 - c0], BF16)
            v = src.rearrange("(k p) n -> p k n", p=128)
            if bw is None:
                for k in range(kch):
                    dma("pool", w[:, k:k + 1, :], v[:, k:k + 1, c0:c1], writes=[w.buf])
            else:
                nb = (c1 - c0) // bw
                w.bw = bw
                w.bufs = [Buf("%s_b%d" % (name, i)) for i in range(nb)]
                for b_ in (order if order is not None else range(nb)):
                    dma("pool", w[:, :, b_ * bw:(b_ + 1) * bw], v[:, :, c0 + b_ * bw:c0 + (b_ + 1) * bw],
                        writes=[w.bufs[b_]])
            return w

        def wb(w, a, b_):
            return w.bufs[a // w.bw:(b_ - 1) // w.bw + 1]

        with ExitStack() as ph:
            xm = sb(ph, "xm", [128, 8, 256], F32)
            sq = sb(ph, "sq", [128, 8, 256], BF16)
            rs = sb(ph, "rs", [128, 256], F32)
            hm = sb(ph, "hm", [128, 8, 256], BF16)
            ko = [sb(ph, "ko%d" % i, [128, 256], BF16) for i in range(2)]
            vo = [sb(ph, "vo%d" % i, [128, 512], BF16) for i in range(2)]
            for l in range(2):
                with ExitStack() as ph2:
                    wkv = load_w(ph2, "wkv", wkv_x[l], 8, 2048)
                    for s in range(3):
                        dma("sp", xm[:], memT[s].rearrange("(c p) t -> p c t", p=128), writes=[xm.buf])
                        rmsnorm(xm, 256, 4 + l, hm, sq, rs, ps[0])
                        for oc in range(8):
                            pt = ps[1 + oc % 2]
                            for kc in range(8):
                                mm(pt, wkv[:, kc, oc * 128:(oc + 1) * 128], hm[:, kc, :], kc == 0, kc == 7,
                                   [wkv.buf, hm.buf], out=pt[:, 0:256])
                            o = ko[oc % 2]
                            op("act", lambda e, o=o, pt=pt: e.copy(out=o[:], in_=pt[:, 0:256]), [pt.buf], [o.buf])
                            dma("sp", KM[s, l, oc], o[:], reads=[o.buf])
                        for mt in range(2):
                            for hf in range(2):
                                pt = ps[3 + hf]
                                for kc in range(8):
                                    mm(pt, hm[:, kc, mt * 128:(mt + 1) * 128],
                                       wkv[:, kc, 1024 + hf * 512:1024 + (hf + 1) * 512], kc == 0, kc == 7,
                                       [wkv.buf, hm.buf])
                                o = vo[hf]
                                op("dve", lambda e, o=o, pt=pt: e.tensor_copy(out=o[:], in_=pt[:]), [pt.buf], [o.buf])
                                dma("sp", VM[s, l, mt][:, hf * 512:(hf + 1) * 512], o[:], reads=[o.buf])
                    S.barrier()
        S.barrier()

        def qknorm_rope(pz, gq_col, ctab, stab, tabbufs, out_t, tmp, pa, pb):
            zsq, rs_, qn, t1, t2 = tmp
            op("act", lambda e: e.activation(out=zsq[:], in_=pz[:], func=AF.Square), [pz.buf], [zsq.buf])
            mm(pa, bd_b, zsq[:], True, True, [zsq.buf] + CB)
            op("act", lambda e: e.activation(out=rs_[:], in_=pa[:], func=AF.Ln, scale=1.0 / 64, bias=eps_col),
               [pa.buf, cst_f.buf], [rs_.buf])
            op("act", lambda e: e.activation(out=rs_[:], in_=rs_[:], func=AF.Exp, scale=-0.5), [rs_.buf], [rs_.buf])
            op("dve", lambda e: e.scalar_tensor_tensor(out=qn[:], in0=pz[:], scalar=gq_col, in1=rs_[:],
                                                       op0=ALU.mult, op1=ALU.mult),
               [pz.buf, rs_.buf, g_sb.buf], [qn.buf])
            mm(pb, pswap_f, qn[:], True, True, [qn.buf, cst_f.buf])
            op("pool", lambda e: e.tensor_tensor(out=t1[:], in0=qn[:], in1=ctab, op=ALU.mult),
               [qn.buf] + tabbufs, [t1.buf])
            op("dve", lambda e: e.tensor_tensor(out=t2[:], in0=pb[:], in1=stab, op=ALU.mult),
               [pb.buf] + tabbufs, [t2.buf])
            op("dve", lambda e: e.tensor_tensor(out=out_t[:], in0=t1[:], in1=t2[:], op=ALU.add),
               [t1.buf, t2.buf], [out_t.buf])

        def qknorm_group(chunks, pas, pbs, tmpl):
            n = len(chunks)
            for i in range(n):
                pz = chunks[i][0]
                zsq = tmpl[i][0]
                op("act", lambda e, zsq=zsq, pz=pz: e.activation(out=zsq[:], in_=pz[:], func=AF.Square), [pz.buf], [zsq.buf])
            for i in range(n):
                mm(pas[i], bd_b, tmpl[i][0][:], True, True, [tmpl[i][0].buf] + CB)
            for i in range(n):
                rs_, pa = tmpl[i][1], pas[i]
                op("act", lambda e, rs_=rs_, pa=pa: e.activation(out=rs_[:], in_=pa[:], func=AF.Ln, scale=1.0 / 64,
                                                                 bias=eps_col), [pa.buf, cst_f.buf], [rs_.buf])
            for i in range(n):
                rs_ = tmpl[i][1]
                op("act", lambda e, rs_=rs_: e.activation(out=rs_[:], in_=rs_[:], func=AF.Exp, scale=-0.5),
                   [rs_.buf], [rs_.buf])
            for i in range(n):
                pz, gq = chunks[i][0], chunks[i][1]
                rs_, qn = tmpl[i][1], tmpl[i][2]
                op("dve", lambda e, pz=pz, gq=gq, rs_=rs_, qn=qn: e.scalar_tensor_tensor(
                    out=qn[:], in0=pz[:], scalar=gq, in1=rs_[:], op0=ALU.mult, op1=ALU.mult),
                   [pz.buf, rs_.buf, g_sb.buf], [qn.buf])
            for i in range(n):
                mm(pbs[i], pswap_f, tmpl[i][2][:], True, True, [tmpl[i][2].buf, cst_f.buf])
            for i in range(n):
                qn, t1 = tmpl[i][2], tmpl[i][3]
                ctab, tb_ = chunks[i][2], chunks[i][4]
                op("pool", lambda e, qn=qn, t1=t1, ctab=ctab: e.tensor_tensor(out=t1[:], in0=qn[:], in1=ctab, op=ALU.mult),
                   [qn.buf] + tb_, [t1.buf])
            for i in range(n):
                t2, pb = tmpl[i][4], pbs[i]
                stab, tb_ = chunks[i][3], chunks[i][4]
                op("dve", lambda e, t2=t2, pb=pb, stab=stab: e.tensor_tensor(out=t2[:], in0=pb[:], in1=stab, op=ALU.mult),
                   [pb.buf] + tb_, [t2.buf])
            for i in range(n):
                t1, t2, out_t = tmpl[i][3], tmpl[i][4], chunks[i][5]
                op("dve", lambda e, t1=t1, t2=t2, out_t=out_t: e.tensor_tensor(out=out_t[:], in0=t1[:], in1=t2[:], op=ALU.add),
                   [t1.buf, t2.buf], [out_t.buf])

        with ExitStack() as ph:
            wkd = sb(ph, "wkd", [128, 8, 256], BF16)
            wv_ = w_in_ab.rearrange("(k p) n -> p k n", p=128)
            for j in range(4):
                dma("pool", wkd[:, :, j * 64:(j + 1) * 64], wv_[:, :, 512 + (j // 2) * 64:512 + (j // 2) * 64 + 64],
                    writes=[wkd.buf])
            w_in = load_w(ph, "w_in", w_in_ab, 8, 3072, bw=128, order=[5, 0, 1, 2, 3, 4] + list(range(6, 24)))
            xt = [sb(ph, "xt%d" % i, [128, 8, 512], F32) for i in range(2)]
            sq = sb(ph, "sq", [128, 8, 512], BF16)
            rs = sb(ph, "rs", [128, 512], F32)
            ht = [sb(ph, "ht%d" % i, [128, 8, 512], BF16) for i in range(2)]
            tmp = [sb(ph, "zsq", [128, 512], BF16), sb(ph, "rs2", [128, 512], F32), sb(ph, "qn", [128, 512], F32),
                   sb(ph, "t1", [128, 512], F32), sb(ph, "t2", [128, 512], F32)]
            rtab = [sb(ph, "rtab%d" % i, [128, 2, 512], F32) for i in range(2)]
            ob = [sb(ph, "ob%d" % i, [128, 512], BF16) for i in range(4)]
            vaug = [sb(ph, "vaug%d" % i, [128, 4, 4, 128], BF16) for i in range(2)]
            vbt = [sb(ph, "vbo%d" % i, [128, 4, 768], BF16) for i in range(2)]
            for v in vaug:
                op("pool", lambda e, v=v: e.memset(v[:], 1.0), [], [v.buf])
            nob = [0]

            def next_ob():
                nob[0] += 1
                return ob[nob[0] % 4]
            tmp2 = [sb(ph, "zsq", [128, 512], BF16), sb(ph, "rs2", [128, 512], F32), sb(ph, "qn", [128, 512], F32),
                    sb(ph, "t1", [128, 512], F32), sb(ph, "t2", [128, 512], F32)]
            tmps = [tmp, tmp2]
            nqk = [0]

            def next_tmp():
                nqk[0] += 1
                return tmps[nqk[0] % 2]
            tiles = []
            for s in range(3):
                for ti in range(SEGS[s]["NA"] // 512):
                    tiles.append(("a", s, ti))
            for s in range(3):
                for ti in range(SEGS[s]["X"] // 512):
                    tiles.append(("b", s, ti))

            def prep_load(n):
                if n >= len(tiles):
                    return
                kind, s, ti = tiles[n]
                sg = SEGS[s]
                t0 = ti * 512
                x_t, rt = xt[n % 2], rtab[n % 2]
                src = (xTa if kind == "a" else xT)[s].rearrange("(c p) t -> p c t", p=128)
                dma("sp", x_t[:], src[:, :, t0:t0 + 512], writes=[x_t.buf])
                if kind == "a":
                    dma("sp", rt[:], ropek[s][:, :, t0:t0 + 512].rearrange("a p t -> p a t"), writes=[rt.buf])
                else:
                    e0 = t0 - sg["E0"]
                    if 0 <= e0 < sg["E"]:
                        dma("sp", rt[:], ropeq[s][:, :, e0:e0 + 512].rearrange("a p t -> p a t"), writes=[rt.buf])

            def prep_norm(n):
                if n >= len(tiles):
                    return
                rmsnorm(xt[n % 2], 512, 0, ht[n % 2], sq, rs, ps[0])
            prep_load(0)
            prep_norm(0)
            for n, (kind, s, ti) in enumerate(tiles):
                prep_load(n + 1)
                sg = SEGS[s]
                t0 = ti * 512
                h_t, rt = ht[n % 2], rtab[n % 2]
                if kind == "a":
                    va = vaug[n % 2]
                    chunks = []
                    for j in range(2):
                        pz = ps[1 + j]
                        for kc in range(8):
                            mm(pz, wkd[:, kc, j * 128:(j + 1) * 128], h_t[:, kc, :], kc == 0, kc == 7,
                               [wkd.buf, h_t.buf])
                        chunks.append((pz, g_sb[:, 73:74], rt[:, 0, :], rt[:, 1, :], [rt.buf], next_ob()))
                    prep_norm(n + 1)
                    qknorm_group(chunks, [ps[3], ps[4]], [ps[5], ps[6]], tmps)
                    for j in range(2):
                        o = chunks[j][5]
                        dma("sp", KAd[s][j][:, t0:t0 + 512], o[:], reads=[o.buf])
                    for sub in range(4):
                        pv = ps[7] if sub % 2 == 0 else ps[0]
                        for kc in range(8):
                            mm(pv, h_t[:, kc, sub * 128:(sub + 1) * 128], w_in[:, kc, 640:768], kc == 0, kc == 7,
                               wb(w_in, 640, 768) + [h_t.buf], out=pv[:, 0:128])
                        for kv in range(2):
                            op("act", lambda e, pv=pv, kv=kv, sub=sub, va=va: e.copy(
                                out=va[:, sub, 2 * kv, 0:64], in_=pv[:, kv * 64:kv * 64 + 64]), [pv.buf], [va.buf])
                            op("dve", lambda e, pv=pv, kv=kv, sub=sub, va=va: e.tensor_copy(
                                out=va[:, sub, 2 * kv + 1, 64:128], in_=pv[:, kv * 64:kv * 64 + 64]), [pv.buf], [va.buf])
                    dma("sp", VAa[s][t0:t0 + 512, :].rearrange("(s p) c -> p s c", p=128),
                        va[:].rearrange("p s a c -> p s (a c)"), reads=[va.buf])
                else:
                    e0 = t0 - sg["E0"]
                    inE = 0 <= e0 < sg["E"]
                    vb = vbt[n % 2]
                    if inE:
                        for jg in range(2):
                            chunks = []
                            for jj in range(2):
                                j = 2 * jg + jj
                                pz = ps[1 + jj]
                                for kc in range(8):
                                    mm(pz, w_in[:, kc, j * 128:(j + 1) * 128], h_t[:, kc, :], kc == 0, kc == 7,
                                       wb(w_in, j * 128, (j + 1) * 128) + [h_t.buf])
                                chunks.append((pz, g_sb[:, 72:73], rt[:, 0, :], rt[:, 1, :], [rt.buf], next_ob()))
                            qknorm_group(chunks, [ps[3], ps[4]], [ps[5], ps[6]], tmps)
                            for jj in range(2):
                                o = chunks[jj][5]
                                dma("sp", QA[s][2 * jg + jj][:, e0:e0 + 512], o[:], reads=[o.buf])
                    for j in range(12):
                        isq = j < 6
                        if isq and not inE:
                            continue
                        pz = ps[1 + j % 2]
                        c0 = 768 + j * 128
                        for kc in range(8):
                            mm(pz, w_in[:, kc, c0:c0 + 128], h_t[:, kc, :], kc == 0, kc == 7, wb(w_in, c0, c0 + 128) + [h_t.buf])
                        o = next_ob()
                        if j % 2 == 0:
                            op("act", lambda e, o=o, pz=pz: e.copy(out=o[:], in_=pz[:]), [pz.buf], [o.buf])
                        else:
                            op("dve", lambda e, o=o, pz=pz: e.tensor_copy(out=o[:], in_=pz[:]), [pz.buf], [o.buf])
                        if isq:
                            dma("sp", QB[s][j][:, e0:e0 + 512], o[:], reads=[o.buf])
                        else:
                            dma("sp", KB[s][j - 6][:, t0:t0 + 512], o[:], reads=[o.buf])
                    prep_norm(n + 1)
                    for sub in range(4):
                        for hf in range(2):
                            pv = ps[5 + hf]
                            c0 = 768 + 1536 + hf * 384
                            for kc in range(8):
                                mm(pv, h_t[:, kc, sub * 128:(sub + 1) * 128], w_in[:, kc, c0:c0 + 384], kc == 0, kc == 7,
                                   wb(w_in, c0, c0 + 384) + [h_t.buf], out=pv[:, 0:384])
                            if hf == 0:
                                op("act", lambda e, pv=pv, sub=sub, vb=vb: e.copy(
                                    out=vb[:, sub, 0:384], in_=pv[:, 0:384]), [pv.buf], [vb.buf])
                            else:
                                op("dve", lambda e, pv=pv, sub=sub, vb=vb: e.tensor_copy(
                                    out=vb[:, sub, 384:768], in_=pv[:, 0:384]), [pv.buf], [vb.buf])
                    dma("sp", VB[s][t0:t0 + 512, :].rearrange("(s p) c -> p s c", p=128), vb[:], reads=[vb.buf])
        S.barrier()

        for s in range(3):
            sg = SEGS[s]
            NA, E = sg["NA"], sg["E"]
            nkt = NA // 128
            for kv in range(2):
                with ExitStack() as ph:
                    kt = sb(ph, "kt", [128, NA], BF16)
                    vt = sb(ph, "vt", [128, nkt, 256], BF16)
                    qz = [[sb(ph, "qz%d_%d" % (i, j), [128, 512], BF16) for j in range(2)] for i in range(3)]
                    pt_ = [sb(ph, "pt%d" % i, [128, 1024], BF16) for i in range(3)]
                    rd = sb(ph, "rd", [128, 512], F32)
                    oo = [sb(ph, "oo%d" % i, [128, 512], BF16) for i in range(2)]
                    ktb = [Buf("ktb%d" % i) for i in range(NA // 2048)]
                    vtb = [Buf("vtb%d" % i) for i in range(nkt // 16)]
                    for ci in range(NA // 2048):
                        c = ci * 2048
                        dma("sp", kt[:, c:c + 2048], KAd[s][kv][:, c:c + 2048], writes=[ktb[ci]])
                        c = ci * 16
                        dma("sp", vt[:, c:c + 16, :],
                            VAa[s][c * 128:(c + 16) * 128, kv * 256:(kv + 1) * 256].rearrange("(t p) c -> p t c", p=128),
                            writes=[vtb[ci]])
                    nqt = E // 512
                    qorder = [(hp, qi) for hp in range(2) for qi in range(nqt)]
                    nk2 = nkt // 2
                    items = [(n, par, kk) for n in range(len(qorder)) for par in range(2) for kk in range(nk2)]
                    for qq in qz:
                        for q1 in qq:
                            op("pool", lambda e, q1=q1: e.memset(q1[:], 0.0), [], [q1.buf])

                    def load_q(n):
                        if n < len(qorder):
                            hp, qi = qorder[n]
                            for par in range(2):
                                q1 = qz[n % 3][par]
                                dma("sp", q1[par * 64:par * 64 + 64, :],
                                    QA[s][kv * 2 + hp][par * 64:par * 64 + 64, qi * 512:(qi + 1) * 512], writes=[q1.buf])

                    def emit_S(i):
                        n, par, kk = items[i]
                        if par == 0 and kk == 0:
                            load_q(n + 1)
                        q1 = qz[n % 3][par]
                        for a_ in range(2):
                            k = 2 * kk + a_
                            pS = ps[2 * (i % 3) + a_]
                            mm(pS, kt[:, k * 128:(k + 1) * 128], q1[:], True, True, [ktb[k // 16], q1.buf])
                    LA = 2
                    load_q(0)
                    for i in range(min(LA, len(items))):
                        emit_S(i)
                    for i, (n, par, kk) in enumerate(items):
                        if i + LA < len(items):
                            emit_S(i + LA)
                        hp, qi = qorder[n]
                        ch = kv * 2 + hp
                        b0 = par * 64
                        d0 = 64 - b0
                        j3 = i % 3
                        p_t = pt_[j3]
                        po = ps[6 + (2 * n + par) % 2]
                        o_t = oo[n % 2]
                        op("act", lambda e, j3=j3, p_t=p_t: e.activation(
                            out=p_t[:], in_=psall[:, 1024 * j3:1024 * j3 + 1024], func=AF.Exp, scale=0.125),
                           [ps[2 * j3].buf, ps[2 * j3 + 1].buf], [p_t.buf])
                        for a_ in range(2):
                            k = 2 * kk + a_
                            mm(po, vt[:, k, par * 128:(par + 1) * 128], p_t[:, a_ * 512:(a_ + 1) * 512], k == 0, k == nkt - 1,
                               [vtb[k // 16], p_t.buf])
                        if kk == nk2 - 1:
                            op("dve", lambda e, po=po, d0=d0: e.reciprocal(out=rd[d0:d0 + 64, :], in_=po[d0:d0 + 64, :]),
                               [po.buf], [rd.buf])
                            op("dve", lambda e, po=po, d0=d0, b0=b0, o_t=o_t: e.tensor_tensor(
                                out=o_t[b0:b0 + 64, :], in0=po[b0:b0 + 64, :], in1=rd[d0:d0 + 64, :], op=ALU.mult),
                               [po.buf, rd.buf], [o_t.buf])
                            if par == 1:
                                dma("sp", OAB[s][ch][:, qi * 512:(qi + 1) * 512], o_t[:], reads=[o_t.buf])
                S.barrier()

        vb_sb = sb(es, "vb_sb", [128, 2048], F32)
        dma("sp", vb_sb[:], vbt_in, writes=[vb_sb.buf])
        ktile_no = 0
        for s in range(3):
            sg = SEGS[s]
            E, X, E0 = sg["E"], sg["X"], sg["E0"]
            jobs = b_jobs(sg)
            with ExitStack() as ph:
                accO = [sb(ph, "accO%d" % i, [128, E], F32) for i in range(2)]
                accD = [sb(ph, "accD%d" % i, [128, E], F32) for i in range(2)]
                qb = [sb(ph, "qb%d" % i, [128, E], BF16) for i in range(2)]
                kb = [sb(ph, "kb%d" % i, [128, X], BF16) for i in range(2)]
                tab = [sb(ph, "tab%d" % i, [128, 256], F32) for i in range(4)]
                vk = [sb(ph, "vk%d" % i, [128, 2, 256], BF16) for i in range(3)]
                sc = [sb(ph, "sc%d" % i, [128, 256], F32) for i in range(2)]
                pp = [sb(ph, "pp%d" % i, [128, 256], BF16) for i in range(3)]
                ofin = sb(ph, "ofin", [128, E], BF16)
                nj0 = [0]
                kt_state = [ktile_no]
                for g, d in enumerate(B_DIL):
                    for pr in range(2):
                        dma("sp", qb[pr][:], QB[s][g * 2 + pr], writes=[qb[pr].buf])
                        dma("sp", kb[pr][:], KB[s][g * 2 + pr], writes=[kb[pr].buf])
                    for h in range(4):
                        c = alibi_c(g, h)
                        op("dve", lambda e, h=h, c=c: e.scalar_tensor_tensor(
                            out=tab[h][:], in0=btabA, scalar=c, in1=btabM, op0=ALU.mult, op1=ALU.add),
                           [cst_f.buf], [tab[h].buf])
                    gjobs = [j for j in jobs if j[0] == g]
                    items = [(ji, h) for ji in range(len(gjobs)) for h in range(4)]
                    jinfo = {}

                    def load_v(ji):
                        nonlocal_k = kt_state
                        jg, r, i0, nq, kts = gjobs[ji]
                        v_t = vk[(nj0[0] + ji) % 3]
                        ids = []
                        for a_, (ja, kp, p0) in enumerate(kts):
                            tok0 = E0 + ja * d + r
                            src = VB[s][tok0:tok0 + (kp - 1) * d + 1:d, g * 256:(g + 1) * 256]
                            dma("sp", v_t[0:kp, a_, :], src, writes=[v_t.buf])
                            ids.append(nonlocal_k[0])
                            nonlocal_k[0] += 1
                        jinfo[ji] = (v_t, ids)

                    def emit_S(i):
                        ji, h = items[i]
                        if h == 0:
                            load_v(ji)
                        jg, r, i0, nq, kts = gjobs[ji]
                        pr, b0 = h // 2, (h % 2) * 64
                        qs = slice(i0 * d + r, (i0 + nq - 1) * d + r + 1, d)
                        pS = ps[i % 3]
                        for a_, (ja, kp, p0) in enumerate(kts):
                            t0k = E0 + ja * d + r
                            ks = slice(t0k, t0k + (kp - 1) * d + 1, d)
                            mm(pS, kb[pr][b0:b0 + 64, ks], qb[pr][b0:b0 + 64, qs], True, True,
                               [kb[pr].buf, qb[pr].buf], out=pS[0:kp, a_ * 128:a_ * 128 + nq])

                    def emit_bias(i):
                        ji, h = items[i]
                        jg, r, i0, nq, kts = gjobs[ji]
                        pS, s_t = ps[i % 3], sc[i % 2]
                        for a_, (ja, kp, p0) in enumerate(kts):
                            op("dve", lambda e, a_=a_, kp=kp, p0=p0: e.scalar_tensor_tensor(
                                out=s_t[0:kp, a_ * 128:a_ * 128 + nq], in0=pS[0:kp, a_ * 128:a_ * 128 + nq], scalar=0.125,
                                in1=tab[h][p0:p0 + kp, a_ * 128:a_ * 128 + nq], op0=ALU.mult, op1=ALU.add),
                               [pS.buf, tab[h].buf], [s_t.buf])
                    emit_S(0)
                    if len(items) > 1:
                        emit_S(1)
                    emit_bias(0)
                    for i, (ji, h) in enumerate(items):
                        if i + 2 < len(items):
                            emit_S(i + 2)
                        if i + 1 < len(items):
                            emit_bias(i + 1)
                        jg, r, i0, nq, kts = gjobs[ji]
                        v_t, kt_ids = jinfo[ji]
                        pr, par = h // 2, h % 2
                        b0 = par * 64
                        qs = slice(i0 * d + r, (i0 + nq - 1) * d + r + 1, d)
                        pO, pD = ps[3 + 2 * (i % 2)], ps[4 + 2 * (i % 2)]
                        s_t, p_t = sc[i % 2], pp[i % 3]
                        na = len(kts)
                        for a_, (ja, kp, p0) in enumerate(kts):
                            kid = kt_ids[a_]
                            op("act", lambda e, a_=a_, kp=kp, kid=kid: e.activation(
                                out=p_t[0:kp, a_ * 128:a_ * 128 + nq], in_=s_t[0:kp, a_ * 128:a_ * 128 + nq], func=AF.Exp,
                                bias=vb_sb[0:kp, kid:kid + 1]), [s_t.buf, vb_sb.buf], [p_t.buf])
                        for a_, (ja, kp, p0) in enumerate(kts):
                            mm(pO, v_t[0:kp, a_, pr * 128:(pr + 1) * 128], p_t[0:kp, a_ * 128:a_ * 128 + nq],
                               a_ == 0, a_ == na - 1, [v_t.buf, p_t.buf], out=pO[:, 0:nq])
                        for a_, (ja, kp, p0) in enumerate(kts):
                            mm(pD, cst_b[0:kp, 0:128], p_t[0:kp, a_ * 128:a_ * 128 + nq],
                               a_ == 0, a_ == na - 1, [p_t.buf] + CB, out=pD[:, 0:nq])
                        for (pX, acc) in ((pO, accO[pr]), (pD, accD[pr])):
                            if g == 0:
                                op("dve", lambda e, pX=pX, acc=acc: e.tensor_copy(
                                    out=acc[b0:b0 + 64, qs], in_=pX[b0:b0 + 64, 0:nq]), [pX.buf], [acc.buf])
                            else:
                                op("dve", lambda e, pX=pX, acc=acc: e.tensor_tensor(
                                    out=acc[b0:b0 + 64, qs], in0=pX[b0:b0 + 64, 0:nq], in1=acc[b0:b0 + 64, qs],
                                    op=ALU.add), [pX.buf, acc.buf], [acc.buf])
                    nj0[0] += len(gjobs)
                ktile_no = kt_state[0]
                assert ktile_no <= 2048
                for pr in range(2):
                    op("dve", lambda e, pr=pr: e.tensor_scalar_max(out=accD[pr][:], in0=accD[pr][:], scalar1=1e-30),
                       [accD[pr].buf], [accD[pr].buf])
                    op("act", lambda e, pr=pr: e.activation(out=accD[pr][:], in_=accD[pr][:], func=AF.Ln),
                       [accD[pr].buf], [accD[pr].buf])
                    op("act", lambda e, pr=pr: e.activation(out=accD[pr][:], in_=accD[pr][:], func=AF.Exp, scale=-1.0),
                       [accD[pr].buf], [accD[pr].buf])
                    op("dve", lambda e, pr=pr: e.tensor_tensor(out=ofin[:], in0=accO[pr][:], in1=accD[pr][:], op=ALU.mult),
                       [accO[pr].buf, accD[pr].buf], [ofin.buf])
                    dma("sp", OAB[s][4 + pr], ofin[:], reads=[ofin.buf])
            S.barrier()

        def post_mixer(l, src_x, src_x_off, src_o, nchunk_o, w_out_dram, dst_x2):
            with ExitStack() as ph:
                wo_ = load_w(ph, "w_o", w_out_dram, nchunk_o, D)
                wq = load_w(ph, "wq", wq_x[l], 8, D)
                wo = load_w(ph, "wo", wo_x[l], 8, D)
                km = [sb(ph, "km", [128, 8, 256], BF16) for i in range(2)]
                vm = [sb(ph, "vm", [128, 2, 1024], BF16) for i in range(2)]
                xt = [sb(ph, "xt%d" % i, [128, 8, 512], F32) for i in range(2)]
                ot = [sb(ph, "ot%d" % i, [128, 8, 512], BF16) for i in range(2)]
                sq = sb(ph, "sq", [128, 8, 512], BF16)
                rs = sb(ph, "rs", [128, 512], F32)
                hx = [sb(ph, "hx%d" % i, [128, 8, 512], BF16) for i in range(2)]
                qx = sb(ph, "qx", [128, 8, 512], BF16)
                px = [sb(ph, "px%d" % i, [128, 2, 512], BF16) for i in range(2)]
                rdx = sb(ph, "rdx", [128, 512], F32)
                ox = [sb(ph, "ox%d" % i, [128, 8, 512], BF16) for i in range(2)]
                tiles = [(s, ti) for s in range(3) for ti in range(dst_x2[s].shape[2] // 512)]

                def S0(n):
                    if n >= len(tiles):
                        return
                    s, ti = tiles[n]
                    t0 = ti * 512
                    x_t, o_t = xt[n % 2], ot[n % 2]
                    if ti == 0:
                        dma("sp", km[s % 2][:], KM[s, l].rearrange("c p m -> p c m"), writes=[km[s % 2].buf])
                        dma("sp", vm[s % 2][:], VM[s, l].rearrange("t p c -> p t c"), writes=[vm[s % 2].buf])
                    xo = src_x_off[s]
                    dma("sp", x_t[:], src_x[s][:, :, xo + t0:xo + t0 + 512].rearrange("c p t -> p c t")
                        if len(src_x[s].shape) == 3 else
                        src_x[s].rearrange("(c p) t -> p c t", p=128)[:, :, xo + t0:xo + t0 + 512], writes=[x_t.buf])
                    dma("sp", o_t[:, 0:nchunk_o, :], src_o[s][:, :, t0:t0 + 512].rearrange("c p t -> p c t"),
                        writes=[o_t.buf])

                def S1(n):
                    if n >= len(tiles):
                        return
                    x_t, o_t = xt[n % 2], ot[n % 2]
                    for oc in range(8):
                        pz = ps[oc % 2]
                        for kc in range(nchunk_o):
                            mm(pz, wo_[:, kc, oc * 128:(oc + 1) * 128], o_t[:, kc, :], kc == 0, kc == nchunk_o - 1,
                               [wo_.buf, o_t.buf])
                        op("dve", lambda e, pz=pz, oc=oc: e.tensor_tensor(
                            out=x_t[:, oc, :], in0=pz[:], in1=x_t[:, oc, :], op=ALU.add), [pz.buf, x_t.buf], [x_t.buf])
                    rmsnorm(x_t, 512, 2 + l, hx[n % 2], sq, rs, ps[2])

                def S2a(n):
                    h_x = hx[n % 2]
                    for oc in range(8):
                        pz = ps[oc % 2]
                        for kc in range(8):
                            mm(pz, wq[:, kc, oc * 128:(oc + 1) * 128], h_x[:, kc, :], kc == 0, kc == 7, [wq.buf, h_x.buf])
                        if oc % 2 == 0:
                            op("act", lambda e, pz=pz, oc=oc: e.copy(out=qx[:, oc, :], in_=pz[:]), [pz.buf], [qx.buf])
                        else:
                            op("dve", lambda e, pz=pz, oc=oc: e.tensor_copy(out=qx[:, oc, :], in_=pz[:]), [pz.buf], [qx.buf])

                def S2(n):
                    s, ti = tiles[n]
                    o_x = ox[n % 2]
                    k_m, v_m = km[s % 2], vm[s % 2]
                    for hd in range(4):
                        p_x = px[hd % 2]
                        for mt in range(2):
                            pS = ps[3 + mt]
                            for dc in range(2):
                                mm(pS, k_m[:, 2 * hd + dc, mt * 128:(mt + 1) * 128], qx[:, 2 * hd + dc, :], dc == 0, dc == 1,
                                   [k_m.buf, qx.buf])
                            op("act", lambda e, pS=pS, mt=mt, p_x=p_x: e.activation(
                                out=p_x[:, mt, :], in_=pS[:], func=AF.Exp, scale=1.0 / 16), [pS.buf], [p_x.buf])
                        pD = ps[5]
                        for mt in range(2):
                            mm(pD, ones_b, p_x[:, mt, :], mt == 0, mt == 1, [p_x.buf] + CB)
                        op("act", lambda e, pD=pD: e.activation(out=rdx[:], in_=pD[:], func=AF.Ln), [pD.buf], [rdx.buf])
                        op("act", lambda e: e.activation(out=rdx[:], in_=rdx[:], func=AF.Exp, scale=-1.0), [rdx.buf], [rdx.buf])
                        for dc in range(2):
                            pO = ps[6 + dc]
                            for mt in range(2):
                                mm(pO, v_m[:, mt, (2 * hd + dc) * 128:(2 * hd + dc + 1) * 128], p_x[:, mt, :], mt == 0, mt == 1,
                                   [v_m.buf, p_x.buf])
                            op("dve", lambda e, pO=pO, hd=hd, dc=dc: e.tensor_tensor(
                                out=o_x[:, 2 * hd + dc, :], in0=pO[:], in1=rdx[:], op=ALU.mult), [pO.buf, rdx.buf], [o_x.buf])

                def S3(n):
                    s, ti = tiles[n]
                    t0 = ti * 512
                    x_t, o_x = xt[n % 2], ox[n % 2]
                    for oc in range(8):
                        pz = ps[oc % 2]
                        for kc in range(8):
                            mm(pz, wo[:, kc, oc * 128:(oc + 1) * 128], o_x[:, kc, :], kc == 0, kc == 7, [wo.buf, o_x.buf])
                        op("dve", lambda e, pz=pz, oc=oc: e.tensor_tensor(
                            out=x_t[:, oc, :], in0=pz[:], in1=x_t[:, oc, :], op=ALU.add), [pz.buf, x_t.buf], [x_t.buf])
                    dma("sp", dst_x2[s][:, :, t0:t0 + 512].rearrange("c p t -> p c t"), x_t[:], reads=[x_t.buf])
                S0(0)
                S1(0)
                for n in range(len(tiles)):
                    S0(n + 1)
                    S2a(n)
                    S1(n + 1)
                    S2(n)
                    S3(n)
            S.barrier()

        def ffn(l, src, dst, final):
            N = 256
            with ExitStack() as ph:
                wgu = load_w(ph, "wgu", w_gu[l], 8, 2 * DFF, bw=256, order=[x for i in range(11) for x in (i, 11 + i)])
                wdn = load_w(ph, "wdn", w_down[l], 22, D)
                xt = [sb(ph, "xt%d" % i, [128, 8, N], F32) for i in range(2)]
                sq = sb(ph, "sq", [128, 8, N], BF16)
                rs = sb(ph, "rs", [128, N], F32)
                hfs = [sb(ph, "hf%d" % i, [128, 8, N], BF16) for i in range(2)]
                at = sb(ph, "at", [128, 22, N], BF16)
                sg_ = [sb(ph, "sg%d" % i, [128, N], F32) for i in range(2)]
                yo = sb(ph, "yo", [128, 8, N], F32)
                sq2 = sb(ph, "sq2", [128, 8, N], BF16) if final else None
                rs2 = sb(ph, "rs2", [128, N], F32) if final else None
                tiles = [(s, ti) for s in range(3) for ti in range(src[s].shape[2] // N)]

                def fload(n):
                    if n >= len(tiles):
                        return
                    s, ti = tiles[n]
                    t0 = ti * N
                    x_t = xt[n % 2]
                    dma("sp", x_t[:], src[s][:, :, t0:t0 + N].rearrange("c p t -> p c t"), writes=[x_t.buf])

                def prep(n):
                    if n >= len(tiles):
                        return
                    rmsnorm(xt[n % 2], N, 6 + l, hfs[n % 2], sq, rs, ps[0])
                fload(0)
                prep(0)
                for n, (s, ti) in enumerate(tiles):
                    fload(n + 1)
                    t0 = ti * N
                    x_t, hf = xt[n % 2], hfs[n % 2]
                    for j in range(22):
                        pg, pu = ps[1 + 2 * (j % 2)], ps[2 + 2 * (j % 2)]
                        for kc in range(8):
                            mm(pg, wgu[:, kc, j * 128:(j + 1) * 128], hf[:, kc, :], kc == 0, kc == 7, wb(wgu, j * 128, (j + 1) * 128) + [hf.buf],
                               out=pg[:, 0:N])
                        for kc in range(8):
                            mm(pu, wgu[:, kc, DFF + j * 128:DFF + (j + 1) * 128], hf[:, kc, :], kc == 0, kc == 7,
                               wb(wgu, DFF + j * 128, DFF + (j + 1) * 128) + [hf.buf], out=pu[:, 0:N])
                        st = sg_[j % 2]
                        op("act", lambda e, pg=pg, st=st: e.activation(out=st[:], in_=pg[:, 0:N], func=AF.Silu),
                           [pg.buf], [st.buf])
                        op("dve", lambda e, pu=pu, st=st, j=j: e.tensor_tensor(
                            out=at[:, j, :], in0=pu[:, 0:N], in1=st[:], op=ALU.mult), [pu.buf, st.buf], [at.buf])
                    prep(n + 1)
                    for oc in range(8):
                        pz = ps[5 + oc % 2]
                        for kc in range(22):
                            mm(pz, wdn[:, kc, oc * 128:(oc + 1) * 128], at[:, kc, :], kc == 0, kc == 21, [wdn.buf, at.buf],
                               out=pz[:, 0:N])
                        op("dve", lambda e, pz=pz, oc=oc, x_t=x_t: e.tensor_tensor(
                            out=x_t[:, oc, :], in0=pz[:, 0:N], in1=x_t[:, oc, :], op=ALU.add), [pz.buf, x_t.buf], [x_t.buf])
                    if not final:
                        dma("sp", dst[s][:, :, t0:t0 + N].rearrange("c p t -> p c t"), x_t[:], reads=[x_t.buf])
                    else:
                        op("act", lambda e, x_t=x_t: e.activation(out=sq2[:], in_=x_t[:], func=AF.Square), [x_t.buf], [sq2.buf])
                        for c in range(8):
                            mm(ps[7], ones_b, sq2[:, c, :], c == 0, c == 7, [sq2.buf] + CB, out=ps[7][:, 0:N])
                        op("act", lambda e: e.activation(out=rs2[:], in_=ps[7][:, 0:N], func=AF.Ln, scale=1.0 / D,
                                                         bias=eps_col), [ps[7].buf, cst_f.buf], [rs2.buf])
                        op("act", lambda e: e.activation(out=rs2[:], in_=rs2[:], func=AF.Exp, scale=-0.5), [rs2.buf], [rs2.buf])
                        g = gcol(8)
                        for c in range(8):
                            op("dve", lambda e, c=c, x_t=x_t: e.scalar_tensor_tensor(
                                out=yo[:, c, :], in0=x_t[:, c, :], scalar=g[:, c:c + 1], in1=rs2[:],
                                op0=ALU.mult, op1=ALU.mult), [x_t.buf, rs2.buf, g_sb.buf], [yo.buf])
                        dma("sp", dst[s].rearrange("(c p) t -> p c t", p=128)[:, :, t0:t0 + N], yo[:], reads=[yo.buf])
            S.barrier()

        post_mixer(0, xT, [SEGS[s]["E0"] for s in range(3)], OAB, 6, w_out_ab[:, :], X2)
        ffn(0, X2, X3, False)

        X2o = [dscr("X2o_%d" % s, [8, 128, SEGS[s]["OWN"]], F32) for s in range(3)]
        with ExitStack() as ph:
            w_c = load_w(ph, "w_c", w_in_c, 8, 3072, bw=128)
            xt = [sb(ph, "xt%d" % i, [128, 8, 512], F32) for i in range(2)]
            sq = sb(ph, "sq", [128, 8, 512], BF16)
            rs = sb(ph, "rs", [128, 512], F32)
            ht = [sb(ph, "ht%d" % i, [128, 8, 512], BF16) for i in range(2)]
            ob = [sb(ph, "ob%d" % i, [128, 512], BF16) for i in range(3)]
            vca = [sb(ph, "vca%d" % i, [128, 4, 16, 128], BF16) for i in range(2)]
            for v in vca:
                op("pool", lambda e, v=v: e.memset(v[:], 1.0), [], [v.buf])
            nob = 0
            tiles5 = [(s, ti) for s in range(3) for ti in range(SEGS[s]["E"] // 512)]

            def load5(n):
                if n >= len(tiles5):
                    return
                s, ti = tiles5[n]
                x_t = xt[n % 2]
                dma("sp", x_t[:], X3[s][:, :, ti * 512:ti * 512 + 512].rearrange("c p t -> p c t"), writes=[x_t.buf])

            def norm5(n):
                if n >= len(tiles5):
                    return
                rmsnorm(xt[n % 2], 512, 1, ht[n % 2], sq, rs, ps[0])
            load5(0)
            norm5(0)
            for tno, (s, ti) in enumerate(tiles5):
                if True:
                    load5(tno + 1)
                    t0 = ti * 512
                    h_t, vc = ht[tno % 2], vca[tno % 2]
                    for j in range(16):
                        pz = ps[1 + j % 2]
                        for kc in range(8):
                            mm(pz, w_c[:, kc, j * 128:(j + 1) * 128], h_t[:, kc, :], kc == 0, kc == 7, wb(w_c, j * 128, (j + 1) * 128) + [h_t.buf])
                        o = ob[nob % 3]
                        nob += 1
                        if j % 2 == 0:
                            op("act", lambda e, o=o, pz=pz: e.copy(out=o[:], in_=pz[:]), [pz.buf], [o.buf])
                        else:
                            op("dve", lambda e, o=o, pz=pz: e.tensor_copy(out=o[:], in_=pz[:]), [pz.buf], [o.buf])
                        dst = QC[s][j] if j < 8 else KC[s][j - 8]
                        dma("sp", dst[:, t0:t0 + 512], o[:], reads=[o.buf])
                    norm5(tno + 1)
                    for sub in range(4):
                        for hf in range(2):
                            pv = ps[5 + hf]
                            c0 = 2048 + hf * 512
                            for kc in range(8):
                                mm(pv, h_t[:, kc, sub * 128:(sub + 1) * 128], w_c[:, kc, c0:c0 + 512], kc == 0, kc == 7,
                                   wb(w_c, c0, c0 + 512) + [h_t.buf])
                            pv4 = pv[:].rearrange("p (h t d) -> p h t d", t=2, d=64)
                            vc5 = vc[:, sub, hf * 8:(hf + 1) * 8, :].rearrange("p (h t) c -> p h t c", t=2)
                            op("act", lambda e, pv4=pv4, vc5=vc5: e.copy(out=vc5[:, :, 0, 0:64], in_=pv4[:, :, 0, :]),
                               [pv.buf], [vc.buf])
                            op("dve", lambda e, pv4=pv4, vc5=vc5: e.tensor_copy(out=vc5[:, :, 1, 64:128], in_=pv4[:, :, 1, :]),
                               [pv.buf], [vc.buf])
                    dma("sp", VCa[s][t0:t0 + 512, :].rearrange("(s p) c -> p s c", p=128),
                        vc[:].rearrange("p s h c -> p s (h c)"), reads=[vc.buf])
        S.barrier()

        cm_sb = sb(es, "cm_sb", [128, 128], mybir.dt.uint32)
        dma("sp", cm_sb[:], cmask_in, writes=[cm_sb.buf])
        for s in range(3):
            sg = SEGS[s]
            E, OWN, O0 = sg["E"], sg["OWN"], sg["O0"]
            jobs = c_jobs(sg)
            nrE = E // 64
            nrO = OWN // 64
            with ExitStack() as ph:
                npA = nrE // 2
                sets = []
                for i in range(2):
                    sets.append(dict(
                        qbd=sb(ph, "qbd", [128, nrO, 128], BF16), kc=sb(ph, "kc", [128, E], BF16),
                        vA=sb(ph, "vA", [128, npA, 256], BF16), vB=sb(ph, "vB", [128, npA - 1, 256], BF16),
                        ctb=[sb(ph, "ctb%d" % j, [128, 15, 64], F32) for j in range(2)],
                        oc=sb(ph, "oc_t", [128, OWN], BF16)))
                    op("pool", lambda e, t=sets[i]["qbd"]: e.memset(t[:], 0.0), [], [sets[i]["qbd"].buf])
                sc = [sb(ph, "sc%d" % i, [128, 512], F32) for i in range(2)]
                pp = [sb(ph, "pp%d" % i, [128, 512], BF16) for i in range(3)]
                rdb = [[sb(ph, "rd%d_%d" % (i, j), [128, 512], F32) for j in range(2)] for i in range(2)]
                oe = sb(ph, "oe", [128, 512], BF16)

                def load_set(ch):
                    if ch >= 8:
                        return
                    st = sets[ch % 2]
                    for par in range(2):
                        b0 = par * 64
                        dma("sp", st["qbd"][b0:b0 + 64, :, b0:b0 + 64],
                            QC[s][ch][b0:b0 + 64, O0:O0 + OWN].rearrange("p (r c) -> p r c", c=64), writes=[st["qbd"].buf])
                    dma("sp", st["kc"][:], KC[s][ch], writes=[st["kc"].buf])
                    dma("sp", st["vA"][:], VCa[s][:, ch * 256:(ch + 1) * 256].rearrange("(m p) c -> p m c", p=128),
                        writes=[st["vA"].buf])
                    dma("sp", st["vB"][:], VCa[s][64:E - 64, ch * 256:(ch + 1) * 256].rearrange("(m p) c -> p m c", p=128),
                        writes=[st["vB"].buf])
                    for par in range(2):
                        dma("sp", st["ctb"][par][:], ctab_in[2 * ch + par].rearrange("p (a b) -> p a b", b=64),
                            writes=[st["ctb"][par].buf])
                load_set(0)
                for ch in range(8):
                    load_set(ch + 1)
                    st = sets[ch % 2]
                    qbd, kc_, vA, vB, ctb, oc_t = st["qbd"], st["kc"], st["vA"], st["vB"], st["ctb"], st["oc"]
                    items = []
                    nrm = [j for j in jobs if j[3] == 0]
                    edg = [j for j in jobs if j[3] != 0]
                    for bi in range(0, len(nrm), 8):
                        grp = nrm[bi:bi + 8]
                        for gi, j in enumerate(grp):
                            items.append(j + (gi, len(grp)))
                    for gi, j in enumerate(edg):
                        items.append(j + (gi, len(edg)))
                    nbatch = [0]

                    def emit_S(i):
                        r, kr0, dr0, kind = items[i][:4]
                        pS = ps[i % 3]
                        for i2 in range(4):
                            k0 = (kr0 + 2 * i2) * 64
                            mm(pS, kc_[:, k0:k0 + 128], qbd[:, r, :], True, True, [kc_.buf, qbd.buf],
                               out=pS[:, i2 * 128:(i2 + 1) * 128])

                    def emit_bias(i):
                        r, kr0, dr0, kind = items[i][:4]
                        pS, s_t = ps[i % 3], sc[i % 2]
                        for par in range(2):
                            op("dve", lambda e, par=par: e.scalar_tensor_tensor(
                                out=s_t[:].rearrange("p (a h b) -> p a h b", h=2, b=64)[:, :, par, :],
                                in0=pS[:].rearrange("p (a h b) -> p a h b", h=2, b=64)[:, :, par, :], scalar=0.125,
                                in1=ctb[par][:, dr0:dr0 + 8:2, :], op0=ALU.mult, op1=ALU.add),
                               [pS.buf, ctb[par].buf], [s_t.buf])

                    def finalize(fin):
                        par, pO, rd_, grp = fin
                        b0 = par * 64
                        d0 = 64 - b0
                        nb = len(grp)
                        W = nb * 64
                        op("act", lambda e: e.activation(out=rd_[d0:d0 + 64, 0:W], in_=pO[d0:d0 + 64, 0:W], func=AF.Ln),
                           [pO.buf], [rd_.buf])
                        op("act", lambda e: e.activation(out=rd_[d0:d0 + 64, 0:W], in_=rd_[d0:d0 + 64, 0:W], func=AF.Exp,
                                                         scale=-1.0), [rd_.buf], [rd_.buf])
                        if grp[0][3] == 0:
                            r0 = grp[0][0]
                            op("dve", lambda e: e.tensor_tensor(
                                out=oc_t[b0:b0 + 64, r0 * 64:r0 * 64 + W], in0=pO[b0:b0 + 64, 0:W], in1=rd_[d0:d0 + 64, 0:W],
                                op=ALU.mult), [pO.buf, rd_.buf], [oc_t.buf])
                        else:
                            op("dve", lambda e: e.tensor_tensor(
                                out=oe[b0:b0 + 64, 0:W], in0=pO[b0:b0 + 64, 0:W], in1=rd_[d0:d0 + 64, 0:W],
                                op=ALU.mult), [pO.buf, rd_.buf], [oe.buf])
                            for gi, (r, kr0, dr0, kind) in enumerate(grp):
                                mk = (kind - 1) * 64
                                op("dve", lambda e, gi=gi, r=r, mk=mk: e.copy_predicated(
                                    out=oc_t[b0:b0 + 64, r * 64:(r + 1) * 64], mask=cm_sb[b0:b0 + 64, mk:mk + 64],
                                    data=oe[b0:b0 + 64, gi * 64:(gi + 1) * 64]), [oe.buf, cm_sb.buf, oc_t.buf], [oc_t.buf])
                    emit_S(0)
                    emit_S(1)
                    emit_bias(0)
                    pending = []
                    cur = []
                    for i, it in enumerate(items):
                        r, kr0, dr0, kind, gi, gn = it
                        if i + 2 < len(items):
                            emit_S(i + 2)
                        if i + 1 < len(items):
                            emit_bias(i + 1)
                        if gi == 0:
                            nbatch[0] += 1
                            cur = []
                        cur.append((r, kr0, dr0, kind))
                        nbm = nbatch[0] % 2
                        s_t, p_t = sc[i % 2], pp[i % 3]
                        op("act", lambda e, s_t=s_t, p_t=p_t: e.activation(out=p_t[:], in_=s_t[:], func=AF.Exp),
                           [s_t.buf], [p_t.buf])
                        for fin in pending:
                            finalize(fin)
                        pending = []
                        V2, m0 = (vA, kr0 // 2) if kr0 % 2 == 0 else (vB, (kr0 - 1) // 2)
                        for par in range(2):
                            pO = ps[3 + 2 * nbm + par]
                            for i2 in range(4):
                                mm(pO, V2[:, m0 + i2, par * 128:(par + 1) * 128],
                                   p_t[:, i2 * 128 + par * 64:i2 * 128 + par * 64 + 64], i2 == 0, i2 == 3,
                                   [V2.buf, p_t.buf], out=pO[:, gi * 64:(gi + 1) * 64])
                        if gi == gn - 1:
                            pending = [(par, ps[3 + 2 * nbm + par], rdb[nbm][par], list(cur)) for par in range(2)]
                    for fin in pending:
                        finalize(fin)
                    dma("sp", OC[s][ch], oc_t[:], reads=[oc_t.buf])
            S.barrier()

        post_mixer(1, X3, [SEGS[s]["O0"] for s in range(3)], OC, 8, w_out_c[:, :], X2o)
        ffn(1, X2o, yT, True)
        S.barrier()
        print("bass instructions:", S.nins)
    return nc


def _rope_tab(pos):
    pos = np.asarray(pos, dtype=np.int64)
    valid = (pos >= 0)
    p = np.where(valid, pos, 0)
    row = (p // 64).astype(np.float32)
    col = (p % 64).astype(np.float32)
    inv = (np.float32(10000.0) ** (-np.arange(0, 32, 2, dtype=np.float32) / np.float32(32))).astype(np.float32)
    ang = np.concatenate([row[:, None] * inv, col[:, None] * inv], axis=-1).astype(np.float32)
    cos, sin = np.cos(ang).astype(np.float32), np.sin(ang).astype(np.float32)
    C = np.repeat(cos, 2, axis=1).T
    Sg = np.repeat(sin, 2, axis=1).T.copy()
    Sg[0::2, :] *= -1.0
    C = np.concatenate([C, C], axis=0)
    Sg = np.concatenate([Sg, Sg], axis=0)
    return np.ascontiguousarray(np.stack([C, Sg]).astype(np.float32))


def _consts():
    c = np.zeros((128, 897), np.float32)
    c[:, 896] = EPS
    c[:, 0:128] = 1.0
    c[0:64, 128:192] = 1.0
    c[64:128, 192:256] = 1.0
    for p in range(128):
        c[p, 256 + (p ^ 1)] = 1.0
    pidx = np.arange(128)[:, None]
    for a in range(2):
        cidx = np.arange(128)[None, :]
        delta = (-64 + 128 * a + pidx) - cidx
        c[:, 384 + a * 128:384 + (a + 1) * 128] = -np.abs(delta)
        c[:, 640 + a * 128:640 + (a + 1) * 128] = np.where(np.abs(delta) <= 64, 0.0, NEG)
    return c


def _ctab(rpb):
    qc = np.arange(64)[None, :]
    kc = np.arange(64)[:, None]
    cstart = np.clip(qc - 8, 0, 48)
    valid = (kc >= cstart) & (kc < cstart + 16)
    idx = np.clip(kc - qc + 15, 0, 30)
    t = rpb[:, :, idx]
    t = np.where(valid[None, None], t, np.float32(NEG)).astype(np.float32)
    t = t.transpose(0, 2, 1, 3)
    up = np.full_like(t, np.float32(NEG))
    up[:, :, 0:14] = t[:, :, 1:15]
    return np.ascontiguousarray(np.concatenate([t, up], axis=1).reshape(16, 128, 15 * 64))


_NC_CACHE = {}


def _prep(inputs):
    f = lambda a: np.ascontiguousarray(np.asarray(a, dtype=np.float32))
    xp, xs = f(inputs["x_prompt"]), f(inputs["x_sample"])
    mp, ms = f(inputs["mem_prompt"]), f(inputs["mem_sample"])
    gt = np.zeros((128, 74), np.float32)
    glist = [inputs["g_mix"][0], inputs["g_mix"][1], inputs["g_xattn"][0], inputs["g_xattn"][1],
             inputs["g_mem"][0], inputs["g_mem"][1], inputs["g_ffn"][0], inputs["g_ffn"][1], inputs["g_final"]]
    for i, g in enumerate(glist):
        gt[:, 8 * i:8 * i + 8] = f(g).reshape(8, 128).T
    gt[:, 72] = np.tile(f(inputs["g_qn"])[0], 2)
    gt[:, 73] = np.tile(f(inputs["g_kn"])[0], 2)
    shared = dict(
        gtab=gt, consts=_consts(), ctab=_ctab(f(inputs["rpb_c"])[0]),
        w_in_ab=f(inputs["w_in_ab"])[0], w_out_ab=f(inputs["w_out_ab"])[0], w_in_c=f(inputs["w_in_c"])[0],
        w_out_c=f(inputs["w_out_c"])[0], wq_x=f(inputs["wq_x"]), wkv_x=f(inputs["wkv_x"]), wo_x=f(inputs["wo_x"]),
        w_gu=f(inputs["w_gu"]), w_down=f(inputs["w_down"]),
        rope_p=_rope_tab(np.arange(2048)), ropek_s=_rope_tab(np.arange(16384)),
    )
    xsT = [np.ascontiguousarray(xs[b].T) for b in range(2)]
    in_maps = []
    sg = SEGS[2]
    for c in range(8):
        sbi, qd = c // 4, c % 4
        own0 = qd * 4096
        x0 = own0 - 1280
        pos = np.arange(x0, x0 + sg["X"])
        ok = (pos >= 0) & (pos < 16384)
        xT2 = np.zeros((D, sg["X"]), np.float32)
        xT2[:, ok] = xsT[sbi][:, pos[ok]]
        epos = pos[sg["E0"]:sg["E0"] + sg["E"]]
        epos = np.where((epos >= 0) & (epos < 16384), epos, -1)
        vb = np.zeros((128, 2048), np.float32)
        kid = 0
        for s in range(3):
            seg = SEGS[s]
            for (g, r, i0, nq, kts) in b_jobs(seg):
                d = B_DIL[g]
                for (ja, kp, p0) in kts:
                    if s == 2:
                        tok = seg["E0"] + (ja + np.arange(kp)) * d + r
                        vb[0:kp, kid] = np.where(ok[tok], 0.0, NEG)
                    kid += 1
        cm = np.zeros((128, 128), np.uint32)
        cm[:, 0:64] = 1 if qd == 0 else 0
        cm[:, 64:128] = 1 if qd == 3 else 0
        m = dict(shared)
        m.update(
            xT0=np.ascontiguousarray(xp[2 * c].T), xT1=np.ascontiguousarray(xp[2 * c + 1].T), xT2=xT2, xTa=xsT[sbi],
            memT=np.ascontiguousarray(np.stack([mp[2 * c].T, mp[2 * c + 1].T, ms[sbi].T])),
            ropeq_s=_rope_tab(epos), vbt=vb, cmask=cm,
        )
        in_maps.append(m)
    return in_maps


def kernel(**inputs):
    in_maps = _prep(inputs)
    if "nc" not in _NC_CACHE:
        _NC_CACHE["nc"] = build()
    nc = _NC_CACHE["nc"]
    res = run_bass_kernel_spmd(nc, in_maps, core_ids=list(range(8)))
    yp = np.zeros((16, 2048, D), np.float32)
    ys = np.zeros((2, 16384, D), np.float32)
    for c in range(8):
        r = res.results[c]
        yp[2 * c] = r["yT0"].T
        yp[2 * c + 1] = r["yT1"].T
        ys[c // 4, (c % 4) * 4096:(c % 4 + 1) * 4096] = r["yT2"].T
    return yp, ys
```
